# Optimizing a Trainium2 kernel written in Bass

```python
import math
import jax, jax.numpy as jnp
from jax import lax
import numpy as np

D_MODEL = 1024
BATCH = 8
SEQ = 2048
DEPTH = 4
DEC_BATCH = 16
DEC_SEQ = 32
PAST_LEN = 4096

CHUNK = 64
Q_BLOCK = 128
N_A_LAYERS = DEPTH // 2
N_B_LAYERS = DEPTH - N_A_LAYERS
MIX_IN = 3 * D_MODEL // 4
MEM_HEADS = 4
MEM_HEAD_DIM = 64
MEM_WIDTH = MEM_HEADS * MEM_HEAD_DIM
N_MEM = 256
SSM_GROUP = 16
SSM_GROUPS = MIX_IN // SSM_GROUP
SSM_STATE = 64
MLA_HEADS = 12
NOPE_DIM = 64
ROPE_DIM = 32
V_DIM = 64
KV_LORA = 256
Q_LORA = MIX_IN
ROPE_BASE = 10000.0
MLA_SCALE = (NOPE_DIM + ROPE_DIM) ** -0.5
MEM_SCALE = MEM_HEAD_DIM ** -0.5
D_FF = 2816
CONV_W = 3
EPS = 1e-6
NEG_INF = -1e30

kernel_name = "yoco_s5_mla_streaming_step"


def rms_norm(x, g):
    xf = x.astype(jnp.float32)
    y = xf * lax.rsqrt(jnp.mean(xf * xf, axis=-1, keepdims=True) + EPS)
    return (y * g.astype(jnp.float32)).astype(x.dtype)


def rope_angles(pos):
    inv_freq = 1.0 / (ROPE_BASE ** (jnp.arange(0, ROPE_DIM, 2, dtype=jnp.float32) / ROPE_DIM))
    ang = pos.astype(jnp.float32)[:, None] * inv_freq[None, :]
    return jnp.cos(ang), jnp.sin(ang)


def apply_rope(x, cos, sin):
    shape = (cos.shape[0],) + (1,) * (x.ndim - 3) + (cos.shape[1],)
    c, s = cos.reshape(shape), sin.reshape(shape)
    xf = x.astype(jnp.float32)
    half = ROPE_DIM // 2
    x1, x2 = xf[..., :half], xf[..., half:]
    return jnp.concatenate([x1 * c - x2 * s, x2 * c + x1 * s], axis=-1).astype(x.dtype)


def conv_ffn(x, ctx, w_in, conv_w, conv_b, w_out):
    u = x @ w_in
    L = u.shape[1]
    padded = jnp.concatenate([ctx.astype(u.dtype), u], axis=1)
    y = sum((padded[:, k:k + L] * conv_w[k] for k in range(CONV_W)), conv_b)
    a, g = jnp.split(y, 2, axis=-1)
    return (jax.nn.silu(g) * a) @ w_out, padded[:, L:]


def _complex_affine_combine(e1, e2):
    a1r, a1i, b1r, b1i = e1
    a2r, a2i, b2r, b2i = e2
    return (a2r * a1r - a2i * a1i, a2r * a1i + a2i * a1r,
            a2r * b1r - a2i * b1i + b2r, a2r * b1i + a2i * b1r + b2i)


def s5_mixer(u, h0_re, h0_im, a_re, a_im, log_dt, b_re, b_im, c_re, c_im, d_skip, w_glu, b_glu):
    f32 = jnp.float32
    Bsz, L, _ = u.shape
    a_re, a_im = a_re.astype(f32), a_im.astype(f32)
    dt = jnp.exp(log_dt.astype(f32))[:, None]
    mag = jnp.exp(a_re * dt)
    lam_re, lam_im = mag * jnp.cos(a_im * dt), mag * jnp.sin(a_im * dt)
    den = a_re * a_re + a_im * a_im
    x_re = lam_re - 1.0
    f_re = (x_re * a_re + lam_im * a_im) / den
    f_im = (lam_im * a_re - x_re * a_im) / den
    b_re, b_im = b_re.astype(f32), b_im.astype(f32)
    bb_re = f_re[..., None] * b_re - f_im[..., None] * b_im
    bb_im = f_re[..., None] * b_im + f_im[..., None] * b_re
    ug = u.astype(f32).reshape(Bsz, L, SSM_GROUPS, SSM_GROUP)
    bu_re = jnp.einsum('blgc,gnc->blgn', ug, bb_re)
    bu_im = jnp.einsum('blgc,gnc->blgn', ug, bb_im)
    if h0_re is not None:
        h0_re, h0_im = h0_re.astype(f32), h0_im.astype(f32)
        bu_re = bu_re.at[:, 0].add(lam_re * h0_re - lam_im * h0_im)
        bu_im = bu_im.at[:, 0].add(lam_re * h0_im + lam_im * h0_re)
    lam_re_t = jnp.broadcast_to(lam_re, (1, L) + lam_re.shape)
    lam_im_t = jnp.broadcast_to(lam_im, (1, L) + lam_im.shape)
    _, _, h_re, h_im = lax.associative_scan(
        _complex_affine_combine, (lam_re_t, lam_im_t, bu_re, bu_im), axis=1)
    y = (jnp.einsum('blgn,gcn->blgc', h_re, c_re.astype(f32))
         - jnp.einsum('blgn,gcn->blgc', h_im, c_im.astype(f32)))
    y = y.reshape(Bsz, L, MIX_IN) + d_skip.astype(f32) * u.astype(f32)
    y = jax.nn.gelu(y).astype(u.dtype)
    out = y * jax.nn.sigmoid(y @ w_glu + b_glu)
    return out, h_re[:, -1], h_im[:, -1]


def memory_kv(mem, mem_norm_g, w_mem_kv, mem_k_norm_g):
    Bsz, M, _ = mem.shape
    m = rms_norm(mem[None], mem_norm_g[:, None, None, :])
    kv = jnp.einsum('lbmd,ldk->lbmk', m, w_mem_kv)
    k, v = jnp.split(kv, 2, axis=-1)
    k = rms_norm(k.reshape(DEPTH, Bsz, M, MEM_HEADS, MEM_HEAD_DIM), mem_k_norm_g[:, None, None, None, :])
    return k, v.reshape(DEPTH, Bsz, M, MEM_HEADS, MEM_HEAD_DIM)


def memory_attend(q, k, v):
    s = jnp.einsum('bqhd,bkhd->bhqk', q, k, preferred_element_type=jnp.float32) * MEM_SCALE
    p = jax.nn.softmax(s, axis=-1).astype(v.dtype)
    return jnp.einsum('bhqk,bkhd->bqhd', p, v)


def mla_attend(q_nope, q_rope, k_nope, k_rope, v, q_chunk, k_chunk):
    s = (jnp.einsum('bqhd,bkhd->bhqk', q_nope, k_nope, preferred_element_type=jnp.float32)
         + jnp.einsum('bqhd,bkd->bhqk', q_rope, k_rope, preferred_element_type=jnp.float32)) * MLA_SCALE
    s = jnp.where(k_chunk[None, :] <= q_chunk[:, None], s, NEG_INF)
    p = jax.nn.softmax(s, axis=-1).astype(v.dtype)
    return jnp.einsum('bhqk,bkhd->bqhd', p, v)


def mla_attention(q_nope, q_rope, k_nope, k_rope, v, q_pos, k_pos):
    q_chunk, k_chunk = q_pos // CHUNK, k_pos // CHUNK
    Bsz, L = q_nope.shape[:2]
    if L % Q_BLOCK != 0:
        return mla_attend(q_nope, q_rope, k_nope, k_rope, v, q_chunk, k_chunk)
    nb = L // Q_BLOCK

    def to_blocks(t):
        return jnp.moveaxis(t.reshape((Bsz, nb, Q_BLOCK) + t.shape[2:]), 1, 0)

    def one_block(args):
        qn, qr, qc = args
        return mla_attend(qn, qr, k_nope, k_rope, v, qc, k_chunk)

    o = lax.map(one_block, (to_blocks(q_nope), to_blocks(q_rope), q_chunk.reshape(nb, Q_BLOCK)))
    return jnp.moveaxis(o, 0, 1).reshape(Bsz, L, MLA_HEADS, V_DIM)


def trunk(x, mem_k, mem_v, ssm_h0_re, ssm_h0_im, conv_ctx, past_latent, past_krope,
          norm_mix_g, w_mix_in, w_mix_out, norm_ffn_g, w_ffn_in, ffn_conv_w, ffn_conv_b, w_ffn_out,
          mem_q_norm_g,
          ssm_a_re, ssm_a_im, ssm_log_dt, ssm_b_re, ssm_b_im, ssm_c_re, ssm_c_im, ssm_d, w_glu, b_glu,
          kv_norm_g, w_dkv, latent_norm_g, krope_norm_g, w_uk, w_uv, k_nope_norm_g,
          q_latent_norm_g, w_uq, q_nope_norm_g, q_rope_norm_g):
    Bsz, L, _ = x.shape
    past = 0 if past_latent is None else past_latent.shape[1]
    q_pos = past + jnp.arange(L, dtype=jnp.int32)
    cos, sin = rope_angles(q_pos)
    if conv_ctx is None:
        conv_ctx = jnp.zeros((DEPTH, Bsz, CONV_W - 1, 2 * D_FF), x.dtype)
    h = x
    ssm_re_out, ssm_im_out, conv_out = [], [], []
    for layer in range(DEPTH):
        z = rms_norm(h, norm_mix_g[layer]) @ w_mix_in[layer]
        z_mix, z_mem = z[..., :MIX_IN], z[..., MIX_IN:]
        mq = rms_norm(z_mem.reshape(Bsz, L, MEM_HEADS, MEM_HEAD_DIM), mem_q_norm_g[layer])
        mem_out = memory_attend(mq, mem_k[layer], mem_v[layer]).reshape(Bsz, L, MEM_WIDTH)
        if layer < N_A_LAYERS:
            i = layer
            mix_out, hr, hi = s5_mixer(
                z_mix,
                None if ssm_h0_re is None else ssm_h0_re[i],
                None if ssm_h0_im is None else ssm_h0_im[i],
                ssm_a_re[i], ssm_a_im[i], ssm_log_dt[i], ssm_b_re[i], ssm_b_im[i],
                ssm_c_re[i], ssm_c_im[i], ssm_d[i], w_glu[i], b_glu[i])
            ssm_re_out.append(hr)
            ssm_im_out.append(hi)
        else:
            if layer == N_A_LAYERS:
                ckv = rms_norm(h, kv_norm_g) @ w_dkv
                new_latent = rms_norm(ckv[..., :KV_LORA], latent_norm_g)
                new_krope = apply_rope(rms_norm(ckv[..., KV_LORA:], krope_norm_g), cos, sin)
                if past_latent is None:
                    latent_all, krope_all = new_latent, new_krope
                else:
                    latent_all = jnp.concatenate([past_latent.astype(new_latent.dtype), new_latent], axis=1)
                    krope_all = jnp.concatenate([past_krope.astype(new_krope.dtype), new_krope], axis=1)
                Lk = latent_all.shape[1]
                k_pos = jnp.arange(Lk, dtype=jnp.int32)
                k_nope = rms_norm((latent_all @ w_uk).reshape(Bsz, Lk, MLA_HEADS, NOPE_DIM), k_nope_norm_g)
                v_all = (latent_all @ w_uv).reshape(Bsz, Lk, MLA_HEADS, V_DIM)
            j = layer - N_A_LAYERS
            q = (rms_norm(z_mix, q_latent_norm_g[j]) @ w_uq[j]).reshape(Bsz, L, MLA_HEADS, NOPE_DIM + ROPE_DIM)
            q_nope = rms_norm(q[..., :NOPE_DIM], q_nope_norm_g[j])
            q_rope = apply_rope(rms_norm(q[..., NOPE_DIM:], q_rope_norm_g[j]), cos, sin)
            mix_out = mla_attention(q_nope, q_rope, k_nope, krope_all, v_all, q_pos, k_pos).reshape(Bsz, L, MIX_IN)
        h = h + jnp.concatenate([mix_out, mem_out], axis=-1) @ w_mix_out[layer]
        f, ctx = conv_ffn(rms_norm(h, norm_ffn_g[layer]), conv_ctx[layer], w_ffn_in[layer],
                          ffn_conv_w[layer], ffn_conv_b[layer], w_ffn_out[layer])
        conv_out.append(ctx)
        h = h + f
    return h, new_latent, new_krope, jnp.stack(ssm_re_out), jnp.stack(ssm_im_out), jnp.stack(conv_out)


def setup_inputs(seed: int = 0) -> dict:
    key = jax.random.key(seed)
    ks = iter(jax.random.split(key, 64))
    f32 = jnp.float32

    def nrm(shape, scale=1.0):
        return scale * jax.random.normal(next(ks), shape, f32)

    def gain(shape):
        return 1.0 + nrm(shape, 0.05)

    D = D_MODEL
    inp = {}
    inp["x_prompt"] = nrm((BATCH, SEQ, D))
    inp["x_sample"] = nrm((DEC_BATCH, DEC_SEQ, D))
    inp["cache_mla_latent"] = nrm((DEC_BATCH, PAST_LEN, KV_LORA))
    inp["cache_mla_krope"] = nrm((DEC_BATCH, PAST_LEN, ROPE_DIM))
    inp["cache_mem_k"] = nrm((DEPTH, DEC_BATCH, N_MEM, MEM_HEADS, MEM_HEAD_DIM))
    inp["cache_mem_v"] = nrm((DEPTH, DEC_BATCH, N_MEM, MEM_HEADS, MEM_HEAD_DIM))
    inp["state_ssm_re"] = nrm((N_A_LAYERS, DEC_BATCH, SSM_GROUPS, SSM_STATE), 0.1)
    inp["state_ssm_im"] = nrm((N_A_LAYERS, DEC_BATCH, SSM_GROUPS, SSM_STATE), 0.1)
    inp["state_conv"] = nrm((DEPTH, DEC_BATCH, CONV_W - 1, 2 * D_FF))
    inp["mem_prompt"] = nrm((BATCH, N_MEM, D))
    inp["norm_mix_g"] = gain((DEPTH, D))
    inp["w_mix_in"] = nrm((DEPTH, D, MIX_IN + MEM_WIDTH), D ** -0.5)
    inp["w_mix_out"] = nrm((DEPTH, MIX_IN + MEM_WIDTH, D), (MIX_IN + MEM_WIDTH) ** -0.5)
    inp["norm_ffn_g"] = gain((DEPTH, D))
    inp["w_ffn_in"] = nrm((DEPTH, D, 2 * D_FF), D ** -0.5)
    inp["ffn_conv_w"] = nrm((DEPTH, CONV_W, 2 * D_FF), CONV_W ** -0.5)
    inp["ffn_conv_b"] = nrm((DEPTH, 2 * D_FF), 0.01)
    inp["w_ffn_out"] = nrm((DEPTH, D_FF, D), D_FF ** -0.5)
    inp["mem_norm_g"] = gain((DEPTH, D))
    inp["w_mem_kv"] = nrm((DEPTH, D, 2 * MEM_WIDTH), D ** -0.5)
    inp["mem_q_norm_g"] = gain((DEPTH, MEM_HEAD_DIM))
    inp["mem_k_norm_g"] = gain((DEPTH, MEM_HEAD_DIM))
    n_idx = jnp.arange(SSM_STATE, dtype=f32)
    inp["ssm_a_re"] = -0.5 + nrm((N_A_LAYERS, SSM_GROUPS, SSM_STATE), 0.01)
    inp["ssm_a_im"] = math.pi * n_idx + nrm((N_A_LAYERS, SSM_GROUPS, SSM_STATE), 0.01)
    inp["ssm_log_dt"] = jax.random.uniform(next(ks), (N_A_LAYERS, SSM_GROUPS), f32,
                                           math.log(1e-3), math.log(1e-1))
    inp["ssm_b_re"] = nrm((N_A_LAYERS, SSM_GROUPS, SSM_STATE, SSM_GROUP), (2 * SSM_GROUP) ** -0.5)
    inp["ssm_b_im"] = nrm((N_A_LAYERS, SSM_GROUPS, SSM_STATE, SSM_GROUP), (2 * SSM_GROUP) ** -0.5)
    inp["ssm_c_re"] = nrm((N_A_LAYERS, SSM_GROUPS, SSM_GROUP, SSM_STATE), SSM_STATE ** -0.5)
    inp["ssm_c_im"] = nrm((N_A_LAYERS, SSM_GROUPS, SSM_GROUP, SSM_STATE), SSM_STATE ** -0.5)
    inp["ssm_d"] = nrm((N_A_LAYERS, MIX_IN))
    inp["w_glu"] = nrm((N_A_LAYERS, MIX_IN, MIX_IN), MIX_IN ** -0.5)
    inp["b_glu"] = nrm((N_A_LAYERS, MIX_IN), 0.01)
    inp["kv_norm_g"] = gain((D,))
    inp["w_dkv"] = nrm((D, KV_LORA + ROPE_DIM), D ** -0.5)
    inp["latent_norm_g"] = gain((KV_LORA,))
    inp["krope_norm_g"] = gain((ROPE_DIM,))
    inp["w_uk"] = nrm((KV_LORA, MLA_HEADS * NOPE_DIM), KV_LORA ** -0.5)
    inp["w_uv"] = nrm((KV_LORA, MLA_HEADS * V_DIM), KV_LORA ** -0.5)
    inp["k_nope_norm_g"] = gain((NOPE_DIM,))
    inp["q_latent_norm_g"] = gain((N_B_LAYERS, Q_LORA))
    inp["w_uq"] = nrm((N_B_LAYERS, Q_LORA, MLA_HEADS * (NOPE_DIM + ROPE_DIM)), Q_LORA ** -0.5)
    inp["q_nope_norm_g"] = gain((N_B_LAYERS, NOPE_DIM))
    inp["q_rope_norm_g"] = gain((N_B_LAYERS, ROPE_DIM))
    return inp


def reference(x_prompt, x_sample, cache_mla_latent, cache_mla_krope, cache_mem_k, cache_mem_v,
              state_ssm_re, state_ssm_im, state_conv, mem_prompt,
              norm_mix_g, w_mix_in, w_mix_out, norm_ffn_g, w_ffn_in, ffn_conv_w, ffn_conv_b, w_ffn_out,
              mem_norm_g, w_mem_kv, mem_q_norm_g, mem_k_norm_g,
              ssm_a_re, ssm_a_im, ssm_log_dt, ssm_b_re, ssm_b_im, ssm_c_re, ssm_c_im, ssm_d, w_glu, b_glu,
              kv_norm_g, w_dkv, latent_norm_g, krope_norm_g, w_uk, w_uv, k_nope_norm_g,
              q_latent_norm_g, w_uq, q_nope_norm_g, q_rope_norm_g):
    weights = (norm_mix_g, w_mix_in, w_mix_out, norm_ffn_g, w_ffn_in, ffn_conv_w, ffn_conv_b, w_ffn_out,
               mem_q_norm_g,
               ssm_a_re, ssm_a_im, ssm_log_dt, ssm_b_re, ssm_b_im, ssm_c_re, ssm_c_im, ssm_d, w_glu, b_glu,
               kv_norm_g, w_dkv, latent_norm_g, krope_norm_g, w_uk, w_uv, k_nope_norm_g,
               q_latent_norm_g, w_uq, q_nope_norm_g, q_rope_norm_g)
    mem_k_p, mem_v_p = memory_kv(mem_prompt, mem_norm_g, w_mem_kv, mem_k_norm_g)
    y_prompt, lat_p, krope_p, ssm_re_p, ssm_im_p, conv_p = trunk(
        x_prompt, mem_k_p, mem_v_p, None, None, None, None, None, *weights)
    y_sample, lat_s, krope_s, ssm_re_s, ssm_im_s, conv_s = trunk(
        x_sample, cache_mem_k, cache_mem_v, state_ssm_re, state_ssm_im, state_conv,
        cache_mla_latent, cache_mla_krope, *weights)
    return (y_prompt, y_sample, mem_k_p, mem_v_p, lat_p, krope_p, ssm_re_p, ssm_im_p, conv_p,
            lat_s, krope_s, ssm_re_s, ssm_im_s, conv_s)
```

```python
import math
import numpy as np
import concourse.bass as bass
import concourse.mybir as mybir
from concourse.bass_utils import run_bass_kernel_spmd

F32 = mybir.dt.float32
BF16 = mybir.dt.bfloat16
I32 = mybir.dt.int32
AF = mybir.ActivationFunctionType
ALU = mybir.AluOpType

D = 1024
DEPTH = 4
NA = 2
TP = 2048
LS = 32
T = TP + 2 * LS
PAST = 4096
DFF = 2816
NJ = DFF // 128
MIX = 768
NMEM = 256
EPS = 1e-6
MLA_SCALE = 96 ** -0.5
MEM_SCALE = 64 ** -0.5
TILES = [(0, 512), (512, 512), (1024, 512), (1536, 512), (2048, 64)]
SEQS = [dict(c0=0, L=TP, past=0, s=None), dict(c0=TP, L=LS, past=PAST, s=0), dict(c0=TP + LS, L=LS, past=PAST, s=1)]

H_OFF = 0
XA_OFF = 67584
Z_OFF = 101376
PH_OFF = 135168
PH_SIZE = 46080
PS_OFF = PH_OFF + PH_SIZE
SC_OFF = PS_OFF
CF_OFF = SC_OFF + 4096
CB_OFF = CF_OFF + 128
LATN_OFF = CB_OFF + 5376
KRN_OFF = LATN_OFF + 2 * T * 2
MKT_OFF = KRN_OFF + T * 2
MV_OFF = MKT_OFF + 4096
CONVB_OFF = MV_OFF + 4096
ARENA_BYTES = CONVB_OFF + 1056
assert ARENA_BYTES <= 212800, ARENA_BYTES
NCF = 32
NCB = 2688

ND_SEMS = 24

SCOL = {}
_n = 0
for _name, _w in [("norm_mix", 32), ("norm_ffn", 32), ("mem_norm", 32), ("kv_norm", 8), ("lat_norm", 2),
                  ("krope_g", 1), ("memq_g", 4), ("memk_g", 4), ("q_g", 2), ("knope_g", 1), ("qlat_g", 12),
                  ("ssm_d", 12), ("b_glu", 12), ("ssm_p", 144), ("qscale", 1), ("conv_w", 528), ("conv_b", 176)]:
    SCOL[_name] = _n
    _n += _w
NSC = 1024
assert _n <= NSC


def _foot(ap):
    name = ap.name
    off = int(ap.offset)
    dims = list(ap.ap)
    es = mybir.dt.size(ap.dtype)
    if str(ap.space) == "DRAM":
        if not name.startswith("scr_"):
            return None
        ext = sum((c - 1) * abs(st_) for st_, c in dims) + 1
        return (name, 0, 1, off * es, (off + ext) * es)
    ps, pc = dims[0]
    if ps == 0:
        ps = 1 << 40
    p0 = off // ps
    lo = off % ps
    ext = sum((c - 1) * abs(s) for s, c in dims[1:]) + 1
    if str(ap.space) == "PSUM":
        return ("PSUM", (p0 // 32) * 32, ((p0 + pc + 31) // 32) * 32, (lo * es // 2048) * 2048,
                (((lo + ext) * es + 2047) // 2048) * 2048)
    return (name, p0, p0 + pc, lo * es, (lo + ext) * es)


class Op:
    __slots__ = ("eng", "fn", "deps", "signal", "sigval", "dma", "dsem", "dval", "idx")

    def __init__(self, eng, fn, dma):
        self.eng = eng
        self.fn = fn
        self.deps = []
        self.signal = False
        self.sigval = 0
        self.dma = dma
        self.dsem = -1
        self.dval = 0
        self.idx = 0


PAGE = 2048
SAME_ENG_GAP = 10 ** 9


class Prog:
    def __init__(self, nc):
        self.nc = nc
        self.ops = []
        self.pages = {}
        self.ndma = 0
        self.nq = {"sp": 0, "pool": 0}
        self.ecnt = {}
        self.eseq = {}

    def _recs(self, f):
        seen = set()
        out = []
        for pg in range(f[3] // PAGE, (f[4] - 1) // PAGE + 1):
            lst = self.pages.get((f[0], pg))
            if not lst:
                continue
            alive = [r for r in lst if r[6]]
            if len(alive) != len(lst):
                lst[:] = alive
            for r in alive:
                if id(r) not in seen and r[1] < f[2] and f[1] < r[2] and r[3] < f[4] and f[3] < r[4]:
                    seen.add(id(r))
                    out.append(r)
        return out

    def _put(self, rec, f):
        for pg in range(f[3] // PAGE, (f[4] - 1) // PAGE + 1):
            self.pages.setdefault((f[0], pg), []).append(rec)

    def add(self, eng, fn, reads=(), writes=(), dma=False):
        op = Op(eng, fn, dma)
        op.idx = len(self.ops)
        if dma:
            half = ND_SEMS // 2
            i = self.nq[eng]
            self.nq[eng] += 1
            op.dsem = (i % half) + (0 if eng == "sp" else half)
            op.dval = (i // half + 1) * 16
            self.ndma += 1
        rf = [x for x in (_foot(a) for a in reads) if x is not None]
        wf = [x for x in (_foot(a) for a in writes) if x is not None]
        deps = {}

        myseq = self.ecnt.get(eng, 0)
        self.ecnt[eng] = myseq + 1
        self.eseq[op.idx] = myseq

        def need(o, raw, psum=False):
            if (not o.dma) and (not dma) and o.eng == eng:
                if eng == "pe" or not raw or psum:
                    return
                if myseq - self.eseq[o.idx] >= SAME_ENG_GAP:
                    return
            deps[o.idx] = o

        prf = [f for f in rf if f[0] == "PSUM"]
        rf = [f for f in rf if f[0] != "PSUM"]
        for f in prf:
            for r in self._recs(f):
                need(r[0], True, True)
        for f in rf:
            for r in self._recs(f):
                if r[5]:
                    need(r[0], True)
        for f in wf:
            for r in self._recs(f):
                need(r[0], False, f[0] == "PSUM")
                if f[1] <= r[1] and r[2] <= f[2] and f[3] <= r[3] and r[4] <= f[4]:
                    r[6] = False
        for f in prf:
            for r in self._recs(f):
                if (not r[0].dma) and r[0].eng == eng and r[1] == f[1] and r[2] == f[2] and r[3] == f[3] and r[4] == f[4]:
                    r[6] = False
            self._put([op, f[1], f[2], f[3], f[4], True, True], f)
        best = {}
        dl = []
        for o in deps.values():
            if o.dma:
                dl.append(o)
            elif o.eng not in best or best[o.eng].idx < o.idx:
                best[o.eng] = o
        op.deps = dl + list(best.values())
        for o in op.deps:
            o.signal = True
        for f in wf:
            self._put([op, f[1], f[2], f[3], f[4], True, True], f)
        for f in rf:
            if not dma:
                for r in self._recs(f):
                    if (not r[5]) and (not r[0].dma) and r[0].eng == eng and r[1] == f[1] and r[2] == f[2] \
                            and r[3] == f[3] and r[4] == f[4]:
                        r[6] = False
            self._put([op, f[1], f[2], f[3], f[4], False, True], f)
        self.ops.append(op)
        return op

    def emit(self):
        nc = self.nc
        engs = ["pe", "act", "dve", "pool", "sp"]
        cnt = {e: 0 for e in engs}
        last = {}
        for op in self.ops:
            if not op.dma:
                last[op.eng] = op
        for op in last.values():
            op.signal = True
        for op in self.ops:
            if op.dma:
                op.signal = True
            elif op.signal:
                cnt[op.eng] += 1
                op.sigval = cnt[op.eng]
        per = {e: [o for o in self.ops if o.eng == e] for e in engs}
        ctx = []
        esem = {}
        for e in engs:
            c = nc.semaphore("s_" + e)
            esem[e] = c.__enter__()
            ctx.append(c)
        dsem = []
        for i in range(ND_SEMS):
            c = nc.semaphore("d_%d" % i)
            dsem.append(c.__enter__())
            ctx.append(c)
        dfinal = [0] * ND_SEMS
        for op in self.ops:
            if op.dma:
                dfinal[op.dsem] = max(dfinal[op.dsem], op.dval)

        def run(e, eng):
            known = {}
            for op in per[e]:
                waits = {}
                for d in op.deps:
                    if d.dma:
                        k = ("d", d.dsem)
                        v = d.dval
                    else:
                        k = ("e", d.eng)
                        v = d.sigval
                    if known.get(k, 0) < v:
                        waits[k] = max(waits.get(k, 0), v)
                if op.dma and op.dval > 16:
                    k = ("d", op.dsem)
                    v = op.dval - 16
                    if known.get(k, 0) < v:
                        waits[k] = max(waits.get(k, 0), v)
                for k, v in waits.items():
                    s = dsem[k[1]] if k[0] == "d" else esem[k[1]]
                    eng.wait_ge(s, v)
                    known[k] = v
                ins = op.fn(eng)
                if op.dma:
                    ins.then_inc(dsem[op.dsem], 16)
                elif op.signal:
                    ins.then_inc(esem[e], 1)
            if e == "sp":
                for i in range(ND_SEMS):
                    if dfinal[i] > known.get(("d", i), 0):
                        eng.wait_ge(dsem[i], dfinal[i])
                for e2 in engs:
                    if cnt[e2] > known.get(("e", e2), 0):
                        eng.wait_ge(esem[e2], cnt[e2])
                for s_ in list(esem.values()) + dsem:
                    eng.sem_clear(s_)

        with nc.Block() as block:
            @block.tensor
            def _(eng):
                run("pe", eng)

            @block.scalar
            def _(eng):
                run("act", eng)

            @block.vector
            def _(eng):
                run("dve", eng)

            @block.gpsimd
            def _(eng):
                run("pool", eng)

            @block.sync
            def _(eng):
                run("sp", eng)
        for c in reversed(ctx):
            c.__exit__(None, None, None)

    def dma(self, out, in_, q="sp"):
        return self.add(q, lambda eng: eng.dma_start(out=out, in_=in_), [in_], [out], dma=True)

    def mm(self, out, lhsT, rhs, start=True, stop=True):
        return self.add("pe", lambda eng: eng.matmul(out, lhsT, rhs, start=start, stop=stop), [lhsT, rhs], [out])

    def act(self, out, in_, func, bias=None, scale=1.0):
        reads = [in_]
        kw = {}
        if bias is not None:
            kw["bias"] = bias
            if not isinstance(bias, (int, float)):
                reads.append(bias)
        if not isinstance(scale, (int, float)):
            reads.append(scale)
        return self.add("act", lambda e: e.activation(out, in_, func, scale=scale, **kw), reads, [out])

    def tt(self, out, in0, in1, op, eng="dve"):
        return self.add(eng, lambda e: e.tensor_tensor(out, in0, in1, op), [in0, in1], [out])

    def ts(self, out, in0, s1, s2, op0, op1=None, eng="dve"):
        reads = [in0] + [s for s in (s1, s2) if s is not None and not isinstance(s, (int, float))]
        kw = {}
        if op1 is not None:
            kw["op1"] = op1
        return self.add(eng, lambda e: e.tensor_scalar(out, in0, s1, s2, op0, **kw), reads, [out])

    def stt(self, out, in0, scalar, in1, op0, op1):
        reads = [in0, in1] + ([] if isinstance(scalar, (int, float)) else [scalar])
        return self.add("dve", lambda e: e.scalar_tensor_tensor(out, in0, scalar, in1, op0, op1), reads, [out])

    def copy(self, out, in_, eng="dve"):
        if eng == "act":
            return self.act(out, in_, AF.Copy)
        return self.add(eng, lambda e: e.tensor_copy(out, in_), [in_], [out])

    def memset(self, out, val, eng="dve"):
        return self.add(eng, lambda e: e.memset(out, val), [], [out])

    def recip(self, out, in_):
        return self.add("dve", lambda e: e.reciprocal(out, in_), [in_], [out])


class K:
    def __init__(self, nc):
        self.nc = nc
        self.P = Prog(nc)
        self.ring_i = 0
        self.bank_i = 0
        self.rr = list(range(8))
        self.held = set()
        self.last_bank = 0

    def V(self, off, dt, *shape, p0=0, p1=128):
        es = mybir.dt.size(dt)
        assert off % es == 0
        n = 1
        for s in shape:
            n *= s
        base = self.A if dt == F32 else self.A.bitcast(dt)
        v = base[p0:p1, off // es: off // es + n]
        if len(shape) == 2:
            v = v.rearrange("p (a b) -> p a b", a=shape[0])
        elif len(shape) == 3:
            v = v.rearrange("p (a b c) -> p a b c", a=shape[0], b=shape[1])
        return v

    def sc(self, name, col=0, p0=0, p1=128):
        c = SCOL[name] + col
        return self.V(SC_OFF + 4 * c, F32, 1, p0=p0, p1=p1)

    def bank(self, b=None):
        if b is None:
            while True:
                b = self.rr[self.bank_i % len(self.rr)]
                self.bank_i += 1
                if b not in self.held:
                    break
            self.last_bank = b
        return self.PS[:, b, :]

    def slot(self):
        i = self.ring_i % 4
        self.ring_i += 1
        return PH_OFF + 8192 * i

    def rms_stats(self, srcs, ones_ap, n, scale, sq_offs, rs_off, rstd_off, p0=0, p1=128, K_=None):
        P = self.P
        ps = self.bank()
        for i, s in enumerate(srcs):
            sq = self.V(sq_offs[i % len(sq_offs)], BF16, 512, p0=p0, p1=p1)[:, 0:n]
            P.act(sq, s, AF.Square)
            P.mm(ps[p0:p1, 0:n], ones_ap, sq, start=(i == 0), stop=(i == len(srcs) - 1))
        rs = self.V(rs_off, F32, 512, p0=p0, p1=p1)[:, 0:n]
        P.act(rs, ps[p0:p1, 0:n], AF.Ln, bias=EPS, scale=scale)
        rstd = self.V(rstd_off, F32, 512, p0=p0, p1=p1)[:, 0:n]
        P.act(rstd, rs, AF.Exp, scale=-0.5)
        return rstd

    def rms_fm(self, src, nk, tiles, gname, gcol0, dst, inv_n):
        L = PH_OFF + 32768
        for (c0, n) in tiles:
            rstd = self.rms_stats([src(k, c0, n) for k in range(nk)], self.ones, n, inv_n,
                                  [L, L + 2048], L + 4096, L + 6144)
            for k in range(nk):
                self.P.stt(dst(k, c0, n), src(k, c0, n), self.sc(gname, gcol0 + k), rstd, ALU.mult, ALU.mult)

    def linear(self, W, nk, rhs, mchunks, tiles, evac):
        P = self.P
        blocks = []
        cur = []
        for mc in mchunks:
            if cur and (mc[0] + mc[1] - cur[0][0] > 512):
                blocks.append(cur)
                cur = []
            cur.append(mc)
        if cur:
            blocks.append(cur)
        mi = 0
        for blk in blocks:
            lo = blk[0][0]
            hi = blk[-1][0] + blk[-1][1]
            so = self.slot()
            wv = self.V(so, BF16, nk, hi - lo)
            P.dma(wv, W[:, lo:hi].rearrange("(k p) o -> p k o", p=128), q="pool")
            for (m0, mn) in blk:
                for (c0, n) in tiles:
                    ps = self.bank()
                    for k in range(nk):
                        P.mm(ps[0:mn, 0:n], wv[:, k, m0 - lo:m0 - lo + mn], rhs(k, c0, n), start=(k == 0), stop=(k == nk - 1))
                    evac(mi, c0, n, ps[0:mn, 0:n])
                mi += 1

    def build(self):
        nc = self.nc
        P = self.P

        def din(name, shape):
            return nc.dram_tensor(name, list(shape), F32, kind="ExternalInput").ap()

        def dout(name, shape):
            return nc.dram_tensor(name, list(shape), F32, kind="ExternalOutput").ap()

        self.xT = din("xT", [D, T])
        self.memT = din("memT", [D, NMEM])
        self.latc = din("latc", [2, 256, PAST])
        self.krc = din("krc", [2, 32, PAST])
        self.cmk = din("cmk", [DEPTH, 2, 256, 256])
        self.cmv = din("cmv", [DEPTH, 2, 256, 256])
        self.sst = din("sst", [NA, 2, 2, 128, 24])
        self.cst = din("cst", [DEPTH, 2, 128, 88])
        self.scal = din("scal", [128, NSC])
        self.consf = din("consf", [128, NCF])
        self.consb = din("consb", [128, NCB])
        self.rope = din("rope", [2, 32, T])
        self.w_mix_in = din("w_mix_in", [DEPTH, D, D])
        self.w_mix_out = din("w_mix_out", [DEPTH, D, D])
        self.w_ffn_in = din("w_ffn_in", [DEPTH, D, 2 * DFF])
        self.w_ffn_out = din("w_ffn_out", [DEPTH, DFF, D])
        self.w_mem_kv = din("w_mem_kv", [DEPTH, D, 512])
        self.w_glu = din("w_glu", [NA, MIX, MIX])
        self.w_dkv = din("w_dkv", [D, 288])
        self.w_uk = din("w_uk", [256, MIX])
        self.w_uv = din("w_uv", [256, MIX])
        self.w_uq = din("w_uq", [2, MIX, 12 * 128])
        self.s5b = din("s5b", [NA, 2, 128, 24 * 128])
        self.s5c = din("s5c", [NA, 2, 128, 24 * 128])
        self.kscr = nc.dram_tensor("scr_k", [2, 12, 64, PAST], BF16, kind="Internal").ap()
        self.vscr = nc.dram_tensor("scr_v", [2, 12, PAST // 1024, 128, 8 * 64], BF16, kind="Internal").ap()
        self.yT = dout("yT", [D, T])
        self.memk_o = dout("memk_o", [DEPTH, 256, NMEM])
        self.memv_o = dout("memv_o", [DEPTH, NMEM, 256])
        self.lat_o = dout("lat_o", [256, T])
        self.kr_o = dout("kr_o", [32, T])
        self.ssm_o = dout("ssm_o", [NA, 3, 2, 128, 24])
        self.conv_o = dout("conv_o", [DEPTH, 3, 128, 88])

        with nc.sbuf_tensor("A", [128, ARENA_BYTES // 4], F32) as A_, \
                nc.psum_tensor("PS", [128, 8, 512], F32) as PS_, \
                nc.allow_low_precision("bf16 matmul operands, fp32 accumulation"):
            self.A = A_[:]
            self.PS = PS_
            self.H = self.V(H_OFF, F32, 8, T)
            self.XA = self.V(XA_OFF, BF16, 8, T)
            self.Z = self.V(Z_OFF, BF16, 8, T)
            cf = self.V(CF_OFF, F32, NCF)
            self.rrot = cf[:, 0:32]
            cb = self.V(CB_OFF, BF16, NCB)
            self.ones = cb[:, 0:128]
            self.onesb = cb[:, 0:64]
            self.negI = cb[:, 128:256]
            self.identb = cb[:, 256:384]
            self.bo64 = cb[:, 384:512]
            self.boq = cb[:, 512:640]
            self.masks = cb[:, 640:2688].rearrange("p (a b) -> p a b", a=4)
            self.latn = self.V(LATN_OFF, BF16, 2, T)
            self.krn = self.V(KRN_OFF, BF16, T)
            self.mkt = self.V(MKT_OFF, BF16, 4, 2, 256)
            self.mv = self.V(MV_OFF, BF16, 4, 2, 256)

            P.dma(self.V(SC_OFF, F32, NSC), self.scal, q="sp")
            P.dma(cf, self.consf, q="sp")
            P.dma(cb, self.consb, q="pool")
            xv = self.xT.rearrange("(k p) t -> p k t", p=128)
            for k in range(8):
                P.dma(self.H[:, k, :], xv[:, k, :], q="sp")

            import os
            self.dbg = int(os.environ.get("KDBG", "99"))
            self.mem_kv()
            for l in range(DEPTH):
                if self.dbg >= 2 and (self.dbg >= 6 or l == 0 or (self.dbg == 5 and l < 2)):
                    self.layer(l)
            yv = self.yT.rearrange("(k p) t -> p k t", p=128)
            for k in range(8):
                P.dma(yv[:, k, :], self.H[:, k, :], q="sp")
            P.emit()

    def mem_kv(self):
        P = self.P
        memf = self.V(XA_OFF, F32, 8, NMEM)
        mn = self.V(Z_OFF, BF16, 8, NMEM)
        P.dma(memf, self.memT.rearrange("(k p) t -> p k t", p=128), q="sp")
        L = PH_OFF + 32768
        stage = self.V(L + 8192, F32, 512)
        for l in range(DEPTH):
            self.rms_fm(lambda k, c0, n: memf[:, k, c0:c0 + n], 8, [(0, NMEM)], "mem_norm", 8 * l,
                        lambda k, c0, n: mn[:, k, c0:c0 + n], 1.0 / D)
            so = self.slot()
            wv = self.V(so, BF16, 8, 512)
            P.dma(wv, self.w_mem_kv[l].rearrange("(k p) o -> p k o", p=128), q="pool")
            for mc in range(2):
                ps = self.bank()
                for k in range(8):
                    P.mm(ps[:, 0:NMEM], wv[:, k, mc * 128:(mc + 1) * 128], mn[:, k, :], start=(k == 0), stop=(k == 7))
                rstd = self.rms_stats([ps[:, 0:NMEM]], self.bo64, NMEM, 1.0 / 64, [L], L + 4096, L + 6144)
                P.stt(stage[:, 0:NMEM], ps[:, 0:NMEM], self.sc("memk_g", l), rstd, ALU.mult, ALU.mult)
                P.dma(self.memk_o[l, mc * 128:(mc + 1) * 128, :], stage[:, 0:NMEM], q="sp")
                P.copy(self.mkt[:, l, mc, :], stage[:, 0:NMEM], eng="act")
            for tt in range(2):
                ps = self.bank()
                for k in range(8):
                    P.mm(ps[:, 0:256], mn[:, k, tt * 128:(tt + 1) * 128], wv[:, k, 256:512], start=(k == 0), stop=(k == 7))
                P.copy(stage[:, 256:512], ps[:, 0:256], eng="act")
                P.dma(self.memv_o[l, tt * 128:(tt + 1) * 128, :], stage[:, 256:512], q="sp")
                P.copy(self.mv[:, l, tt, :], stage[:, 256:512])

    def layer(self, l):
        P = self.P
        H, XA, Z = self.H, self.XA, self.Z
        if l == NA:
            self.ckv()
        self.rms_fm(lambda k, c0, n: H[:, k, c0:c0 + n], 8, TILES, "norm_mix", 8 * l,
                    lambda k, c0, n: XA[:, k, c0:c0 + n], 1.0 / D)
        self.linear(self.w_mix_in[l], 8, lambda k, c0, n: XA[:, k, c0:c0 + n], [(m * 128, 128) for m in range(8)], TILES,
                    lambda mi, c0, n, ps: P.copy(Z[:, mi, c0:c0 + n], ps, eng="act"))
        for sq in SEQS:
            self.mem_attend(l, sq)
        if self.dbg < 3:
            return
        if l < NA:
            self.s5(l)
        else:
            self.mla(l - NA)
        if self.dbg < 4:
            return
        self.linear(self.w_mix_out[l], 8, lambda k, c0, n: XA[:, k, c0:c0 + n], [(m * 128, 128) for m in range(8)], TILES,
                    lambda mi, c0, n, ps: P.tt(H[:, mi, c0:c0 + n], H[:, mi, c0:c0 + n], ps, ALU.add))
        self.ffn(l)

    def mem_attend(self, l, sq):
        P = self.P
        Z, XA = self.Z, self.XA
        B0 = PH_OFF
        if sq["s"] is None:
            Kt = self.mkt[:, l]
            Vm = self.mv[:, l]
        else:
            Kt = self.V(B0, BF16, 2, 256)
            Vm = self.V(B0 + 1024, BF16, 2, 256)
            P.dma(Kt, self.cmk[l, sq["s"]].rearrange("(k p) t -> p k t", p=128), q="pool")
            P.dma(Vm, self.cmv[l, sq["s"]].rearrange("(k p) t -> p k t", p=128), q="pool")
        tiles = [(c0, n) for (c0, n) in TILES if c0 < TP] if sq["s"] is None else [(sq["c0"], sq["L"])]
        for (c0, n) in tiles:
            for mc in range(2):
                zc = Z[:, 6 + mc, c0:c0 + n]
                rstd = self.rms_stats([zc], self.bo64, n, 1.0 / 64, [B0 + 3072], B0 + 5120, B0 + 7168)
                qmb = self.V(B0 + 2048, BF16, 512)[:, 0:n]
                P.stt(qmb, zc, self.sc("memq_g", l), rstd, ALU.mult, ALU.mult)
                pso = self.bank()
                psd = self.bank()
                for hh in range(2):
                    r0, r1 = hh * 64, hh * 64 + 64
                    hd = 2 * mc + hh
                    for kc in range(2):
                        pss = self.bank()
                        P.mm(pss[:, 0:n], Kt[r0:r1, mc, kc * 128:(kc + 1) * 128], qmb[r0:r1, :])
                        pT = self.V(B0 + 9216 + 1024 * kc, BF16, 512)[:, 0:n]
                        P.act(pT, pss[:, 0:n], AF.Exp, scale=MEM_SCALE)
                        P.mm(pso[r0:r1, 0:n], Vm[:, kc, hd * 64:(hd + 1) * 64], pT, start=(kc == 0), stop=(kc == 1))
                        P.mm(psd[r0:r1, 0:n], self.onesb, pT, start=(kc == 0), stop=(kc == 1))
                rd = self.V(B0 + 11264, F32, 512)[:, 0:n]
                lg = self.V(B0 + 5120, F32, 512)[:, 0:n]
                P.act(lg, psd[:, 0:n], AF.Ln)
                P.act(rd, lg, AF.Exp, scale=-1.0)
                P.tt(XA[:, 6 + mc, c0:c0 + n], pso[:, 0:n], rd, ALU.mult)

    def s5(self, l):
        P = self.P
        Z, XA = self.Z, self.XA
        B0 = PH_OFF
        NU = 10
        pb = B0 + 34816

        def pv(i):
            return self.V(pb + 96 * i, F32, 24)
        a_re = self.V(SC_OFF + 4 * (SCOL["ssm_p"] + 72 * l), F32, 24)
        a_im = self.V(SC_OFF + 4 * (SCOL["ssm_p"] + 72 * l + 24), F32, 24)
        ldt = self.V(SC_OFF + 4 * (SCOL["ssm_p"] + 72 * l + 48), F32, 24)
        dt, mag, v, vi, fr, t0, t1p, den, xre, f_re, f_im, lam_re, lam_im = [pv(i) for i in range(13)]
        vI = self.V(pb + 96 * 13, I32, 24)
        Uc = self.V(pb + 1344, F32, NU, 24)
        Us = self.V(pb + 2304, F32, NU, 24)
        cin = self.V(pb + 3264, F32, 4)
        P.act(dt, ldt, AF.Exp)
        P.tt(t0, a_re, dt, ALU.mult)
        P.act(mag, t0, AF.Exp)
        P.tt(t1p, a_im, dt, ALU.mult)
        twopi = 2 * math.pi
        for which in range(2):
            P.ts(v, t1p, 1.0 / twopi, 0.25 if which == 0 else 0.0, ALU.mult, ALU.add)
            P.copy(vI, v)
            P.copy(vi, vI)
            P.tt(fr, v, vi, ALU.subtract)
            P.act(Uc[:, 0, :] if which == 0 else Us[:, 0, :], fr, AF.Sin, scale=6.2831845)
        P.tt(lam_re, Uc[:, 0, :], mag, ALU.mult)
        P.tt(lam_im, Us[:, 0, :], mag, ALU.mult)
        P.tt(t0, a_re, a_re, ALU.mult)
        P.tt(t1p, a_im, a_im, ALU.mult)
        P.tt(den, t0, t1p, ALU.add)
        P.recip(den, den)
        P.ts(xre, lam_re, -1.0, None, ALU.add)
        P.tt(t0, xre, a_re, ALU.mult)
        P.tt(t1p, lam_im, a_im, ALU.mult)
        P.tt(t0, t0, t1p, ALU.add)
        P.tt(f_re, t0, den, ALU.mult)
        P.tt(t0, lam_im, a_re, ALU.mult)
        P.tt(t1p, xre, a_im, ALU.mult)
        P.tt(t0, t0, t1p, ALU.subtract)
        P.tt(f_im, t0, den, ALU.mult)
        for k in range(NU - 1):
            P.tt(t0, Uc[:, k, :], Uc[:, k, :], ALU.mult)
            P.tt(t1p, Us[:, k, :], Us[:, k, :], ALU.mult)
            P.tt(Uc[:, k + 1, :], t0, t1p, ALU.subtract)
            P.tt(t0, Uc[:, k, :], Us[:, k, :], ALU.mult)
            P.ts(Us[:, k + 1, :], t0, 2.0, None, ALU.mult)

        cosT = self.V(B0 + 8192, F32, 512)
        sinT = self.V(B0 + 10240, F32, 512)
        Freb = self.V(B0 + 12288, BF16, 512)
        Fimb = self.V(B0 + 13312, BF16, 512)
        cosb = self.V(B0 + 14336, BF16, 512)
        sinb = self.V(B0 + 15360, BF16, 512)
        nsinb = self.V(B0 + 39424, BF16, 512)
        t1 = self.V(B0 + 16384, F32, 512)
        t2 = self.V(B0 + 18432, F32, 512)
        xr = self.V(B0 + 20480, BF16, 512)
        xi = self.V(B0 + 21504, BF16, 512)
        ab = self.V(B0 + 22528, BF16, 512)
        bb = self.V(B0 + 23552, BF16, 512)
        cb_ = self.V(B0 + 44544, BF16, 512)
        rr_ = self.V(B0 + 24576, F32, 512)
        ri_ = self.V(B0 + 26624, F32, 512)
        rawr = self.V(B0 + 40448, BF16, 512)
        rawi = self.V(B0 + 41472, BF16, 512)
        rrbs = [self.V(B0 + 42496, BF16, 512), self.V(B0 + 4096, BF16, 512)]
        ribs = [self.V(B0 + 43520, BF16, 512), self.V(B0 + 5120, BF16, 512)]
        ytmp = self.V(B0 + 32768, F32, 512)
        cars = [self.V(pb + 3296 + 192 * si, F32, 2, 24) for si in range(3)]
        fins = [self.V(pb + 3872 + 192 * si, F32, 2, 24) for si in range(3)]
        for si, sq in enumerate(SEQS):
            if sq["s"] is not None:
                P.dma(cars[si], self.sst[l, sq["s"]].rearrange("r p i -> p r i"), q="sp")
        blocks = []
        for si, sq in enumerate(SEQS):
            TB = min(512, sq["L"])
            for tb in range(sq["L"] // TB):
                blocks.append((si, tb, TB, sq["c0"] + tb * TB))
        assert len(blocks) <= 6
        hb_i = 0
        for cb in range(6):
            wo = B0
            wB = self.V(wo, BF16, 2, 4, 128)
            wC = self.V(wo + 2048, BF16, 2, 4, 128)
            for r in range(2):
                P.dma(wB[:, r], self.s5b[l, r][:, cb * 512:(cb + 1) * 512].rearrange("p (q m) -> p q m", q=4), q="pool")
                P.dma(wC[:, r], self.s5c[l, r][:, cb * 512:(cb + 1) * 512].rearrange("p (q m) -> p q m", q=4), q="pool")
            psy = [self.bank(2 + bi) for bi in range(len(blocks))]
            for q in range(4):
                i = 4 * cb + q
                P.memset(cosT[:, 0:1], 1.0)
                P.memset(sinT[:, 0:1], 0.0)
                for k in range(9):
                    d = 1 << k
                    uc, us = Uc[:, k, i:i + 1], Us[:, k, i:i + 1]
                    P.ts(t1[:, 0:d], sinT[:, 0:d], us, -1.0, ALU.mult, ALU.mult)
                    P.stt(cosT[:, d:2 * d], cosT[:, 0:d], uc, t1[:, 0:d], ALU.mult, ALU.add)
                    P.ts(t2[:, 0:d], cosT[:, 0:d], us, None, ALU.mult)
                    P.stt(sinT[:, d:2 * d], sinT[:, 0:d], uc, t2[:, 0:d], ALU.mult, ALU.add)
                P.ts(t1, sinT, f_im[:, i:i + 1], None, ALU.mult)
                P.stt(Freb, cosT, f_re[:, i:i + 1], t1, ALU.mult, ALU.add)
                P.ts(t1, sinT, f_re[:, i:i + 1], -1.0, ALU.mult, ALU.mult)
                P.stt(Fimb, cosT, f_im[:, i:i + 1], t1, ALU.mult, ALU.add)
                P.copy(cosb, cosT, eng="act")
                P.copy(sinb, sinT, eng="act")
                P.act(nsinb, sinT, AF.Copy, scale=-1.0)
                def front_a(bi):
                    si, tb, TB, c0 = blocks[bi]
                    u = Z[:, cb, c0:c0 + TB]
                    psr = self.bank(0)
                    psi = self.bank(1)
                    P.mm(psr[:, 0:TB], wB[:, 0, q, :], u)
                    P.mm(psi[:, 0:TB], wB[:, 1, q, :], u)
                    P.copy(rawr[:, 0:TB], psr[:, 0:TB], eng="act")
                    P.copy(rawi[:, 0:TB], psi[:, 0:TB], eng="act")

                def front_b(bi):
                    si, tb, TB, c0 = blocks[bi]
                    a_, b_, c_ = ab[:, 0:TB], bb[:, 0:TB], cb_[:, 0:TB]
                    P.tt(a_, rawr[:, 0:TB], Freb[:, 0:TB], ALU.mult)
                    P.tt(b_, rawi[:, 0:TB], Fimb[:, 0:TB], ALU.mult)
                    P.tt(c_, rawr[:, 0:TB], Fimb[:, 0:TB], ALU.mult)
                    P.tt(xi[:, 0:TB], rawi[:, 0:TB], Freb[:, 0:TB], ALU.mult)
                    P.tt(xr[:, 0:TB], a_, b_, ALU.subtract)
                    P.tt(xi[:, 0:TB], c_, xi[:, 0:TB], ALU.add)

                front_a(0)
                front_b(0)
                for bi, (si, tb, TB, c0) in enumerate(blocks):
                    sq = SEQS[si]
                    car = cars[si]
                    a_, b_ = ab[:, 0:TB], bb[:, 0:TB]
                    rrb, rib = rrbs[hb_i % 2], ribs[hb_i % 2]
                    if tb == 0 and sq["s"] is None:
                        ini_r, ini_i = 0.0, 0.0
                        rd_ = []
                    else:
                        kk = 0 if tb == 0 else int(math.log2(TB))
                        uc, us = Uc[:, kk, i:i + 1], Us[:, kk, i:i + 1]
                        cr, ci = car[:, 0, i:i + 1], car[:, 1, i:i + 1]
                        P.ts(cin[:, 0:1], ci, us, -1.0, ALU.mult, ALU.mult)
                        P.stt(cin[:, 0:1], cr, uc, cin[:, 0:1], ALU.mult, ALU.add)
                        P.ts(cin[:, 1:2], cr, us, None, ALU.mult)
                        P.stt(cin[:, 1:2], ci, uc, cin[:, 1:2], ALU.mult, ALU.add)
                        ini_r, ini_i = cin[:, 0:1], cin[:, 1:2]
                        rd_ = [cin]
                    if bi + 1 < len(blocks):
                        front_a(bi + 1)
                    mbc = mag[:, i:i + 1].to_broadcast([128, TB])
                    for (o_, x_, in_) in ((rr_, xr, ini_r), (ri_, xi, ini_i)):
                        P.add("dve", (lambda o_=o_, x_=x_, in_=in_, mbc=mbc, TB=TB:
                                      (lambda e: e.tensor_tensor_scan(o_[:, 0:TB], mbc, x_[:, 0:TB], in_, ALU.mult, ALU.add)))(),
                              [x_[:, 0:TB], mag[:, i:i + 1]] + rd_, [o_[:, 0:TB]])
                    P.copy(car[:, 0, i:i + 1], rr_[:, TB - 1:TB], eng="act")
                    P.copy(car[:, 1, i:i + 1], ri_[:, TB - 1:TB], eng="act")
                    P.copy(rrb[:, 0:TB], rr_[:, 0:TB], eng="act")
                    P.copy(rib[:, 0:TB], ri_[:, 0:TB], eng="act")
                    if tb == sq["L"] // TB - 1:
                        fin = fins[si]
                        cc, ss = cosT[:, TB - 1:TB], sinT[:, TB - 1:TB]
                        P.ts(cin[:, 2:3], ri_[:, TB - 1:TB], ss, -1.0, ALU.mult, ALU.mult)
                        P.stt(fin[:, 0, i:i + 1], rr_[:, TB - 1:TB], cc, cin[:, 2:3], ALU.mult, ALU.add)
                        P.ts(cin[:, 3:4], rr_[:, TB - 1:TB], ss, None, ALU.mult)
                        P.stt(fin[:, 1, i:i + 1], ri_[:, TB - 1:TB], cc, cin[:, 3:4], ALU.mult, ALU.add)
                    if bi + 1 < len(blocks):
                        front_b(bi + 1)
                    hb = self.V(B0 + 28672 + 2048 * (hb_i % 2), BF16, 2, 512)
                    hb_i += 1
                    c_ = cb_[:, 0:TB]
                    P.tt(a_, rrb[:, 0:TB], cosb[:, 0:TB], ALU.mult)
                    P.tt(b_, rib[:, 0:TB], sinb[:, 0:TB], ALU.mult)
                    P.tt(c_, rrb[:, 0:TB], nsinb[:, 0:TB], ALU.mult)
                    P.tt(hb[:, 1, 0:TB], rib[:, 0:TB], cosb[:, 0:TB], ALU.mult)
                    P.tt(hb[:, 0, 0:TB], a_, b_, ALU.subtract)
                    P.tt(hb[:, 1, 0:TB], c_, hb[:, 1, 0:TB], ALU.subtract)
                    P.mm(psy[bi][:, 0:TB], wC[:, 0, q, :], hb[:, 0, 0:TB], start=(q == 0), stop=False)
                    P.mm(psy[bi][:, 0:TB], wC[:, 1, q, :], hb[:, 1, 0:TB], start=False, stop=(q == 3))
            for bi, (si, tb, TB, c0) in enumerate(blocks):
                yt = ytmp[:, 0:TB]
                P.stt(yt, Z[:, cb, c0:c0 + TB], self.sc("ssm_d", 6 * l + cb), psy[bi][:, 0:TB], ALU.mult, ALU.add)
                P.act(XA[:, cb, c0:c0 + TB], yt, AF.Gelu_apprx_tanh)
        for si in range(3):
            P.dma(self.ssm_o[l, si].rearrange("r p i -> p r i"), fins[si], q="sp")
        self.linear(self.w_glu[l], 6, lambda k, c0, n: XA[:, k, c0:c0 + n], [(m * 128, 128) for m in range(6)], TILES,
                    lambda mi, c0, n, ps: P.act(Z[:, mi, c0:c0 + n], ps, AF.Sigmoid, bias=self.sc("b_glu", 6 * l + mi)))
        for m in range(6):
            P.tt(XA[:, m, :], XA[:, m, :], Z[:, m, :], ALU.mult)

    def ckv(self):
        P = self.P
        H, XA = self.H, self.XA
        B0 = PH_OFF
        self.rms_fm(lambda k, c0, n: H[:, k, c0:c0 + n], 8, TILES, "kv_norm", 0,
                    lambda k, c0, n: XA[:, k, c0:c0 + n], 1.0 / D)
        wv = self.V(B0, BF16, 8, 288)
        P.dma(wv, self.w_dkv.rearrange("(k p) o -> p k o", p=128), q="pool")
        S0 = B0 + 8192
        lst = self.V(B0 + 16384, F32, 2, 512)
        rC = self.V(B0 + 20480, F32, 512)
        rS = self.V(B0 + 22528, F32, 512)
        knf = self.V(B0 + 24576, F32, 512)
        t1 = self.V(B0 + 26624, F32, 512)
        t2 = self.V(B0 + 28672, F32, 512)
        krs = self.V(B0 + 30720, F32, 512)
        for (c0, n) in TILES:
            pss = [self.bank(), self.bank(), self.bank()]
            for mc, (m0, mn) in enumerate([(0, 128), (128, 128), (256, 32)]):
                o = pss[mc][0:128, 0:n] if mc < 2 else pss[mc][64:96, 0:n]
                for k in range(8):
                    P.mm(o, wv[:, k, m0:m0 + mn], XA[:, k, c0:c0 + n], start=(k == 0), stop=(k == 7))
            rstd = self.rms_stats([pss[0][:, 0:n], pss[1][:, 0:n]], self.ones, n, 1.0 / 256, [S0, S0 + 2048], S0 + 4096, S0 + 6144)
            for mc in range(2):
                P.stt(lst[:, mc, 0:n], pss[mc][:, 0:n], self.sc("lat_norm", mc), rstd, ALU.mult, ALU.mult)
                P.dma(self.lat_o[mc * 128:(mc + 1) * 128, c0:c0 + n], lst[:, mc, 0:n], q="sp")
                P.copy(self.latn[:, mc, c0:c0 + n], lst[:, mc, 0:n], eng="act")
            pk = pss[2][64:96, 0:n]
            rstd2 = self.rms_stats([pk], self.boq[64:96, 64:96], n, 1.0 / 32, [S0], S0 + 4096, S0 + 6144, p0=64, p1=96)
            P.stt(knf[64:96, 0:n], pk, self.sc("krope_g", 0, 64, 96), rstd2, ALU.mult, ALU.mult)
            self.rope_apply(knf, krs, c0, n, rC, rS, t1, t2)
            P.dma(self.kr_o[:, c0:c0 + n], krs[64:96, 0:n], q="sp")
            P.copy(self.krn[64:96, c0:c0 + n], krs[64:96, 0:n], eng="act")

    def rope_apply(self, src, dst, c0, n, rC, rS, t1, t2):
        P = self.P
        P.dma(rC[64:96, 0:n], self.rope[0][:, c0:c0 + n], q="sp")
        P.dma(rS[64:96, 0:n], self.rope[1][:, c0:c0 + n], q="sp")
        pr = self.bank()
        P.mm(pr[64:96, 0:n], self.rrot[64:96, :], src[64:96, 0:n])
        P.tt(t1[64:96, 0:n], pr[64:96, 0:n], rS[64:96, 0:n], ALU.mult)
        P.tt(t2[64:96, 0:n], src[64:96, 0:n], rC[64:96, 0:n], ALU.mult)
        P.tt(dst[64:96, 0:n], t1[64:96, 0:n], t2[64:96, 0:n], ALU.add)

    def mla(self, j):
        P = self.P
        Z, XA = self.Z, self.XA
        B0 = PH_OFF
        self.rms_fm(lambda k, c0, n: Z[:, k, c0:c0 + n], 6, TILES, "qlat_g", 6 * j,
                    lambda k, c0, n: Z[:, k, c0:c0 + n], 1.0 / MIX)
        wuk = self.V(B0, BF16, 2, MIX)
        wuv = self.V(B0 + 3072, BF16, 2, MIX)
        P.dma(wuk, self.w_uk.rearrange("(k p) o -> p k o", p=128), q="pool")
        P.dma(wuv, self.w_uv.rearrange("(k p) o -> p k o", p=128), q="pool")
        wqs = [self.V(B0 + 6144 + 1536 * i, BF16, 6, 128) for i in range(2)]
        qnf = self.V(B0 + 28672, F32, 512)
        SQ, RS, RSTD = B0 + 30720, B0 + 31744, B0 + 33792
        rC = self.V(B0 + 38912, F32, 512)
        rS = self.V(B0 + 40960, F32, 512)
        t1 = self.V(B0 + 43008, F32, 512)
        t2 = self.V(RS, F32, 512)
        acc = self.V(B0 + 25600, F32, 12, 32)
        fz = self.V(B0 + 43008, F32, 512)
        PT0 = B0 + 35840
        qscale = self.sc("qscale", 0)
        self.rr = [0, 1, 2, 3, 4, 5]
        st = dict(pt=0, u=0)

        def head_build(h, lat, nkeys, Kcat, Vh):
            stages = []
            nkc = (nkeys + 127) // 128
            for kt in range(0, nkeys, 512):
                kn = min(512, nkeys - kt)
                box = {}

                def s1(kt=kt, kn=kn, box=box):
                    ps = self.bank()
                    box["ps"] = ps
                    box["b"] = self.last_bank
                    self.held.add(self.last_bank)
                    for kc2 in range(2):
                        P.mm(ps[0:64, 0:kn], wuk[:, kc2, h * 64:(h + 1) * 64], lat[:, kc2, kt:kt + kn], start=(kc2 == 0), stop=(kc2 == 1))
                    sq = self.V(SQ, BF16, 512, p0=0, p1=64)[:, 0:kn]
                    P.act(sq, ps[0:64, 0:kn], AF.Square)

                def s2(kt=kt, kn=kn, box=box):
                    sq = self.V(SQ, BF16, 512, p0=0, p1=64)[:, 0:kn]
                    p2 = self.bank()
                    box["p2"] = p2
                    P.mm(p2[0:64, 0:kn], self.bo64[0:64, 0:64], sq)
                    rs = self.V(RS, F32, 512, p0=0, p1=64)[:, 0:kn]
                    P.act(rs, p2[0:64, 0:kn], AF.Ln, bias=EPS, scale=1.0 / 64)
                    rstd = self.V(RSTD, F32, 512, p0=0, p1=64)[:, 0:kn]
                    P.act(rstd, rs, AF.Exp, scale=-0.5)

                def s3(kt=kt, kn=kn, box=box):
                    rstd = self.V(RSTD, F32, 512, p0=0, p1=64)[:, 0:kn]
                    P.stt(Kcat[0:64, kt:kt + kn], box["ps"][0:64, 0:kn], self.sc("knope_g", 0, 0, 64), rstd, ALU.mult, ALU.mult)
                    self.held.discard(box["b"])
                stages += [s1, s2, s3]
            for g0 in range(0, nkc, 8):
                def sv(g0=g0):
                    ps = self.bank()
                    g1 = min(nkc, g0 + 8)
                    for kc in range(g0, g1):
                        kn = min(128, nkeys - kc * 128)
                        for kc2 in range(2):
                            P.mm(ps[0:kn, (kc - g0) * 64:(kc - g0 + 1) * 64], lat[:, kc2, kc * 128:kc * 128 + kn],
                                 wuv[:, kc2, h * 64:(h + 1) * 64], start=(kc2 == 0), stop=(kc2 == 1))
                    knl = min(128, nkeys - (g1 - 1) * 128)
                    vo = (h % 2) * 64
                    if knl == 128:
                        P.copy(Vh[:, g0:g1, vo:vo + 64], ps[:, 0:(g1 - g0) * 64].rearrange("p (a b) -> p a b", b=64))
                    else:
                        if g1 - 1 > g0:
                            P.copy(Vh[:, g0:g1 - 1, vo:vo + 64], ps[:, 0:(g1 - 1 - g0) * 64].rearrange("p (a b) -> p a b", b=64))
                        P.copy(Vh[0:knl, g1 - 1, vo:vo + 64], ps[0:knl, (g1 - 1 - g0) * 64:(g1 - g0) * 64])
                stages.append(sv)
            return stages

        def load_wq(h):
            P.dma(wqs[h % 2], self.w_uq[j][:, h * 128:(h + 1) * 128].rearrange("(k p) o -> p k o", p=128), q="pool")

        def q_build(h, c0, n, Qdst):
            wq = wqs[h % 2]
            box = {}

            def s1():
                psq = self.bank()
                box["psq"] = psq
                box["b"] = self.last_bank
                self.held.add(self.last_bank)
                for k in range(6):
                    P.mm(psq[:, 0:n], wq[:, k, :], Z[:, k, c0:c0 + n], start=(k == 0), stop=(k == 5))
                sq = self.V(SQ, BF16, 512)[:, 0:n]
                P.act(sq, psq[:, 0:n], AF.Square)
                P.dma(rC[64:96, 0:n], self.rope[0][:, c0:c0 + n], q="sp")
                P.dma(rS[64:96, 0:n], self.rope[1][:, c0:c0 + n], q="sp")

            def s2():
                sq = self.V(SQ, BF16, 512)[:, 0:n]
                p2 = self.bank()
                P.mm(p2[:, 0:n], self.boq, sq)
                rs = self.V(RS, F32, 512)[:, 0:n]
                P.act(rs, p2[:, 0:n], AF.Ln, bias=EPS, scale=qscale)
                rstd = self.V(RSTD, F32, 512)[:, 0:n]
                P.act(rstd, rs, AF.Exp, scale=-0.5)

            def s3():
                rstd = self.V(RSTD, F32, 512)[:, 0:n]
                P.stt(qnf[:, 0:n], box["psq"][:, 0:n], self.sc("q_g", j), rstd, ALU.mult, ALU.mult)
                P.copy(Qdst[0:64, :], qnf[0:64, 0:n])
                self.held.discard(box["b"])

            def s4():
                pr = self.bank()
                box["pr"] = pr
                box["b2"] = self.last_bank
                self.held.add(self.last_bank)
                P.mm(pr[64:96, 0:n], self.rrot[64:96, :], qnf[64:96, 0:n])
                P.tt(t2[64:96, 0:n], qnf[64:96, 0:n], rC[64:96, 0:n], ALU.mult)

            def s5_():
                P.tt(t1[64:96, 0:n], box["pr"][64:96, 0:n], rS[64:96, 0:n], ALU.mult)
                P.tt(Qdst[64:96, :], t1[64:96, 0:n], t2[64:96, 0:n], ALU.add)
                self.held.discard(box["b2"])
            return [s1, s2, s3, s4, s5_]

        def run_all(stages):
            for f in stages:
                f()

        def core(Kcat, Vh, Q, n, kcs, nkeys, diag0, pso, psd, hp, fillers=()):
            G = 512 // n if n < 512 else 1
            groups = [kcs[gi:gi + G] for gi in range(0, len(kcs), G)]

            def S(grp):
                pss = self.bank()
                pb_ = self.last_bank
                self.held.add(pb_)
                pT = self.V(PT0 + 1024 * (st["pt"] % 3), BF16, 512)
                st["pt"] += 1
                cl = 0
                for gj, kc in enumerate(grp):
                    kn = min(128, nkeys - kc * 128)
                    diag = diag0 is not None and kc >= diag0
                    if diag:
                        cl = min(128 * (kc - diag0), n - 128)
                    P.mm(pss[0:kn, gj * n + cl:(gj + 1) * n], Kcat[0:96, kc * 128:kc * 128 + kn], Q[0:96, cl:n], start=True, stop=not diag)
                    if diag:
                        P.mm(pss[0:kn, cl:n], self.negI, self.masks[:, kc - diag0, cl:n], start=False, stop=True)
                return (grp, pss, pT, pb_, cl)

            def E(item):
                grp, pss, pT, pb_, cl = item
                self.held.discard(pb_)
                full = [kc for kc in grp if min(128, nkeys - kc * 128) == 128]
                if full:
                    P.act(pT[:, cl:len(full) * n], pss[:, cl:len(full) * n], AF.Exp, scale=MLA_SCALE)
                if len(full) < len(grp):
                    kn = nkeys - grp[-1] * 128
                    gj = len(grp) - 1
                    P.act(pT[0:kn, gj * n:(gj + 1) * n], pss[0:kn, gj * n:(gj + 1) * n], AF.Exp, scale=MLA_SCALE)
                for gj, kc in enumerate(grp):
                    kn = min(128, nkeys - kc * 128)
                    first = (kc == kcs[0])
                    last = (kc == kcs[-1])
                    P.mm(pso[:, cl:n], Vh[0:kn, kc, :], pT[0:kn, gj * n + cl:(gj + 1) * n], start=first, stop=last)

            fillers = list(fillers)
            per = -(-len(fillers) // max(1, len(groups)))
            pend = []
            for grp in groups:
                pend.append(S(grp))
                if len(pend) > 2:
                    E(pend.pop(0))
                for _ in range(per):
                    if fillers:
                        fillers.pop(0)()
            while pend:
                E(pend.pop(0))
            run_all(fillers)

        def acc_banks():
            u = st["u"]
            st["u"] += 1
            return self.bank(6 + (u % 2)), None

        Kc = [self.V(B0 + 9216, BF16, 2048), self.V(B0 + 17408, BF16, 2048)]
        Vhs = [self.V(B0 + 13312, BF16, 16, 128), self.V(B0 + 21504, BF16, 16, 128)]
        Qcs = [self.V(B0 + 25600 + 1024 * i, BF16, 512) for i in range(3)]
        P.memset(Vhs[0][:, :, 64:128], 1.0)
        P.memset(Vhs[1][:, :, 0:64], 1.0)
        ptiles = [(c0, n) for (c0, n) in TILES if c0 < TP]
        latp = self.latn[:, :, 0:TP]
        for b_ in range(2):
            P.copy(Kc[b_][64:96, 0:TP], self.krn[64:96, 0:TP], eng="act")
        units = [(h, c0, n) for h in range(12) for (c0, n) in ptiles]
        load_wq(0)
        run_all(head_build(0, latp, TP, Kc[0], Vhs[0]))
        for u0 in range(2):
            h0, c0_, n0_ = units[u0]
            run_all(q_build(h0, c0_, n0_, Qcs[u0 % 3][:, 0:n0_]))
        for ui, (h, c0, n) in enumerate(units):
            fill = []
            if ui + 1 < len(units) and units[ui + 1][0] != h:
                fill += head_build(units[ui + 1][0], latp, TP, Kc[units[ui + 1][0] % 2], Vhs[units[ui + 1][0] % 2])
            if ui + 2 < len(units):
                h2, c2, n2 = units[ui + 2]
                if h2 != units[ui + 1][0] or (ui == 0 and False):
                    load_wq(h2)
                fill += q_build(h2, c2, n2, Qcs[(ui + 2) % 3][:, 0:n2])
            hp = (h % 2) * 64
            qt = c0 // 512
            pso, psd = acc_banks()
            core(Kc[h % 2], Vhs[h % 2], Qcs[ui % 3][:, 0:n], n, list(range(4 * (qt + 1))), TP, 4 * qt, pso, psd, hp, fill)
            dp = 64 - hp
            lg = fz[dp:dp + 64, 0:n]
            P.recip(lg, pso[dp:dp + 64, 0:n])
            P.tt(XA[hp:hp + 64, h // 2, c0:c0 + n], pso[hp:hp + 64, 0:n], lg, ALU.mult)

        SBK = 1024
        latsb = self.V(B0 + 9216, BF16, 2, SBK)
        KcS = [self.V(B0 + 13312, BF16, SBK), self.V(B0 + 17408, BF16, SBK)]
        VhS = [self.V(B0 + 15360, BF16, 8, 128), self.V(B0 + 19456, BF16, 8, 128)]
        Qs = self.V(B0 + 21504, BF16, 12, LS)
        P.memset(VhS[0][:, :, 64:128], 1.0)
        P.memset(VhS[1][:, :, 0:64], 1.0)
        for sq in SEQS[1:]:
            c0, n = sq["c0"], sq["L"]
            for h in range(12):
                if h % 2 == 0:
                    load_wq(h)
                    if h + 1 < 12:
                        load_wq(h + 1)
                run_all(q_build(h, c0, n, Qs[:, h, :]))
            sbs = [("cache", k0_, SBK) for k0_ in range(0, PAST, SBK)] + [("new", c0, n)]
            for sbi, (kind, k0, nkeys) in enumerate(sbs):
                reuse = (kind == "cache" and j == 1)
                spill = (kind == "cache" and j == 0)
                if kind == "new":
                    lat = self.latn[:, :, k0:k0 + nkeys]
                    for b_ in range(2):
                        P.copy(KcS[b_][64:96, 0:nkeys], self.krn[64:96, k0:k0 + nkeys], eng="act")
                else:
                    lat = latsb[:, :, 0:nkeys]
                    if not reuse:
                        for kc2 in range(2):
                            P.dma(latsb[:, kc2, 0:nkeys], self.latc[sq["s"], kc2 * 128:(kc2 + 1) * 128, k0:k0 + nkeys], q="pool")
                    for b_ in range(2):
                        P.dma(KcS[b_][64:96, 0:nkeys], self.krc[sq["s"], :, k0:k0 + nkeys], q="pool")
                nkc = (nkeys + 127) // 128

                def hb_(h):
                    b_ = h % 2
                    vo = b_ * 64
                    if reuse:
                        P.dma(KcS[b_][0:64, 0:nkeys], self.kscr[sq["s"], h, :, k0:k0 + nkeys], q="sp")
                        P.dma(VhS[b_][:, :, vo:vo + 64], self.vscr[sq["s"], h, sbi].rearrange("p (a b) -> p a b", b=64), q="sp")
                        return
                    run_all(head_build(h, lat, nkeys, KcS[b_], VhS[b_]))
                    if spill:
                        P.dma(self.kscr[sq["s"], h, :, k0:k0 + nkeys], KcS[b_][0:64, 0:nkeys], q="sp")
                        P.dma(self.vscr[sq["s"], h, sbi].rearrange("p (a b) -> p a b", b=64), VhS[b_][:, :, vo:vo + 64], q="sp")

                hb_(0)
                for h in range(12):
                    if h + 1 < 12:
                        hb_(h + 1)
                    hp = (h % 2) * 64
                    pso, psd = acc_banks()
                    core(KcS[h % 2], VhS[h % 2], Qs[:, h, :], n, list(range(nkc)), nkeys, None, pso, psd, hp)
                    ah = acc[:, h, :]
                    if sbi == 0:
                        P.copy(ah, pso[:, 0:n])
                    else:
                        P.tt(ah, ah, pso[:, 0:n], ALU.add)
                    if sbi == len(sbs) - 1:
                        dp = 64 - hp
                        rdh = fz[hp:hp + 64, 0:n]
                        P.recip(rdh, acc[dp:dp + 64, h, :])
                        P.tt(XA[hp:hp + 64, h // 2, c0:c0 + n], acc[hp:hp + 64, h, :], rdh, ALU.mult)
        self.rr = list(range(8))

    def ffn(self, l):
        P = self.P
        H, XA, Z = self.H, self.XA, self.Z
        B0 = PH_OFF
        self.rms_fm(lambda k, c0, n: H[:, k, c0:c0 + n], 8, TILES, "norm_ffn", 8 * l,
                    lambda k, c0, n: XA[:, k, c0:c0 + n], 1.0 / D)
        UW = T + 6
        ub = [self.V(B0 + 32768 + 4352 * i, BF16, 2176)[:, 0:UW] for i in range(2)]
        sg = self.V(B0 + 41472, F32, 512)
        Dm = self.V(B0 + 43520, BF16, 6, 128)
        cbufL = self.V(CONVB_OFF, F32, 3, 88)
        ubase = [sq["c0"] + 2 * si for si, sq in enumerate(SEQS)]
        groups = [list(range(0, 8)), list(range(8, 16)), list(range(16, 22))]
        Wi = self.w_ffn_in[l]
        Wo = self.w_ffn_out[l]
        for grp in groups:
            wblk = {}
            for jj, jc in enumerate(grp):
                if jj % 4 == 0:
                    nb_ = min(4, len(grp) - jj) * 128
                    for part in range(2):
                        so = self.slot()
                        wv = self.V(so, BF16, 8, 512)[:, :, 0:nb_]
                        P.dma(wv, Wi[:, part * DFF + jc * 128: part * DFF + jc * 128 + nb_].rearrange("(k p) o -> p k o", p=128), q="pool")
                        wblk[part] = wv
                for part in range(2):
                    wv = wblk[part]
                    col = part * NJ + jc
                    for si, sq in enumerate(SEQS):
                        if sq["s"] is None:
                            P.memset(ub[part][:, ubase[si]:ubase[si] + 2], 0.0)
                        else:
                            P.dma(ub[part][:, ubase[si]:ubase[si] + 2], self.cst[l, sq["s"]][:, 2 * col:2 * col + 2], q="pool")
                    for k3 in range(3):
                        P.ts(Dm[:, part * 3 + k3, :], self.identb, self.sc("conv_w", l * 132 + k3 * 44 + col), None, ALU.mult)
                    for (c0, n) in TILES:
                        ps = self.bank()
                        for k in range(8):
                            P.mm(ps[:, 0:n], wv[:, k, (jj % 4) * 128:(jj % 4 + 1) * 128], XA[:, k, c0:c0 + n], start=(k == 0), stop=(k == 7))
                        if c0 < TP:
                            P.copy(ub[part][:, 2 + c0:2 + c0 + n], ps[:, 0:n], eng="act")
                            if c0 + n == TP:
                                P.copy(cbufL[:, 0, 2 * col:2 * col + 2], ps[:, n - 2:n])
                        else:
                            for si in (1, 2):
                                o = (si - 1) * LS
                                P.copy(ub[part][:, ubase[si] + 2:ubase[si] + 2 + LS], ps[:, o:o + LS], eng="act")
                                P.copy(cbufL[:, si, 2 * col:2 * col + 2], ps[:, o + LS - 2:o + LS])
                for si, sq in enumerate(SEQS):
                    tl = [(c0, n) for (c0, n) in TILES if c0 < TP] if sq["s"] is None else [(sq["c0"], sq["L"])]
                    for (c0, n) in tl:
                        pa = self.bank()
                        pg = self.bank()
                        u0 = ubase[si] + (c0 - sq["c0"])
                        for part, pp in ((0, pa), (1, pg)):
                            for k3 in range(3):
                                P.mm(pp[:, 0:n], Dm[:, part * 3 + k3, :], ub[part][:, u0 + k3:u0 + k3 + n], start=(k3 == 0), stop=(k3 == 2))
                        P.act(sg[:, 0:n], pg[:, 0:n], AF.Silu, bias=self.sc("conv_b", l * 44 + NJ + jc))
                        P.stt(Z[:, jj, c0:c0 + n], pa[:, 0:n], self.sc("conv_b", l * 44 + jc), sg[:, 0:n], ALU.add, ALU.mult)
            ng = len(grp)
            for half in range(2):
                so = self.slot()
                wv = self.V(so, BF16, 8, 512)[:, 0:ng, :]
                P.dma(wv, Wo[grp[0] * 128:(grp[-1] + 1) * 128, half * 512:(half + 1) * 512].rearrange("(k p) o -> p k o", p=128), q="pool")
                for mo in range(4):
                    m = half * 4 + mo
                    for (c0, n) in TILES:
                        ps = self.bank()
                        for jj in range(ng):
                            P.mm(ps[:, 0:n], wv[:, jj, mo * 128:(mo + 1) * 128], Z[:, jj, c0:c0 + n], start=(jj == 0), stop=(jj == ng - 1))
                        P.tt(H[:, m, c0:c0 + n], H[:, m, c0:c0 + n], ps[:, 0:n], ALU.add)
        P.dma(self.conv_o[l].rearrange("s p c -> p s c"), cbufL, q="sp")


_CACHE = {}


def _get_nc():
    if "nc" not in _CACHE:
        nc = bass.Bass("TRN2", target_bir_lowering=False)
        K(nc).build()
        _CACHE["nc"] = nc
    return _CACHE["nc"]


def _pp(v):
    v = np.asarray(v, np.float32)
    return np.ascontiguousarray(v.reshape(-1, 128).T)


def _consts():
    cf = np.zeros((128, NCF), np.float32)
    R = np.zeros((32, 32), np.float32)
    for m in range(16):
        R[16 + m, m] = -1.0
        R[m, 16 + m] = 1.0
    cf[64:96, 0:32] = R
    cb = np.zeros((128, NCB), np.float32)
    cb[:, 0:128] = 1.0
    cb[:, 128:256] = -30000.0 * np.eye(128, dtype=np.float32)
    cb[:, 256:384] = np.eye(128, dtype=np.float32)
    cb[0:64, 384:448] = 1.0
    cb[64:128, 448:512] = 1.0
    cb[0:64, 512:576] = 1.0
    cb[64:96, 576:608] = 1.0
    p = np.arange(128)[:, None]
    c = np.arange(512)[None, :]
    for i in range(4):
        cb[:, 640 + 512 * i: 640 + 512 * (i + 1)] = (((128 * i + p) // 64) > (c // 64)).astype(np.float32)
    return cf, cb


def _rope_tables():
    inv = (1.0 / (np.float32(10000.0) ** (np.arange(0, 32, 2, dtype=np.float32) / np.float32(32)))).astype(np.float32)
    pos = np.concatenate([np.arange(TP), PAST + np.arange(LS), PAST + np.arange(LS)]).astype(np.float32)
    ang = (pos[:, None] * inv[None, :]).astype(np.float32)
    c = np.cos(ang).astype(np.float32).T
    s = np.sin(ang).astype(np.float32).T
    return np.ascontiguousarray(np.stack([np.concatenate([c, c], 0), np.concatenate([s, s], 0)], 0))


def _pack_shared(inp):
    f = lambda k: np.asarray(inp[k], np.float32)
    sc = np.zeros((128, NSC), np.float32)

    def put(name, col, arr):
        arr = np.asarray(arr, np.float32)
        if arr.ndim == 1:
            arr = arr[:, None]
        sc[:arr.shape[0], SCOL[name] + col: SCOL[name] + col + arr.shape[1]] = arr

    for l in range(DEPTH):
        put("norm_mix", 8 * l, _pp(f("norm_mix_g")[l]))
        put("norm_ffn", 8 * l, _pp(f("norm_ffn_g")[l]))
        put("mem_norm", 8 * l, _pp(f("mem_norm_g")[l]))
        put("memq_g", l, np.tile(f("mem_q_norm_g")[l], 2))
        put("memk_g", l, np.tile(f("mem_k_norm_g")[l], 2))
        cw = f("ffn_conv_w")[l]
        for k3 in range(3):
            put("conv_w", l * 132 + k3 * 44, _pp(cw[k3]))
        put("conv_b", l * 44, _pp(f("ffn_conv_b")[l]))
    put("kv_norm", 0, _pp(f("kv_norm_g")))
    put("lat_norm", 0, _pp(f("latent_norm_g")))
    kg = np.zeros(128, np.float32)
    kg[64:96] = f("krope_norm_g")
    put("krope_g", 0, kg)
    kn = np.zeros(128, np.float32)
    kn[0:64] = f("k_nope_norm_g")
    put("knope_g", 0, kn)
    qs = np.zeros(128, np.float32)
    qs[0:64] = 1.0 / 64
    qs[64:96] = 1.0 / 32
    put("qscale", 0, qs)
    for j in range(2):
        qg = np.zeros(128, np.float32)
        qg[0:64] = f("q_nope_norm_g")[j]
        qg[64:96] = f("q_rope_norm_g")[j]
        put("q_g", j, qg)
        put("qlat_g", 6 * j, _pp(f("q_latent_norm_g")[j]))
        put("ssm_d", 6 * j, _pp(f("ssm_d")[j]))
        put("b_glu", 6 * j, _pp(f("b_glu")[j]))
        put("ssm_p", 72 * j, _pp(f("ssm_a_re")[j].reshape(-1)))
        put("ssm_p", 72 * j + 24, _pp(f("ssm_a_im")[j].reshape(-1)))
        put("ssm_p", 72 * j + 48, _pp(np.repeat(f("ssm_log_dt")[j], 64)))
    s5b = np.zeros((NA, 2, 128, 24, 128), np.float32)
    s5c = np.zeros((NA, 2, 128, 24, 128), np.float32)
    for l in range(NA):
        for r, (bk, ck) in enumerate((("ssm_b_re", "ssm_c_re"), ("ssm_b_im", "ssm_c_im"))):
            b = f(bk)[l]
            c = f(ck)[l]
            for i in range(24):
                q = i % 4
                for gg in range(2):
                    g = 2 * i + gg
                    rows = slice(32 * q + 16 * gg, 32 * q + 16 * gg + 16)
                    cols = slice(64 * gg, 64 * gg + 64)
                    s5b[l, r, rows, i, cols] = b[g].T
                    s5c[l, r, cols, i, rows] = c[g].T
    wuq = np.zeros((2, MIX, 12, 128), np.float32)
    wuq[:, :, :, 0:96] = f("w_uq").reshape(2, MIX, 12, 96)
    cf, cb = _consts()
    return dict(scal=sc, consf=cf, consb=cb, rope=_rope_tables(),
                w_mix_in=f("w_mix_in"), w_mix_out=f("w_mix_out"), w_ffn_in=f("w_ffn_in"), w_ffn_out=f("w_ffn_out"),
                w_mem_kv=f("w_mem_kv"), w_glu=f("w_glu"), w_dkv=f("w_dkv"), w_uk=f("w_uk"), w_uv=f("w_uv"),
                w_uq=np.ascontiguousarray(wuq.reshape(2, MIX, 12 * 128)),
                s5b=np.ascontiguousarray(s5b.reshape(NA, 2, 128, 24 * 128)),
                s5c=np.ascontiguousarray(s5c.reshape(NA, 2, 128, 24 * 128)))


def _pack_core(inp, c):
    f = lambda k: np.asarray(inp[k], np.float32)
    s0, s1 = 2 * c, 2 * c + 1
    xT = np.concatenate([f("x_prompt")[c].T, f("x_sample")[s0].T, f("x_sample")[s1].T], axis=1)
    d = dict(xT=np.ascontiguousarray(xT), memT=np.ascontiguousarray(f("mem_prompt")[c].T))
    d["latc"] = np.ascontiguousarray(np.stack([f("cache_mla_latent")[s].T for s in (s0, s1)]))
    d["krc"] = np.ascontiguousarray(np.stack([f("cache_mla_krope")[s].T for s in (s0, s1)]))
    d["cmk"] = np.ascontiguousarray(np.stack([np.stack([f("cache_mem_k")[l, s].reshape(256, 256).T for s in (s0, s1)]) for l in range(DEPTH)]))
    d["cmv"] = np.ascontiguousarray(np.stack([np.stack([f("cache_mem_v")[l, s].reshape(256, 256) for s in (s0, s1)]) for l in range(DEPTH)]))
    d["sst"] = np.ascontiguousarray(np.stack([np.stack([np.stack([_pp(f(k)[l, s].reshape(-1)) for k in ("state_ssm_re", "state_ssm_im")])
                                                        for s in (s0, s1)]) for l in range(NA)]))
    cst = np.zeros((DEPTH, 2, 128, 44, 2), np.float32)
    for l in range(DEPTH):
        for si, s in enumerate((s0, s1)):
            sc_ = f("state_conv")[l, s]
            cst[l, si] = sc_.reshape(2, 44, 128).transpose(2, 1, 0)
    d["cst"] = np.ascontiguousarray(cst.reshape(DEPTH, 2, 128, 88))
    return d


def kernel(**inputs):
    nc = _get_nc()
    shared = _pack_shared(inputs)
    in_maps = []
    for c in range(8):
        d = dict(shared)
        d.update(_pack_core(inputs, c))
        in_maps.append(d)
    res = run_bass_kernel_spmd(nc, in_maps, core_ids=list(range(8)))
    R = res.results
    B, DB = 8, 16
    y_p = np.stack([R[c]["yT"][:, :TP].T for c in range(B)])
    y_s = np.zeros((DB, LS, D), np.float32)
    lat_p = np.stack([R[c]["lat_o"][:, :TP].T for c in range(B)])
    kr_p = np.stack([R[c]["kr_o"][:, :TP].T for c in range(B)])
    lat_s = np.zeros((DB, LS, 256), np.float32)
    kr_s = np.zeros((DB, LS, 32), np.float32)
    memk = np.zeros((DEPTH, B, NMEM, 4, 64), np.float32)
    memv = np.zeros((DEPTH, B, NMEM, 4, 64), np.float32)
    ssm_p = np.zeros((2, NA, B, 48, 64), np.float32)
    ssm_s = np.zeros((2, NA, DB, 48, 64), np.float32)
    conv_p = np.zeros((DEPTH, B, 2, 2 * DFF), np.float32)
    conv_s = np.zeros((DEPTH, DB, 2, 2 * DFF), np.float32)
    for c in range(B):
        r = R[c]
        for l in range(DEPTH):
            memk[l, c] = r["memk_o"][l].T.reshape(NMEM, 4, 64)
            memv[l, c] = r["memv_o"][l].reshape(NMEM, 4, 64)
            cv = r["conv_o"][l].reshape(3, 128, 44, 2)
            for si in range(3):
                arr = cv[si].transpose(2, 1, 0).reshape(2, 2 * DFF)
                if si == 0:
                    conv_p[l, c] = arr
                else:
                    conv_s[l, 2 * c + si - 1] = arr
        for l in range(NA):
            for si in range(3):
                for ri in range(2):
                    arr = r["ssm_o"][l, si, ri].T.reshape(48, 64)
                    if si == 0:
                        ssm_p[ri, l, c] = arr
                    else:
                        ssm_s[ri, l, 2 * c + si - 1] = arr
        for si in range(2):
            cs = slice(TP + si * LS, TP + (si + 1) * LS)
            y_s[2 * c + si] = r["yT"][:, cs].T
            lat_s[2 * c + si] = r["lat_o"][:, cs].T
            kr_s[2 * c + si] = r["kr_o"][:, cs].T
    return (y_p, y_s, memk, memv, lat_p, kr_p, ssm_p[0], ssm_p[1], conv_p, lat_s, kr_s, ssm_s[0], ssm_s[1], conv_s)
```

```python
import math
import numpy as np
import concourse.bass as bass
import concourse.mybir as mybir
from concourse.bass_utils import run_bass_kernel_spmd

F32 = mybir.dt.float32
BF16 = mybir.dt.bfloat16
I32 = mybir.dt.int32
AF = mybir.ActivationFunctionType
ALU = mybir.AluOpType

D = 1024
DEPTH = 4
NA = 2
TP = 2048
LS = 32
T = TP + 2 * LS
PAST = 4096
DFF = 2816
NJ = DFF // 128
MIX = 768
NMEM = 256
EPS = 1e-6
MLA_SCALE = 96 ** -0.5
MEM_SCALE = 64 ** -0.5
TILES = [(0, 512), (512, 512), (1024, 512), (1536, 512), (2048, 64)]
SEQS = [dict(c0=0, L=TP, past=0, s=None), dict(c0=TP, L=LS, past=PAST, s=0), dict(c0=TP + LS, L=LS, past=PAST, s=1)]

H_OFF = 0
XA_OFF = 67584
Z_OFF = 101376
PH_OFF = 135168
PH_SIZE = 46080
PS_OFF = PH_OFF + PH_SIZE
SC_OFF = PS_OFF
CF_OFF = SC_OFF + 4096
CB_OFF = CF_OFF + 128
LATN_OFF = CB_OFF + 5376
KRN_OFF = LATN_OFF + 2 * T * 2
MKT_OFF = KRN_OFF + T * 2
MV_OFF = MKT_OFF + 4096
CONVB_OFF = MV_OFF + 4096
ARENA_BYTES = CONVB_OFF + 1056
assert ARENA_BYTES <= 212800, ARENA_BYTES
NCF = 32
NCB = 2688

ND_SEMS = 24

SCOL = {}
_n = 0
for _name, _w in [("norm_mix", 32), ("norm_ffn", 32), ("mem_norm", 32), ("kv_norm", 8), ("lat_norm", 2),
                  ("krope_g", 1), ("memq_g", 4), ("memk_g", 4), ("q_g", 2), ("knope_g", 1), ("qlat_g", 12),
                  ("ssm_d", 12), ("b_glu", 12), ("ssm_p", 144), ("qscale", 1), ("conv_w", 528), ("conv_b", 176)]:
    SCOL[_name] = _n
    _n += _w
NSC = 1024
assert _n <= NSC


def _foot(ap):
    name = ap.name
    off = int(ap.offset)
    dims = list(ap.ap)
    es = mybir.dt.size(ap.dtype)
    if str(ap.space) == "DRAM":
        if not name.startswith("scr_"):
            return None
        ext = sum((c - 1) * abs(st_) for st_, c in dims) + 1
        return (name, 0, 1, off * es, (off + ext) * es)
    ps, pc = dims[0]
    if ps == 0:
        ps = 1 << 40
    p0 = off // ps
    lo = off % ps
    ext = sum((c - 1) * abs(s) for s, c in dims[1:]) + 1
    if str(ap.space) == "PSUM":
        return ("PSUM", (p0 // 32) * 32, ((p0 + pc + 31) // 32) * 32, (lo * es // 2048) * 2048,
                (((lo + ext) * es + 2047) // 2048) * 2048)
    return (name, p0, p0 + pc, lo * es, (lo + ext) * es)


class Op:
    __slots__ = ("eng", "fn", "deps", "signal", "sigval", "dma", "dsem", "dval", "idx")

    def __init__(self, eng, fn, dma):
        self.eng = eng
        self.fn = fn
        self.deps = []
        self.signal = False
        self.sigval = 0
        self.dma = dma
        self.dsem = -1
        self.dval = 0
        self.idx = 0


PAGE = 2048
SAME_ENG_GAP = 10 ** 9


class Prog:
    def __init__(self, nc):
        self.nc = nc
        self.ops = []
        self.pages = {}
        self.ndma = 0
        self.nq = {"sp": 0, "pool": 0}
        self.ecnt = {}
        self.eseq = {}

    def _recs(self, f):
        seen = set()
        out = []
        for pg in range(f[3] // PAGE, (f[4] - 1) // PAGE + 1):
            lst = self.pages.get((f[0], pg))
            if not lst:
                continue
            alive = [r for r in lst if r[6]]
            if len(alive) != len(lst):
                lst[:] = alive
            for r in alive:
                if id(r) not in seen and r[1] < f[2] and f[1] < r[2] and r[3] < f[4] and f[3] < r[4]:
                    seen.add(id(r))
                    out.append(r)
        return out

    def _put(self, rec, f):
        for pg in range(f[3] // PAGE, (f[4] - 1) // PAGE + 1):
            self.pages.setdefault((f[0], pg), []).append(rec)

    def add(self, eng, fn, reads=(), writes=(), dma=False):
        op = Op(eng, fn, dma)
        op.idx = len(self.ops)
        if dma:
            half = ND_SEMS // 2
            i = self.nq[eng]
            self.nq[eng] += 1
            op.dsem = (i % half) + (0 if eng == "sp" else half)
            op.dval = (i // half + 1) * 16
            self.ndma += 1
        rf = [x for x in (_foot(a) for a in reads) if x is not None]
        wf = [x for x in (_foot(a) for a in writes) if x is not None]
        deps = {}

        myseq = self.ecnt.get(eng, 0)
        self.ecnt[eng] = myseq + 1
        self.eseq[op.idx] = myseq

        def need(o, raw, psum=False):
            if (not o.dma) and (not dma) and o.eng == eng:
                if eng == "pe" or not raw or psum:
                    return
                if myseq - self.eseq[o.idx] >= SAME_ENG_GAP:
                    return
            deps[o.idx] = o

        prf = [f for f in rf if f[0] == "PSUM"]
        rf = [f for f in rf if f[0] != "PSUM"]
        for f in prf:
            for r in self._recs(f):
                need(r[0], True, True)
        for f in rf:
            for r in self._recs(f):
                if r[5]:
                    need(r[0], True)
        for f in wf:
            for r in self._recs(f):
                need(r[0], False, f[0] == "PSUM")
                if f[1] <= r[1] and r[2] <= f[2] and f[3] <= r[3] and r[4] <= f[4]:
                    r[6] = False
        for f in prf:
            for r in self._recs(f):
                if (not r[0].dma) and r[0].eng == eng and r[1] == f[1] and r[2] == f[2] and r[3] == f[3] and r[4] == f[4]:
                    r[6] = False
            self._put([op, f[1], f[2], f[3], f[4], True, True], f)
        best = {}
        dl = []
        for o in deps.values():
            if o.dma:
                dl.append(o)
            elif o.eng not in best or best[o.eng].idx < o.idx:
                best[o.eng] = o
        op.deps = dl + list(best.values())
        for o in op.deps:
            o.signal = True
        for f in wf:
            self._put([op, f[1], f[2], f[3], f[4], True, True], f)
        for f in rf:
            if not dma:
                for r in self._recs(f):
                    if (not r[5]) and (not r[0].dma) and r[0].eng == eng and r[1] == f[1] and r[2] == f[2] \
                            and r[3] == f[3] and r[4] == f[4]:
                        r[6] = False
            self._put([op, f[1], f[2], f[3], f[4], False, True], f)
        self.ops.append(op)
        return op

    def emit(self):
        nc = self.nc
        engs = ["pe", "act", "dve", "pool", "sp"]
        cnt = {e: 0 for e in engs}
        last = {}
        for op in self.ops:
            if not op.dma:
                last[op.eng] = op
        for op in last.values():
            op.signal = True
        for op in self.ops:
            if op.dma:
                op.signal = True
            elif op.signal:
                cnt[op.eng] += 1
                op.sigval = cnt[op.eng]
        per = {e: [o for o in self.ops if o.eng == e] for e in engs}
        ctx = []
        esem = {}
        for e in engs:
            c = nc.semaphore("s_" + e)
            esem[e] = c.__enter__()
            ctx.append(c)
        dsem = []
        for i in range(ND_SEMS):
            c = nc.semaphore("d_%d" % i)
            dsem.append(c.__enter__())
            ctx.append(c)
        dfinal = [0] * ND_SEMS
        for op in self.ops:
            if op.dma:
                dfinal[op.dsem] = max(dfinal[op.dsem], op.dval)

        def run(e, eng):
            known = {}
            for op in per[e]:
                waits = {}
                for d in op.deps:
                    if d.dma:
                        k = ("d", d.dsem)
                        v = d.dval
                    else:
                        k = ("e", d.eng)
                        v = d.sigval
                    if known.get(k, 0) < v:
                        waits[k] = max(waits.get(k, 0), v)
                if op.dma and op.dval > 16:
                    k = ("d", op.dsem)
                    v = op.dval - 16
                    if known.get(k, 0) < v:
                        waits[k] = max(waits.get(k, 0), v)
                for k, v in waits.items():
                    s = dsem[k[1]] if k[0] == "d" else esem[k[1]]
                    eng.wait_ge(s, v)
                    known[k] = v
                ins = op.fn(eng)
                if op.dma:
                    ins.then_inc(dsem[op.dsem], 16)
                elif op.signal:
                    ins.then_inc(esem[e], 1)
            if e == "sp":
                for i in range(ND_SEMS):
                    if dfinal[i] > known.get(("d", i), 0):
                        eng.wait_ge(dsem[i], dfinal[i])
                for e2 in engs:
                    if cnt[e2] > known.get(("e", e2), 0):
                        eng.wait_ge(esem[e2], cnt[e2])
                for s_ in list(esem.values()) + dsem:
                    eng.sem_clear(s_)

        with nc.Block() as block:
            @block.tensor
            def _(eng):
                run("pe", eng)

            @block.scalar
            def _(eng):
                run("act", eng)

            @block.vector
            def _(eng):
                run("dve", eng)

            @block.gpsimd
            def _(eng):
                run("pool", eng)

            @block.sync
            def _(eng):
                run("sp", eng)
        for c in reversed(ctx):
            c.__exit__(None, None, None)

    def dma(self, out, in_, q="sp"):
        return self.add(q, lambda eng: eng.dma_start(out=out, in_=in_), [in_], [out], dma=True)

    def mm(self, out, lhsT, rhs, start=True, stop=True):
        return self.add("pe", lambda eng: eng.matmul(out, lhsT, rhs, start=start, stop=stop), [lhsT, rhs], [out])

    def act(self, out, in_, func, bias=None, scale=1.0):
        reads = [in_]
        kw = {}
        if bias is not None:
            kw["bias"] = bias
            if not isinstance(bias, (int, float)):
                reads.append(bias)
        if not isinstance(scale, (int, float)):
            reads.append(scale)
        return self.add("act", lambda e: e.activation(out, in_, func, scale=scale, **kw), reads, [out])

    def tt(self, out, in0, in1, op, eng="dve"):
        return self.add(eng, lambda e: e.tensor_tensor(out, in0, in1, op), [in0, in1], [out])

    def ts(self, out, in0, s1, s2, op0, op1=None, eng="dve"):
        reads = [in0] + [s for s in (s1, s2) if s is not None and not isinstance(s, (int, float))]
        kw = {}
        if op1 is not None:
            kw["op1"] = op1
        return self.add(eng, lambda e: e.tensor_scalar(out, in0, s1, s2, op0, **kw), reads, [out])

    def stt(self, out, in0, scalar, in1, op0, op1):
        reads = [in0, in1] + ([] if isinstance(scalar, (int, float)) else [scalar])
        return self.add("dve", lambda e: e.scalar_tensor_tensor(out, in0, scalar, in1, op0, op1), reads, [out])

    def copy(self, out, in_, eng="dve"):
        if eng == "act":
            return self.act(out, in_, AF.Copy)
        return self.add(eng, lambda e: e.tensor_copy(out, in_), [in_], [out])

    def memset(self, out, val, eng="dve"):
        return self.add(eng, lambda e: e.memset(out, val), [], [out])

    def recip(self, out, in_):
        return self.add("dve", lambda e: e.reciprocal(out, in_), [in_], [out])


class K:
    def __init__(self, nc):
        self.nc = nc
        self.P = Prog(nc)
        self.ring_i = 0
        self.bank_i = 0
        self.rr = list(range(8))
        self.held = set()
        self.last_bank = 0

    def V(self, off, dt, *shape, p0=0, p1=128):
        es = mybir.dt.size(dt)
        assert off % es == 0
        n = 1
        for s in shape:
            n *= s
        base = self.A if dt == F32 else self.A.bitcast(dt)
        v = base[p0:p1, off // es: off // es + n]
        if len(shape) == 2:
            v = v.rearrange("p (a b) -> p a b", a=shape[0])
        elif len(shape) == 3:
            v = v.rearrange("p (a b c) -> p a b c", a=shape[0], b=shape[1])
        return v

    def sc(self, name, col=0, p0=0, p1=128):
        c = SCOL[name] + col
        return self.V(SC_OFF + 4 * c, F32, 1, p0=p0, p1=p1)

    def bank(self, b=None):
        if b is None:
            while True:
                b = self.rr[self.bank_i % len(self.rr)]
                self.bank_i += 1
                if b not in self.held:
                    break
            self.last_bank = b
        return self.PS[:, b, :]

    def slot(self):
        i = self.ring_i % 4
        self.ring_i += 1
        return PH_OFF + 8192 * i

    def rms_stats(self, srcs, ones_ap, n, scale, sq_offs, rs_off, rstd_off, p0=0, p1=128, K_=None):
        P = self.P
        ps = self.bank()
        for i, s in enumerate(srcs):
            sq = self.V(sq_offs[i % len(sq_offs)], BF16, 512, p0=p0, p1=p1)[:, 0:n]
            P.act(sq, s, AF.Square)
            P.mm(ps[p0:p1, 0:n], ones_ap, sq, start=(i == 0), stop=(i == len(srcs) - 1))
        rs = self.V(rs_off, F32, 512, p0=p0, p1=p1)[:, 0:n]
        P.act(rs, ps[p0:p1, 0:n], AF.Ln, bias=EPS, scale=scale)
        rstd = self.V(rstd_off, F32, 512, p0=p0, p1=p1)[:, 0:n]
        P.act(rstd, rs, AF.Exp, scale=-0.5)
        return rstd

    def rms_fm(self, src, nk, tiles, gname, gcol0, dst, inv_n):
        L = PH_OFF + 32768
        for (c0, n) in tiles:
            rstd = self.rms_stats([src(k, c0, n) for k in range(nk)], self.ones, n, inv_n,
                                  [L, L + 2048], L + 4096, L + 6144)
            for k in range(nk):
                self.P.stt(dst(k, c0, n), src(k, c0, n), self.sc(gname, gcol0 + k), rstd, ALU.mult, ALU.mult)

    def linear(self, W, nk, rhs, mchunks, tiles, evac):
        P = self.P
        blocks = []
        cur = []
        for mc in mchunks:
            if cur and (mc[0] + mc[1] - cur[0][0] > 512):
                blocks.append(cur)
                cur = []
            cur.append(mc)
        if cur:
            blocks.append(cur)
        mi = 0
        for blk in blocks:
            lo = blk[0][0]
            hi = blk[-1][0] + blk[-1][1]
            so = self.slot()
            wv = self.V(so, BF16, nk, hi - lo)
            P.dma(wv, W[:, lo:hi].rearrange("(k p) o -> p k o", p=128), q="pool")
            for (m0, mn) in blk:
                for (c0, n) in tiles:
                    ps = self.bank()
                    for k in range(nk):
                        P.mm(ps[0:mn, 0:n], wv[:, k, m0 - lo:m0 - lo + mn], rhs(k, c0, n), start=(k == 0), stop=(k == nk - 1))
                    evac(mi, c0, n, ps[0:mn, 0:n])
                mi += 1

    def build(self):
        nc = self.nc
        P = self.P

        def din(name, shape):
            return nc.dram_tensor(name, list(shape), F32, kind="ExternalInput").ap()

        def dout(name, shape):
            return nc.dram_tensor(name, list(shape), F32, kind="ExternalOutput").ap()

        self.xT = din("xT", [D, T])
        self.memT = din("memT", [D, NMEM])
        self.latc = din("latc", [2, 256, PAST])
        self.krc = din("krc", [2, 32, PAST])
        self.cmk = din("cmk", [DEPTH, 2, 256, 256])
        self.cmv = din("cmv", [DEPTH, 2, 256, 256])
        self.sst = din("sst", [NA, 2, 2, 128, 24])
        self.cst = din("cst", [DEPTH, 2, 128, 88])
        self.scal = din("scal", [128, NSC])
        self.consf = din("consf", [128, NCF])
        self.consb = din("consb", [128, NCB])
        self.rope = din("rope", [2, 32, T])
        self.w_mix_in = din("w_mix_in", [DEPTH, D, D])
        self.w_mix_out = din("w_mix_out", [DEPTH, D, D])
        self.w_ffn_in = din("w_ffn_in", [DEPTH, D, 2 * DFF])
        self.w_ffn_out = din("w_ffn_out", [DEPTH, DFF, D])
        self.w_mem_kv = din("w_mem_kv", [DEPTH, D, 512])
        self.w_glu = din("w_glu", [NA, MIX, MIX])
        self.w_dkv = din("w_dkv", [D, 288])
        self.w_uk = din("w_uk", [256, MIX])
        self.w_uv = din("w_uv", [256, MIX])
        self.w_uq = din("w_uq", [2, MIX, 12 * 128])
        self.s5b = din("s5b", [NA, 2, 128, 24 * 128])
        self.s5c = din("s5c", [NA, 2, 128, 24 * 128])
        self.kscr = nc.dram_tensor("scr_k", [2, 12, 64, PAST], BF16, kind="Internal").ap()
        self.vscr = nc.dram_tensor("scr_v", [2, 12, PAST // 1024, 128, 8 * 64], BF16, kind="Internal").ap()
        self.yT = dout("yT", [D, T])
        self.memk_o = dout("memk_o", [DEPTH, 256, NMEM])
        self.memv_o = dout("memv_o", [DEPTH, NMEM, 256])
        self.lat_o = dout("lat_o", [256, T])
        self.kr_o = dout("kr_o", [32, T])
        self.ssm_o = dout("ssm_o", [NA, 3, 2, 128, 24])
        self.conv_o = dout("conv_o", [DEPTH, 3, 128, 88])

        with nc.sbuf_tensor("A", [128, ARENA_BYTES // 4], F32) as A_, \
                nc.psum_tensor("PS", [128, 8, 512], F32) as PS_, \
                nc.allow_low_precision("bf16 matmul operands, fp32 accumulation"):
            self.A = A_[:]
            self.PS = PS_
            self.H = self.V(H_OFF, F32, 8, T)
            self.XA = self.V(XA_OFF, BF16, 8, T)
            self.Z = self.V(Z_OFF, BF16, 8, T)
            cf = self.V(CF_OFF, F32, NCF)
            self.rrot = cf[:, 0:32]
            cb = self.V(CB_OFF, BF16, NCB)
            self.ones = cb[:, 0:128]
            self.onesb = cb[:, 0:64]
            self.negI = cb[:, 128:256]
            self.identb = cb[:, 256:384]
            self.bo64 = cb[:, 384:512]
            self.boq = cb[:, 512:640]
            self.masks = cb[:, 640:2688].rearrange("p (a b) -> p a b", a=4)
            self.latn = self.V(LATN_OFF, BF16, 2, T)
            self.krn = self.V(KRN_OFF, BF16, T)
            self.mkt = self.V(MKT_OFF, BF16, 4, 2, 256)
            self.mv = self.V(MV_OFF, BF16, 4, 2, 256)

            P.dma(self.V(SC_OFF, F32, NSC), self.scal, q="sp")
            P.dma(cf, self.consf, q="sp")
            P.dma(cb, self.consb, q="pool")
            xv = self.xT.rearrange("(k p) t -> p k t", p=128)
            for k in range(8):
                P.dma(self.H[:, k, :], xv[:, k, :], q="sp")

            import os
            self.dbg = int(os.environ.get("KDBG", "99"))
            self.mem_kv()
            for l in range(DEPTH):
                if self.dbg >= 2 and (self.dbg >= 6 or l == 0 or (self.dbg == 5 and l < 2)):
                    self.layer(l)
            yv = self.yT.rearrange("(k p) t -> p k t", p=128)
            for k in range(8):
                P.dma(yv[:, k, :], self.H[:, k, :], q="sp")
            P.emit()

    def mem_kv(self):
        P = self.P
        memf = self.V(XA_OFF, F32, 8, NMEM)
        mn = self.V(Z_OFF, BF16, 8, NMEM)
        P.dma(memf, self.memT.rearrange("(k p) t -> p k t", p=128), q="sp")
        L = PH_OFF + 32768
        stage = self.V(L + 8192, F32, 512)
        for l in range(DEPTH):
            self.rms_fm(lambda k, c0, n: memf[:, k, c0:c0 + n], 8, [(0, NMEM)], "mem_norm", 8 * l,
                        lambda k, c0, n: mn[:, k, c0:c0 + n], 1.0 / D)
            so = self.slot()
            wv = self.V(so, BF16, 8, 512)
            P.dma(wv, self.w_mem_kv[l].rearrange("(k p) o -> p k o", p=128), q="pool")
            for mc in range(2):
                ps = self.bank()
                for k in range(8):
                    P.mm(ps[:, 0:NMEM], wv[:, k, mc * 128:(mc + 1) * 128], mn[:, k, :], start=(k == 0), stop=(k == 7))
                rstd = self.rms_stats([ps[:, 0:NMEM]], self.bo64, NMEM, 1.0 / 64, [L], L + 4096, L + 6144)
                P.stt(stage[:, 0:NMEM], ps[:, 0:NMEM], self.sc("memk_g", l), rstd, ALU.mult, ALU.mult)
                P.dma(self.memk_o[l, mc * 128:(mc + 1) * 128, :], stage[:, 0:NMEM], q="sp")
                P.copy(self.mkt[:, l, mc, :], stage[:, 0:NMEM], eng="act")
            for tt in range(2):
                ps = self.bank()
                for k in range(8):
                    P.mm(ps[:, 0:256], mn[:, k, tt * 128:(tt + 1) * 128], wv[:, k, 256:512], start=(k == 0), stop=(k == 7))
                P.copy(stage[:, 256:512], ps[:, 0:256], eng="act")
                P.dma(self.memv_o[l, tt * 128:(tt + 1) * 128, :], stage[:, 256:512], q="sp")
                P.copy(self.mv[:, l, tt, :], stage[:, 256:512])

    def layer(self, l):
        P = self.P
        H, XA, Z = self.H, self.XA, self.Z
        if l == NA:
            self.ckv()
        self.rms_fm(lambda k, c0, n: H[:, k, c0:c0 + n], 8, TILES, "norm_mix", 8 * l,
                    lambda k, c0, n: XA[:, k, c0:c0 + n], 1.0 / D)
        self.linear(self.w_mix_in[l], 8, lambda k, c0, n: XA[:, k, c0:c0 + n], [(m * 128, 128) for m in range(8)], TILES,
                    lambda mi, c0, n, ps: P.copy(Z[:, mi, c0:c0 + n], ps, eng="act"))
        for sq in SEQS:
            self.mem_attend(l, sq)
        if self.dbg < 3:
            return
        if l < NA:
            self.s5(l)
        else:
            self.mla(l - NA)
        if self.dbg < 4:
            return
        self.linear(self.w_mix_out[l], 8, lambda k, c0, n: XA[:, k, c0:c0 + n], [(m * 128, 128) for m in range(8)], TILES,
                    lambda mi, c0, n, ps: P.tt(H[:, mi, c0:c0 + n], H[:, mi, c0:c0 + n], ps, ALU.add))
        self.ffn(l)

    def mem_attend(self, l, sq):
        P = self.P
        Z, XA = self.Z, self.XA
        B0 = PH_OFF
        if sq["s"] is None:
            Kt = self.mkt[:, l]
            Vm = self.mv[:, l]
        else:
            Kt = self.V(B0, BF16, 2, 256)
            Vm = self.V(B0 + 1024, BF16, 2, 256)
            P.dma(Kt, self.cmk[l, sq["s"]].rearrange("(k p) t -> p k t", p=128), q="pool")
            P.dma(Vm, self.cmv[l, sq["s"]].rearrange("(k p) t -> p k t", p=128), q="pool")
        tiles = [(c0, n) for (c0, n) in TILES if c0 < TP] if sq["s"] is None else [(sq["c0"], sq["L"])]
        for (c0, n) in tiles:
            for mc in range(2):
                zc = Z[:, 6 + mc, c0:c0 + n]
                rstd = self.rms_stats([zc], self.bo64, n, 1.0 / 64, [B0 + 3072], B0 + 5120, B0 + 7168)
                qmb = self.V(B0 + 2048, BF16, 512)[:, 0:n]
                P.stt(qmb, zc, self.sc("memq_g", l), rstd, ALU.mult, ALU.mult)
                pso = self.bank()
                psd = self.bank()
                for hh in range(2):
                    r0, r1 = hh * 64, hh * 64 + 64
                    hd = 2 * mc + hh
                    for kc in range(2):
                        pss = self.bank()
                        P.mm(pss[:, 0:n], Kt[r0:r1, mc, kc * 128:(kc + 1) * 128], qmb[r0:r1, :])
                        pT = self.V(B0 + 9216 + 1024 * kc, BF16, 512)[:, 0:n]
                        P.act(pT, pss[:, 0:n], AF.Exp, scale=MEM_SCALE)
                        P.mm(pso[r0:r1, 0:n], Vm[:, kc, hd * 64:(hd + 1) * 64], pT, start=(kc == 0), stop=(kc == 1))
                        P.mm(psd[r0:r1, 0:n], self.onesb, pT, start=(kc == 0), stop=(kc == 1))
                rd = self.V(B0 + 11264, F32, 512)[:, 0:n]
                lg = self.V(B0 + 5120, F32, 512)[:, 0:n]
                P.act(lg, psd[:, 0:n], AF.Ln)
                P.act(rd, lg, AF.Exp, scale=-1.0)
                P.tt(XA[:, 6 + mc, c0:c0 + n], pso[:, 0:n], rd, ALU.mult)

    def s5(self, l):
        P = self.P
        Z, XA = self.Z, self.XA
        B0 = PH_OFF
        NU = 10
        pb = B0 + 34816

        def pv(i):
            return self.V(pb + 96 * i, F32, 24)
        a_re = self.V(SC_OFF + 4 * (SCOL["ssm_p"] + 72 * l), F32, 24)
        a_im = self.V(SC_OFF + 4 * (SCOL["ssm_p"] + 72 * l + 24), F32, 24)
        ldt = self.V(SC_OFF + 4 * (SCOL["ssm_p"] + 72 * l + 48), F32, 24)
        dt, mag, v, vi, fr, t0, t1p, den, xre, f_re, f_im, lam_re, lam_im = [pv(i) for i in range(13)]
        vI = self.V(pb + 96 * 13, I32, 24)
        Uc = self.V(pb + 1344, F32, NU, 24)
        Us = self.V(pb + 2304, F32, NU, 24)
        cin = self.V(pb + 3264, F32, 4)
        P.act(dt, ldt, AF.Exp)
        P.tt(t0, a_re, dt, ALU.mult)
        P.act(mag, t0, AF.Exp)
        P.tt(t1p, a_im, dt, ALU.mult)
        twopi = 2 * math.pi
        for which in range(2):
            P.ts(v, t1p, 1.0 / twopi, 0.25 if which == 0 else 0.0, ALU.mult, ALU.add)
            P.copy(vI, v)
            P.copy(vi, vI)
            P.tt(fr, v, vi, ALU.subtract)
            P.act(Uc[:, 0, :] if which == 0 else Us[:, 0, :], fr, AF.Sin, scale=6.2831845)
        P.tt(lam_re, Uc[:, 0, :], mag, ALU.mult)
        P.tt(lam_im, Us[:, 0, :], mag, ALU.mult)
        P.tt(t0, a_re, a_re, ALU.mult)
        P.tt(t1p, a_im, a_im, ALU.mult)
        P.tt(den, t0, t1p, ALU.add)
        P.recip(den, den)
        P.ts(xre, lam_re, -1.0, None, ALU.add)
        P.tt(t0, xre, a_re, ALU.mult)
        P.tt(t1p, lam_im, a_im, ALU.mult)
        P.tt(t0, t0, t1p, ALU.add)
        P.tt(f_re, t0, den, ALU.mult)
        P.tt(t0, lam_im, a_re, ALU.mult)
        P.tt(t1p, xre, a_im, ALU.mult)
        P.tt(t0, t0, t1p, ALU.subtract)
        P.tt(f_im, t0, den, ALU.mult)
        for k in range(NU - 1):
            P.tt(t0, Uc[:, k, :], Uc[:, k, :], ALU.mult)
            P.tt(t1p, Us[:, k, :], Us[:, k, :], ALU.mult)
            P.tt(Uc[:, k + 1, :], t0, t1p, ALU.subtract)
            P.tt(t0, Uc[:, k, :], Us[:, k, :], ALU.mult)
            P.ts(Us[:, k + 1, :], t0, 2.0, None, ALU.mult)

        cosT = self.V(B0 + 8192, F32, 512)
        sinT = self.V(B0 + 10240, F32, 512)
        Freb = self.V(B0 + 12288, BF16, 512)
        Fimb = self.V(B0 + 13312, BF16, 512)
        cosb = self.V(B0 + 14336, BF16, 512)
        sinb = self.V(B0 + 15360, BF16, 512)
        nsinb = self.V(B0 + 39424, BF16, 512)
        t1 = self.V(B0 + 16384, F32, 512)
        t2 = self.V(B0 + 18432, F32, 512)
        xr = self.V(B0 + 20480, BF16, 512)
        xi = self.V(B0 + 21504, BF16, 512)
        ab = self.V(B0 + 22528, BF16, 512)
        bb = self.V(B0 + 23552, BF16, 512)
        cb_ = self.V(B0 + 44544, BF16, 512)
        rr_ = self.V(B0 + 24576, F32, 512)
        ri_ = self.V(B0 + 26624, F32, 512)
        rawr = self.V(B0 + 40448, BF16, 512)
        rawi = self.V(B0 + 41472, BF16, 512)
        rrbs = [self.V(B0 + 42496, BF16, 512), self.V(B0 + 4096, BF16, 512)]
        ribs = [self.V(B0 + 43520, BF16, 512), self.V(B0 + 5120, BF16, 512)]
        ytmp = self.V(B0 + 32768, F32, 512)
        cars = [self.V(pb + 3296 + 192 * si, F32, 2, 24) for si in range(3)]
        fins = [self.V(pb + 3872 + 192 * si, F32, 2, 24) for si in range(3)]
        for si, sq in enumerate(SEQS):
            if sq["s"] is not None:
                P.dma(cars[si], self.sst[l, sq["s"]].rearrange("r p i -> p r i"), q="sp")
        blocks = []
        for si, sq in enumerate(SEQS):
            TB = min(512, sq["L"])
            for tb in range(sq["L"] // TB):
                blocks.append((si, tb, TB, sq["c0"] + tb * TB))
        assert len(blocks) <= 6
        hb_i = 0
        for cb in range(6):
            wo = B0
            wB = self.V(wo, BF16, 2, 4, 128)
            wC = self.V(wo + 2048, BF16, 2, 4, 128)
            for r in range(2):
                P.dma(wB[:, r], self.s5b[l, r][:, cb * 512:(cb + 1) * 512].rearrange("p (q m) -> p q m", q=4), q="pool")
                P.dma(wC[:, r], self.s5c[l, r][:, cb * 512:(cb + 1) * 512].rearrange("p (q m) -> p q m", q=4), q="pool")
            psy = [self.bank(2 + bi) for bi in range(len(blocks))]
            for q in range(4):
                i = 4 * cb + q
                P.memset(cosT[:, 0:1], 1.0)
                P.memset(sinT[:, 0:1], 0.0)
                for k in range(9):
                    d = 1 << k
                    uc, us = Uc[:, k, i:i + 1], Us[:, k, i:i + 1]
                    P.ts(t1[:, 0:d], sinT[:, 0:d], us, -1.0, ALU.mult, ALU.mult)
                    P.ts(t2[:, 0:d], cosT[:, 0:d], us, None, ALU.mult)
                    P.stt(cosT[:, d:2 * d], cosT[:, 0:d], uc, t1[:, 0:d], ALU.mult, ALU.add)
                    P.stt(sinT[:, d:2 * d], sinT[:, 0:d], uc, t2[:, 0:d], ALU.mult, ALU.add)
                P.ts(t1, sinT, f_im[:, i:i + 1], None, ALU.mult)
                P.ts(t2, sinT, f_re[:, i:i + 1], -1.0, ALU.mult, ALU.mult)
                P.stt(Freb, cosT, f_re[:, i:i + 1], t1, ALU.mult, ALU.add)
                P.stt(Fimb, cosT, f_im[:, i:i + 1], t2, ALU.mult, ALU.add)
                P.copy(cosb, cosT, eng="act")
                P.copy(sinb, sinT, eng="act")
                P.act(nsinb, sinT, AF.Copy, scale=-1.0)
                def front_a(bi):
                    si, tb, TB, c0 = blocks[bi]
                    u = Z[:, cb, c0:c0 + TB]
                    psr = self.bank(0)
                    psi = self.bank(1)
                    P.mm(psr[:, 0:TB], wB[:, 0, q, :], u)
                    P.mm(psi[:, 0:TB], wB[:, 1, q, :], u)
                    P.copy(rawr[:, 0:TB], psr[:, 0:TB], eng="act")
                    P.copy(rawi[:, 0:TB], psi[:, 0:TB], eng="act")

                def front_b(bi):
                    si, tb, TB, c0 = blocks[bi]
                    a_, b_, c_ = ab[:, 0:TB], bb[:, 0:TB], cb_[:, 0:TB]
                    P.tt(a_, rawr[:, 0:TB], Freb[:, 0:TB], ALU.mult)
                    P.tt(b_, rawi[:, 0:TB], Fimb[:, 0:TB], ALU.mult)
                    P.tt(c_, rawr[:, 0:TB], Fimb[:, 0:TB], ALU.mult)
                    P.tt(xi[:, 0:TB], rawi[:, 0:TB], Freb[:, 0:TB], ALU.mult)
                    P.tt(xr[:, 0:TB], a_, b_, ALU.subtract)
                    P.tt(xi[:, 0:TB], c_, xi[:, 0:TB], ALU.add)

                front_a(0)
                front_b(0)
                for bi, (si, tb, TB, c0) in enumerate(blocks):
                    sq = SEQS[si]
                    car = cars[si]
                    a_, b_ = ab[:, 0:TB], bb[:, 0:TB]
                    rrb, rib = rrbs[hb_i % 2], ribs[hb_i % 2]
                    if tb == 0 and sq["s"] is None:
                        ini_r, ini_i = 0.0, 0.0
                        rd_ = []
                    else:
                        kk = 0 if tb == 0 else int(math.log2(TB))
                        uc, us = Uc[:, kk, i:i + 1], Us[:, kk, i:i + 1]
                        cr, ci = car[:, 0, i:i + 1], car[:, 1, i:i + 1]
                        P.ts(cin[:, 0:1], ci, us, -1.0, ALU.mult, ALU.mult)
                        P.ts(cin[:, 1:2], cr, us, None, ALU.mult)
                        P.stt(cin[:, 0:1], cr, uc, cin[:, 0:1], ALU.mult, ALU.add)
                        P.stt(cin[:, 1:2], ci, uc, cin[:, 1:2], ALU.mult, ALU.add)
                        ini_r, ini_i = cin[:, 0:1], cin[:, 1:2]
                        rd_ = [cin]
                    if bi + 1 < len(blocks):
                        front_a(bi + 1)
                    mbc = mag[:, i:i + 1].to_broadcast([128, TB])
                    for (o_, x_, in_) in ((rr_, xr, ini_r), (ri_, xi, ini_i)):
                        P.add("dve", (lambda o_=o_, x_=x_, in_=in_, mbc=mbc, TB=TB:
                                      (lambda e: e.tensor_tensor_scan(o_[:, 0:TB], mbc, x_[:, 0:TB], in_, ALU.mult, ALU.add)))(),
                              [x_[:, 0:TB], mag[:, i:i + 1]] + rd_, [o_[:, 0:TB]])
                    P.copy(car[:, 0, i:i + 1], rr_[:, TB - 1:TB], eng="act")
                    P.copy(car[:, 1, i:i + 1], ri_[:, TB - 1:TB], eng="act")
                    P.copy(rrb[:, 0:TB], rr_[:, 0:TB], eng="act")
                    P.copy(rib[:, 0:TB], ri_[:, 0:TB], eng="act")
                    if tb == sq["L"] // TB - 1:
                        fin = fins[si]
                        cc, ss = cosT[:, TB - 1:TB], sinT[:, TB - 1:TB]
                        P.ts(cin[:, 2:3], ri_[:, TB - 1:TB], ss, -1.0, ALU.mult, ALU.mult)
                        P.ts(cin[:, 3:4], rr_[:, TB - 1:TB], ss, None, ALU.mult)
                        P.stt(fin[:, 0, i:i + 1], rr_[:, TB - 1:TB], cc, cin[:, 2:3], ALU.mult, ALU.add)
                        P.stt(fin[:, 1, i:i + 1], ri_[:, TB - 1:TB], cc, cin[:, 3:4], ALU.mult, ALU.add)
                    if bi + 1 < len(blocks):
                        front_b(bi + 1)
                    hb = self.V(B0 + 28672 + 2048 * (hb_i % 2), BF16, 2, 512)
                    hb_i += 1
                    c_ = cb_[:, 0:TB]
                    P.tt(a_, rrb[:, 0:TB], cosb[:, 0:TB], ALU.mult)
                    P.tt(b_, rib[:, 0:TB], sinb[:, 0:TB], ALU.mult)
                    P.tt(c_, rrb[:, 0:TB], nsinb[:, 0:TB], ALU.mult)
                    P.tt(hb[:, 1, 0:TB], rib[:, 0:TB], cosb[:, 0:TB], ALU.mult)
                    P.tt(hb[:, 0, 0:TB], a_, b_, ALU.subtract)
                    P.tt(hb[:, 1, 0:TB], c_, hb[:, 1, 0:TB], ALU.subtract)
                    P.mm(psy[bi][:, 0:TB], wC[:, 0, q, :], hb[:, 0, 0:TB], start=(q == 0), stop=False)
                    P.mm(psy[bi][:, 0:TB], wC[:, 1, q, :], hb[:, 1, 0:TB], start=False, stop=(q == 3))
            for bi, (si, tb, TB, c0) in enumerate(blocks):
                yt = ytmp[:, 0:TB]
                P.stt(yt, Z[:, cb, c0:c0 + TB], self.sc("ssm_d", 6 * l + cb), psy[bi][:, 0:TB], ALU.mult, ALU.add)
                P.act(XA[:, cb, c0:c0 + TB], yt, AF.Gelu_apprx_tanh)
        for si in range(3):
            P.dma(self.ssm_o[l, si].rearrange("r p i -> p r i"), fins[si], q="sp")
        self.linear(self.w_glu[l], 6, lambda k, c0, n: XA[:, k, c0:c0 + n], [(m * 128, 128) for m in range(6)], TILES,
                    lambda mi, c0, n, ps: P.act(Z[:, mi, c0:c0 + n], ps, AF.Sigmoid, bias=self.sc("b_glu", 6 * l + mi)))
        for m in range(6):
            P.tt(XA[:, m, :], XA[:, m, :], Z[:, m, :], ALU.mult)

    def ckv(self):
        P = self.P
        H, XA = self.H, self.XA
        B0 = PH_OFF
        self.rms_fm(lambda k, c0, n: H[:, k, c0:c0 + n], 8, TILES, "kv_norm", 0,
                    lambda k, c0, n: XA[:, k, c0:c0 + n], 1.0 / D)
        wv = self.V(B0, BF16, 8, 288)
        P.dma(wv, self.w_dkv.rearrange("(k p) o -> p k o", p=128), q="pool")
        S0 = B0 + 8192
        lst = self.V(B0 + 16384, F32, 2, 512)
        rC = self.V(B0 + 20480, F32, 512)
        rS = self.V(B0 + 22528, F32, 512)
        knf = self.V(B0 + 24576, F32, 512)
        t1 = self.V(B0 + 26624, F32, 512)
        t2 = self.V(B0 + 28672, F32, 512)
        krs = self.V(B0 + 30720, F32, 512)
        for (c0, n) in TILES:
            pss = [self.bank(), self.bank(), self.bank()]
            for mc, (m0, mn) in enumerate([(0, 128), (128, 128), (256, 32)]):
                o = pss[mc][0:128, 0:n] if mc < 2 else pss[mc][64:96, 0:n]
                for k in range(8):
                    P.mm(o, wv[:, k, m0:m0 + mn], XA[:, k, c0:c0 + n], start=(k == 0), stop=(k == 7))
            rstd = self.rms_stats([pss[0][:, 0:n], pss[1][:, 0:n]], self.ones, n, 1.0 / 256, [S0, S0 + 2048], S0 + 4096, S0 + 6144)
            for mc in range(2):
                P.stt(lst[:, mc, 0:n], pss[mc][:, 0:n], self.sc("lat_norm", mc), rstd, ALU.mult, ALU.mult)
                P.dma(self.lat_o[mc * 128:(mc + 1) * 128, c0:c0 + n], lst[:, mc, 0:n], q="sp")
                P.copy(self.latn[:, mc, c0:c0 + n], lst[:, mc, 0:n], eng="act")
            pk = pss[2][64:96, 0:n]
            rstd2 = self.rms_stats([pk], self.boq[64:96, 64:96], n, 1.0 / 32, [S0], S0 + 4096, S0 + 6144, p0=64, p1=96)
            P.stt(knf[64:96, 0:n], pk, self.sc("krope_g", 0, 64, 96), rstd2, ALU.mult, ALU.mult)
            self.rope_apply(knf, krs, c0, n, rC, rS, t1, t2)
            P.dma(self.kr_o[:, c0:c0 + n], krs[64:96, 0:n], q="sp")
            P.copy(self.krn[64:96, c0:c0 + n], krs[64:96, 0:n], eng="act")

    def rope_apply(self, src, dst, c0, n, rC, rS, t1, t2):
        P = self.P
        P.dma(rC[64:96, 0:n], self.rope[0][:, c0:c0 + n], q="sp")
        P.dma(rS[64:96, 0:n], self.rope[1][:, c0:c0 + n], q="sp")
        pr = self.bank()
        P.mm(pr[64:96, 0:n], self.rrot[64:96, :], src[64:96, 0:n])
        P.tt(t1[64:96, 0:n], pr[64:96, 0:n], rS[64:96, 0:n], ALU.mult)
        P.tt(t2[64:96, 0:n], src[64:96, 0:n], rC[64:96, 0:n], ALU.mult)
        P.tt(dst[64:96, 0:n], t1[64:96, 0:n], t2[64:96, 0:n], ALU.add)

    def mla(self, j):
        P = self.P
        Z, XA = self.Z, self.XA
        B0 = PH_OFF
        self.rms_fm(lambda k, c0, n: Z[:, k, c0:c0 + n], 6, TILES, "qlat_g", 6 * j,
                    lambda k, c0, n: Z[:, k, c0:c0 + n], 1.0 / MIX)
        wuk = self.V(B0, BF16, 2, MIX)
        wuv = self.V(B0 + 3072, BF16, 2, MIX)
        P.dma(wuk, self.w_uk.rearrange("(k p) o -> p k o", p=128), q="pool")
        P.dma(wuv, self.w_uv.rearrange("(k p) o -> p k o", p=128), q="pool")
        wqs = [self.V(B0 + 6144 + 1536 * i, BF16, 6, 128) for i in range(2)]
        qnf = self.V(B0 + 28672, F32, 512)
        SQ, RS, RSTD = B0 + 30720, B0 + 31744, B0 + 33792
        rC = self.V(B0 + 38912, F32, 512)
        rS = self.V(B0 + 40960, F32, 512)
        t1 = self.V(B0 + 43008, F32, 512)
        t2 = self.V(RS, F32, 512)
        acc = self.V(B0 + 25600, F32, 12, 32)
        fz = self.V(B0 + 43008, F32, 512)
        PT0 = B0 + 35840
        qscale = self.sc("qscale", 0)
        self.rr = [0, 1, 2, 3, 4, 5]
        st = dict(pt=0, u=0)

        def head_build(h, lat, nkeys, Kcat, Vh):
            stages = []
            nkc = (nkeys + 127) // 128
            for kt in range(0, nkeys, 512):
                kn = min(512, nkeys - kt)
                box = {}

                def s1(kt=kt, kn=kn, box=box):
                    ps = self.bank()
                    box["ps"] = ps
                    box["b"] = self.last_bank
                    self.held.add(self.last_bank)
                    for kc2 in range(2):
                        P.mm(ps[0:64, 0:kn], wuk[:, kc2, h * 64:(h + 1) * 64], lat[:, kc2, kt:kt + kn], start=(kc2 == 0), stop=(kc2 == 1))
                    sq = self.V(SQ, BF16, 512, p0=0, p1=64)[:, 0:kn]
                    P.act(sq, ps[0:64, 0:kn], AF.Square)

                def s2(kt=kt, kn=kn, box=box):
                    sq = self.V(SQ, BF16, 512, p0=0, p1=64)[:, 0:kn]
                    p2 = self.bank()
                    box["p2"] = p2
                    P.mm(p2[0:64, 0:kn], self.bo64[0:64, 0:64], sq)
                    rs = self.V(RS, F32, 512, p0=0, p1=64)[:, 0:kn]
                    P.act(rs, p2[0:64, 0:kn], AF.Ln, bias=EPS, scale=1.0 / 64)
                    rstd = self.V(RSTD, F32, 512, p0=0, p1=64)[:, 0:kn]
                    P.act(rstd, rs, AF.Exp, scale=-0.5)

                def s3(kt=kt, kn=kn, box=box):
                    rstd = self.V(RSTD, F32, 512, p0=0, p1=64)[:, 0:kn]
                    P.stt(Kcat[0:64, kt:kt + kn], box["ps"][0:64, 0:kn], self.sc("knope_g", 0, 0, 64), rstd, ALU.mult, ALU.mult)
                    self.held.discard(box["b"])
                stages += [s1, s2, s3]
            for g0 in range(0, nkc, 8):
                def sv(g0=g0):
                    ps = self.bank()
                    g1 = min(nkc, g0 + 8)
                    for kc in range(g0, g1):
                        kn = min(128, nkeys - kc * 128)
                        for kc2 in range(2):
                            P.mm(ps[0:kn, (kc - g0) * 64:(kc - g0 + 1) * 64], lat[:, kc2, kc * 128:kc * 128 + kn],
                                 wuv[:, kc2, h * 64:(h + 1) * 64], start=(kc2 == 0), stop=(kc2 == 1))
                    knl = min(128, nkeys - (g1 - 1) * 128)
                    vo = (h % 2) * 64
                    if knl == 128:
                        P.copy(Vh[:, g0:g1, vo:vo + 64], ps[:, 0:(g1 - g0) * 64].rearrange("p (a b) -> p a b", b=64))
                    else:
                        if g1 - 1 > g0:
                            P.copy(Vh[:, g0:g1 - 1, vo:vo + 64], ps[:, 0:(g1 - 1 - g0) * 64].rearrange("p (a b) -> p a b", b=64))
                        P.copy(Vh[0:knl, g1 - 1, vo:vo + 64], ps[0:knl, (g1 - 1 - g0) * 64:(g1 - g0) * 64])
                stages.append(sv)
            return stages

        def load_wq(h):
            P.dma(wqs[h % 2], self.w_uq[j][:, h * 128:(h + 1) * 128].rearrange("(k p) o -> p k o", p=128), q="pool")

        def q_build(h, c0, n, Qdst):
            wq = wqs[h % 2]
            box = {}

            def s1():
                psq = self.bank()
                box["psq"] = psq
                box["b"] = self.last_bank
                self.held.add(self.last_bank)
                for k in range(6):
                    P.mm(psq[:, 0:n], wq[:, k, :], Z[:, k, c0:c0 + n], start=(k == 0), stop=(k == 5))
                sq = self.V(SQ, BF16, 512)[:, 0:n]
                P.act(sq, psq[:, 0:n], AF.Square)
                P.dma(rC[64:96, 0:n], self.rope[0][:, c0:c0 + n], q="sp")
                P.dma(rS[64:96, 0:n], self.rope[1][:, c0:c0 + n], q="sp")

            def s2():
                sq = self.V(SQ, BF16, 512)[:, 0:n]
                p2 = self.bank()
                P.mm(p2[:, 0:n], self.boq, sq)
                rs = self.V(RS, F32, 512)[:, 0:n]
                P.act(rs, p2[:, 0:n], AF.Ln, bias=EPS, scale=qscale)
                rstd = self.V(RSTD, F32, 512)[:, 0:n]
                P.act(rstd, rs, AF.Exp, scale=-0.5)

            def s3():
                rstd = self.V(RSTD, F32, 512)[:, 0:n]
                P.stt(qnf[:, 0:n], box["psq"][:, 0:n], self.sc("q_g", j), rstd, ALU.mult, ALU.mult)
                P.copy(Qdst[0:64, :], qnf[0:64, 0:n])
                self.held.discard(box["b"])

            def s4():
                pr = self.bank()
                box["pr"] = pr
                box["b2"] = self.last_bank
                self.held.add(self.last_bank)
                P.mm(pr[64:96, 0:n], self.rrot[64:96, :], qnf[64:96, 0:n])
                P.tt(t2[64:96, 0:n], qnf[64:96, 0:n], rC[64:96, 0:n], ALU.mult)

            def s5_():
                P.tt(t1[64:96, 0:n], box["pr"][64:96, 0:n], rS[64:96, 0:n], ALU.mult)
                P.tt(Qdst[64:96, :], t1[64:96, 0:n], t2[64:96, 0:n], ALU.add)
                self.held.discard(box["b2"])
            return [s1, s2, s3, s4, s5_]

        def run_all(stages):
            for f in stages:
                f()

        def core(Kcat, Vh, Q, n, kcs, nkeys, diag0, pso, psd, hp, fillers=()):
            G = 512 // n if n < 512 else 1
            groups = [kcs[gi:gi + G] for gi in range(0, len(kcs), G)]

            def S(grp):
                pss = self.bank()
                pb_ = self.last_bank
                self.held.add(pb_)
                pT = self.V(PT0 + 1024 * (st["pt"] % 3), BF16, 512)
                st["pt"] += 1
                cl = 0
                for gj, kc in enumerate(grp):
                    kn = min(128, nkeys - kc * 128)
                    diag = diag0 is not None and kc >= diag0
                    if diag:
                        cl = min(128 * (kc - diag0), n - 128)
                    P.mm(pss[0:kn, gj * n + cl:(gj + 1) * n], Kcat[0:96, kc * 128:kc * 128 + kn], Q[0:96, cl:n], start=True, stop=not diag)
                    if diag:
                        P.mm(pss[0:kn, cl:n], self.negI, self.masks[:, kc - diag0, cl:n], start=False, stop=True)
                return (grp, pss, pT, pb_, cl)

            def E(item):
                grp, pss, pT, pb_, cl = item
                self.held.discard(pb_)
                full = [kc for kc in grp if min(128, nkeys - kc * 128) == 128]
                if full:
                    P.act(pT[:, cl:len(full) * n], pss[:, cl:len(full) * n], AF.Exp, scale=MLA_SCALE)
                if len(full) < len(grp):
                    kn = nkeys - grp[-1] * 128
                    gj = len(grp) - 1
                    P.act(pT[0:kn, gj * n:(gj + 1) * n], pss[0:kn, gj * n:(gj + 1) * n], AF.Exp, scale=MLA_SCALE)
                for gj, kc in enumerate(grp):
                    kn = min(128, nkeys - kc * 128)
                    first = (kc == kcs[0])
                    last = (kc == kcs[-1])
                    P.mm(pso[:, cl:n], Vh[0:kn, kc, :], pT[0:kn, gj * n + cl:(gj + 1) * n], start=first, stop=last)

            fillers = list(fillers)
            per = -(-len(fillers) // max(1, len(groups)))
            pend = []
            for grp in groups:
                pend.append(S(grp))
                if len(pend) > 2:
                    E(pend.pop(0))
                for _ in range(per):
                    if fillers:
                        fillers.pop(0)()
            while pend:
                E(pend.pop(0))
            run_all(fillers)

        def acc_banks():
            u = st["u"]
            st["u"] += 1
            return self.bank(6 + (u % 2)), None

        Kc = [self.V(B0 + 9216, BF16, 2048), self.V(B0 + 17408, BF16, 2048)]
        Vhs = [self.V(B0 + 13312, BF16, 16, 128), self.V(B0 + 21504, BF16, 16, 128)]
        Qcs = [self.V(B0 + 25600 + 1024 * i, BF16, 512) for i in range(3)]
        P.memset(Vhs[0][:, :, 64:128], 1.0)
        P.memset(Vhs[1][:, :, 0:64], 1.0)
        ptiles = [(c0, n) for (c0, n) in TILES if c0 < TP]
        latp = self.latn[:, :, 0:TP]
        for b_ in range(2):
            P.copy(Kc[b_][64:96, 0:TP], self.krn[64:96, 0:TP], eng="act")
        units = [(h, c0, n) for h in range(12) for (c0, n) in ptiles]
        load_wq(0)
        run_all(head_build(0, latp, TP, Kc[0], Vhs[0]))
        for u0 in range(2):
            h0, c0_, n0_ = units[u0]
            run_all(q_build(h0, c0_, n0_, Qcs[u0 % 3][:, 0:n0_]))
        for ui, (h, c0, n) in enumerate(units):
            fill = []
            if ui + 1 < len(units) and units[ui + 1][0] != h:
                fill += head_build(units[ui + 1][0], latp, TP, Kc[units[ui + 1][0] % 2], Vhs[units[ui + 1][0] % 2])
            if ui + 2 < len(units):
                h2, c2, n2 = units[ui + 2]
                if h2 != units[ui + 1][0] or (ui == 0 and False):
                    load_wq(h2)
                fill += q_build(h2, c2, n2, Qcs[(ui + 2) % 3][:, 0:n2])
            hp = (h % 2) * 64
            qt = c0 // 512
            pso, psd = acc_banks()
            core(Kc[h % 2], Vhs[h % 2], Qcs[ui % 3][:, 0:n], n, list(range(4 * (qt + 1))), TP, 4 * qt, pso, psd, hp, fill)
            dp = 64 - hp
            lg = fz[dp:dp + 64, 0:n]
            P.recip(lg, pso[dp:dp + 64, 0:n])
            P.tt(XA[hp:hp + 64, h // 2, c0:c0 + n], pso[hp:hp + 64, 0:n], lg, ALU.mult)

        SBK = 1024
        latsb = self.V(B0 + 9216, BF16, 2, SBK)
        KcS = [self.V(B0 + 13312, BF16, SBK), self.V(B0 + 17408, BF16, SBK)]
        VhS = [self.V(B0 + 15360, BF16, 8, 128), self.V(B0 + 19456, BF16, 8, 128)]
        Qs = self.V(B0 + 21504, BF16, 12, LS)
        P.memset(VhS[0][:, :, 64:128], 1.0)
        P.memset(VhS[1][:, :, 0:64], 1.0)
        for sq in SEQS[1:]:
            c0, n = sq["c0"], sq["L"]
            for h in range(12):
                if h % 2 == 0:
                    load_wq(h)
                    if h + 1 < 12:
                        load_wq(h + 1)
                run_all(q_build(h, c0, n, Qs[:, h, :]))
            sbs = [("cache", k0_, SBK) for k0_ in range(0, PAST, SBK)] + [("new", c0, n)]
            for sbi, (kind, k0, nkeys) in enumerate(sbs):
                reuse = (kind == "cache" and j == 1)
                spill = (kind == "cache" and j == 0)
                if kind == "new":
                    lat = self.latn[:, :, k0:k0 + nkeys]
                    for b_ in range(2):
                        P.copy(KcS[b_][64:96, 0:nkeys], self.krn[64:96, k0:k0 + nkeys], eng="act")
                else:
                    lat = latsb[:, :, 0:nkeys]
                    if not reuse:
                        for kc2 in range(2):
                            P.dma(latsb[:, kc2, 0:nkeys], self.latc[sq["s"], kc2 * 128:(kc2 + 1) * 128, k0:k0 + nkeys], q="pool")
                    for b_ in range(2):
                        P.dma(KcS[b_][64:96, 0:nkeys], self.krc[sq["s"], :, k0:k0 + nkeys], q="pool")
                nkc = (nkeys + 127) // 128

                def hb_(h):
                    b_ = h % 2
                    vo = b_ * 64
                    if reuse:
                        P.dma(KcS[b_][0:64, 0:nkeys], self.kscr[sq["s"], h, :, k0:k0 + nkeys], q="sp")
                        P.dma(VhS[b_][:, :, vo:vo + 64], self.vscr[sq["s"], h, sbi].rearrange("p (a b) -> p a b", b=64), q="sp")
                        return
                    run_all(head_build(h, lat, nkeys, KcS[b_], VhS[b_]))
                    if spill:
                        P.dma(self.kscr[sq["s"], h, :, k0:k0 + nkeys], KcS[b_][0:64, 0:nkeys], q="sp")
                        P.dma(self.vscr[sq["s"], h, sbi].rearrange("p (a b) -> p a b", b=64), VhS[b_][:, :, vo:vo + 64], q="sp")

                hb_(0)
                for h in range(12):
                    if h + 1 < 12:
                        hb_(h + 1)
                    hp = (h % 2) * 64
                    pso, psd = acc_banks()
                    core(KcS[h % 2], VhS[h % 2], Qs[:, h, :], n, list(range(nkc)), nkeys, None, pso, psd, hp)
                    ah = acc[:, h, :]
                    if sbi == 0:
                        P.copy(ah, pso[:, 0:n])
                    else:
                        P.tt(ah, ah, pso[:, 0:n], ALU.add)
                    if sbi == len(sbs) - 1:
                        dp = 64 - hp
                        rdh = fz[hp:hp + 64, 0:n]
                        P.recip(rdh, acc[dp:dp + 64, h, :])
                        P.tt(XA[hp:hp + 64, h // 2, c0:c0 + n], acc[hp:hp + 64, h, :], rdh, ALU.mult)
        self.rr = list(range(8))

    def ffn(self, l):
        P = self.P
        H, XA, Z = self.H, self.XA, self.Z
        B0 = PH_OFF
        self.rms_fm(lambda k, c0, n: H[:, k, c0:c0 + n], 8, TILES, "norm_ffn", 8 * l,
                    lambda k, c0, n: XA[:, k, c0:c0 + n], 1.0 / D)
        UW = T + 6
        ub = [self.V(B0 + 32768 + 4352 * i, BF16, 2176)[:, 0:UW] for i in range(2)]
        sg = self.V(B0 + 41472, F32, 512)
        Dm = self.V(B0 + 43520, BF16, 6, 128)
        cbufL = self.V(CONVB_OFF, F32, 3, 88)
        ubase = [sq["c0"] + 2 * si for si, sq in enumerate(SEQS)]
        groups = [list(range(0, 8)), list(range(8, 16)), list(range(16, 22))]
        Wi = self.w_ffn_in[l]
        Wo = self.w_ffn_out[l]
        for grp in groups:
            wblk = {}
            for jj, jc in enumerate(grp):
                if jj % 4 == 0:
                    nb_ = min(4, len(grp) - jj) * 128
                    for part in range(2):
                        so = self.slot()
                        wv = self.V(so, BF16, 8, 512)[:, :, 0:nb_]
                        P.dma(wv, Wi[:, part * DFF + jc * 128: part * DFF + jc * 128 + nb_].rearrange("(k p) o -> p k o", p=128), q="pool")
                        wblk[part] = wv
                for part in range(2):
                    wv = wblk[part]
                    col = part * NJ + jc
                    for si, sq in enumerate(SEQS):
                        if sq["s"] is None:
                            P.memset(ub[part][:, ubase[si]:ubase[si] + 2], 0.0)
                        else:
                            P.dma(ub[part][:, ubase[si]:ubase[si] + 2], self.cst[l, sq["s"]][:, 2 * col:2 * col + 2], q="pool")
                    for k3 in range(3):
                        P.ts(Dm[:, part * 3 + k3, :], self.identb, self.sc("conv_w", l * 132 + k3 * 44 + col), None, ALU.mult)
                    for (c0, n) in TILES:
                        ps = self.bank()
                        for k in range(8):
                            P.mm(ps[:, 0:n], wv[:, k, (jj % 4) * 128:(jj % 4 + 1) * 128], XA[:, k, c0:c0 + n], start=(k == 0), stop=(k == 7))
                        if c0 < TP:
                            P.copy(ub[part][:, 2 + c0:2 + c0 + n], ps[:, 0:n], eng="act")
                            if c0 + n == TP:
                                P.copy(cbufL[:, 0, 2 * col:2 * col + 2], ps[:, n - 2:n])
                        else:
                            for si in (1, 2):
                                o = (si - 1) * LS
                                P.copy(ub[part][:, ubase[si] + 2:ubase[si] + 2 + LS], ps[:, o:o + LS], eng="act")
                                P.copy(cbufL[:, si, 2 * col:2 * col + 2], ps[:, o + LS - 2:o + LS])
                for si, sq in enumerate(SEQS):
                    tl = [(c0, n) for (c0, n) in TILES if c0 < TP] if sq["s"] is None else [(sq["c0"], sq["L"])]
                    for (c0, n) in tl:
                        pa = self.bank()
                        pg = self.bank()
                        u0 = ubase[si] + (c0 - sq["c0"])
                        for part, pp in ((0, pa), (1, pg)):
                            for k3 in range(3):
                                P.mm(pp[:, 0:n], Dm[:, part * 3 + k3, :], ub[part][:, u0 + k3:u0 + k3 + n], start=(k3 == 0), stop=(k3 == 2))
                        P.act(sg[:, 0:n], pg[:, 0:n], AF.Silu, bias=self.sc("conv_b", l * 44 + NJ + jc))
                        P.stt(Z[:, jj, c0:c0 + n], pa[:, 0:n], self.sc("conv_b", l * 44 + jc), sg[:, 0:n], ALU.add, ALU.mult)
            ng = len(grp)
            for half in range(2):
                so = self.slot()
                wv = self.V(so, BF16, 8, 512)[:, 0:ng, :]
                P.dma(wv, Wo[grp[0] * 128:(grp[-1] + 1) * 128, half * 512:(half + 1) * 512].rearrange("(k p) o -> p k o", p=128), q="pool")
                for mo in range(4):
                    m = half * 4 + mo
                    for (c0, n) in TILES:
                        ps = self.bank()
                        for jj in range(ng):
                            P.mm(ps[:, 0:n], wv[:, jj, mo * 128:(mo + 1) * 128], Z[:, jj, c0:c0 + n], start=(jj == 0), stop=(jj == ng - 1))
                        P.tt(H[:, m, c0:c0 + n], H[:, m, c0:c0 + n], ps[:, 0:n], ALU.add)
        P.dma(self.conv_o[l].rearrange("s p c -> p s c"), cbufL, q="sp")


_CACHE = {}


def _get_nc():
    if "nc" not in _CACHE:
        nc = bass.Bass("TRN2", target_bir_lowering=False)
        K(nc).build()
        _CACHE["nc"] = nc
    return _CACHE["nc"]


def _pp(v):
    v = np.asarray(v, np.float32)
    return np.ascontiguousarray(v.reshape(-1, 128).T)


def _consts():
    cf = np.zeros((128, NCF), np.float32)
    R = np.zeros((32, 32), np.float32)
    for m in range(16):
        R[16 + m, m] = -1.0
        R[m, 16 + m] = 1.0
    cf[64:96, 0:32] = R
    cb = np.zeros((128, NCB), np.float32)
    cb[:, 0:128] = 1.0
    cb[:, 128:256] = -30000.0 * np.eye(128, dtype=np.float32)
    cb[:, 256:384] = np.eye(128, dtype=np.float32)
    cb[0:64, 384:448] = 1.0
    cb[64:128, 448:512] = 1.0
    cb[0:64, 512:576] = 1.0
    cb[64:96, 576:608] = 1.0
    p = np.arange(128)[:, None]
    c = np.arange(512)[None, :]
    for i in range(4):
        cb[:, 640 + 512 * i: 640 + 512 * (i + 1)] = (((128 * i + p) // 64) > (c // 64)).astype(np.float32)
    return cf, cb


def _rope_tables():
    inv = (1.0 / (np.float32(10000.0) ** (np.arange(0, 32, 2, dtype=np.float32) / np.float32(32)))).astype(np.float32)
    pos = np.concatenate([np.arange(TP), PAST + np.arange(LS), PAST + np.arange(LS)]).astype(np.float32)
    ang = (pos[:, None] * inv[None, :]).astype(np.float32)
    c = np.cos(ang).astype(np.float32).T
    s = np.sin(ang).astype(np.float32).T
    return np.ascontiguousarray(np.stack([np.concatenate([c, c], 0), np.concatenate([s, s], 0)], 0))


def _pack_shared(inp):
    f = lambda k: np.asarray(inp[k], np.float32)
    sc = np.zeros((128, NSC), np.float32)

    def put(name, col, arr):
        arr = np.asarray(arr, np.float32)
        if arr.ndim == 1:
            arr = arr[:, None]
        sc[:arr.shape[0], SCOL[name] + col: SCOL[name] + col + arr.shape[1]] = arr

    for l in range(DEPTH):
        put("norm_mix", 8 * l, _pp(f("norm_mix_g")[l]))
        put("norm_ffn", 8 * l, _pp(f("norm_ffn_g")[l]))
        put("mem_norm", 8 * l, _pp(f("mem_norm_g")[l]))
        put("memq_g", l, np.tile(f("mem_q_norm_g")[l], 2))
        put("memk_g", l, np.tile(f("mem_k_norm_g")[l], 2))
        cw = f("ffn_conv_w")[l]
        for k3 in range(3):
            put("conv_w", l * 132 + k3 * 44, _pp(cw[k3]))
        put("conv_b", l * 44, _pp(f("ffn_conv_b")[l]))
    put("kv_norm", 0, _pp(f("kv_norm_g")))
    put("lat_norm", 0, _pp(f("latent_norm_g")))
    kg = np.zeros(128, np.float32)
    kg[64:96] = f("krope_norm_g")
    put("krope_g", 0, kg)
    kn = np.zeros(128, np.float32)
    kn[0:64] = f("k_nope_norm_g")
    put("knope_g", 0, kn)
    qs = np.zeros(128, np.float32)
    qs[0:64] = 1.0 / 64
    qs[64:96] = 1.0 / 32
    put("qscale", 0, qs)
    for j in range(2):
        qg = np.zeros(128, np.float32)
        qg[0:64] = f("q_nope_norm_g")[j]
        qg[64:96] = f("q_rope_norm_g")[j]
        put("q_g", j, qg)
        put("qlat_g", 6 * j, _pp(f("q_latent_norm_g")[j]))
        put("ssm_d", 6 * j, _pp(f("ssm_d")[j]))
        put("b_glu", 6 * j, _pp(f("b_glu")[j]))
        put("ssm_p", 72 * j, _pp(f("ssm_a_re")[j].reshape(-1)))
        put("ssm_p", 72 * j + 24, _pp(f("ssm_a_im")[j].reshape(-1)))
        put("ssm_p", 72 * j + 48, _pp(np.repeat(f("ssm_log_dt")[j], 64)))
    s5b = np.zeros((NA, 2, 128, 24, 128), np.float32)
    s5c = np.zeros((NA, 2, 128, 24, 128), np.float32)
    for l in range(NA):
        for r, (bk, ck) in enumerate((("ssm_b_re", "ssm_c_re"), ("ssm_b_im", "ssm_c_im"))):
            b = f(bk)[l]
            c = f(ck)[l]
            for i in range(24):
                q = i % 4
                for gg in range(2):
                    g = 2 * i + gg
                    rows = slice(32 * q + 16 * gg, 32 * q + 16 * gg + 16)
                    cols = slice(64 * gg, 64 * gg + 64)
                    s5b[l, r, rows, i, cols] = b[g].T
                    s5c[l, r, cols, i, rows] = c[g].T
    wuq = np.zeros((2, MIX, 12, 128), np.float32)
    wuq[:, :, :, 0:96] = f("w_uq").reshape(2, MIX, 12, 96)
    cf, cb = _consts()
    return dict(scal=sc, consf=cf, consb=cb, rope=_rope_tables(),
                w_mix_in=f("w_mix_in"), w_mix_out=f("w_mix_out"), w_ffn_in=f("w_ffn_in"), w_ffn_out=f("w_ffn_out"),
                w_mem_kv=f("w_mem_kv"), w_glu=f("w_glu"), w_dkv=f("w_dkv"), w_uk=f("w_uk"), w_uv=f("w_uv"),
                w_uq=np.ascontiguousarray(wuq.reshape(2, MIX, 12 * 128)),
                s5b=np.ascontiguousarray(s5b.reshape(NA, 2, 128, 24 * 128)),
                s5c=np.ascontiguousarray(s5c.reshape(NA, 2, 128, 24 * 128)))


def _pack_core(inp, c):
    f = lambda k: np.asarray(inp[k], np.float32)
    s0, s1 = 2 * c, 2 * c + 1
    xT = np.concatenate([f("x_prompt")[c].T, f("x_sample")[s0].T, f("x_sample")[s1].T], axis=1)
    d = dict(xT=np.ascontiguousarray(xT), memT=np.ascontiguousarray(f("mem_prompt")[c].T))
    d["latc"] = np.ascontiguousarray(np.stack([f("cache_mla_latent")[s].T for s in (s0, s1)]))
    d["krc"] = np.ascontiguousarray(np.stack([f("cache_mla_krope")[s].T for s in (s0, s1)]))
    d["cmk"] = np.ascontiguousarray(np.stack([np.stack([f("cache_mem_k")[l, s].reshape(256, 256).T for s in (s0, s1)]) for l in range(DEPTH)]))
    d["cmv"] = np.ascontiguousarray(np.stack([np.stack([f("cache_mem_v")[l, s].reshape(256, 256) for s in (s0, s1)]) for l in range(DEPTH)]))
    d["sst"] = np.ascontiguousarray(np.stack([np.stack([np.stack([_pp(f(k)[l, s].reshape(-1)) for k in ("state_ssm_re", "state_ssm_im")])
                                                        for s in (s0, s1)]) for l in range(NA)]))
    cst = np.zeros((DEPTH, 2, 128, 44, 2), np.float32)
    for l in range(DEPTH):
        for si, s in enumerate((s0, s1)):
            sc_ = f("state_conv")[l, s]
            cst[l, si] = sc_.reshape(2, 44, 128).transpose(2, 1, 0)
    d["cst"] = np.ascontiguousarray(cst.reshape(DEPTH, 2, 128, 88))
    return d


def kernel(**inputs):
    nc = _get_nc()
    shared = _pack_shared(inputs)
    in_maps = []
    for c in range(8):
        d = dict(shared)
        d.update(_pack_core(inputs, c))
        in_maps.append(d)
    res = run_bass_kernel_spmd(nc, in_maps, core_ids=list(range(8)))
    R = res.results
    B, DB = 8, 16
    y_p = np.stack([R[c]["yT"][:, :TP].T for c in range(B)])
    y_s = np.zeros((DB, LS, D), np.float32)
    lat_p = np.stack([R[c]["lat_o"][:, :TP].T for c in range(B)])
    kr_p = np.stack([R[c]["kr_o"][:, :TP].T for c in range(B)])
    lat_s = np.zeros((DB, LS, 256), np.float32)
    kr_s = np.zeros((DB, LS, 32), np.float32)
    memk = np.zeros((DEPTH, B, NMEM, 4, 64), np.float32)
    memv = np.zeros((DEPTH, B, NMEM, 4, 64), np.float32)
    ssm_p = np.zeros((2, NA, B, 48, 64), np.float32)
    ssm_s = np.zeros((2, NA, DB, 48, 64), np.float32)
    conv_p = np.zeros((DEPTH, B, 2, 2 * DFF), np.float32)
    conv_s = np.zeros((DEPTH, DB, 2, 2 * DFF), np.float32)
    for c in range(B):
        r = R[c]
        for l in range(DEPTH):
            memk[l, c] = r["memk_o"][l].T.reshape(NMEM, 4, 64)
            memv[l, c] = r["memv_o"][l].reshape(NMEM, 4, 64)
            cv = r["conv_o"][l].reshape(3, 128, 44, 2)
            for si in range(3):
                arr = cv[si].transpose(2, 1, 0).reshape(2, 2 * DFF)
                if si == 0:
                    conv_p[l, c] = arr
                else:
                    conv_s[l, 2 * c + si - 1] = arr
        for l in range(NA):
            for si in range(3):
                for ri in range(2):
                    arr = r["ssm_o"][l, si, ri].T.reshape(48, 64)
                    if si == 0:
                        ssm_p[ri, l, c] = arr
                    else:
                        ssm_s[ri, l, 2 * c + si - 1] = arr
        for si in range(2):
            cs = slice(TP + si * LS, TP + (si + 1) * LS)
            y_s[2 * c + si] = r["yT"][:, cs].T
            lat_s[2 * c + si] = r["lat_o"][:, cs].T
            kr_s[2 * c + si] = r["kr_o"][:, cs].T
    return (y_p, y_s, memk, memv, lat_p, kr_p, ssm_p[0], ssm_p[1], conv_p, lat_s, kr_s, ssm_s[0], ssm_s[1], conv_s)
```

```python
import math
import numpy as np
import concourse.bass as bass
import concourse.mybir as mybir
from concourse.bass_utils import run_bass_kernel_spmd

F32 = mybir.dt.float32
BF16 = mybir.dt.bfloat16
I32 = mybir.dt.int32
AF = mybir.ActivationFunctionType
ALU = mybir.AluOpType

D = 1024
DEPTH = 4
NA = 2
TP = 2048
LS = 32
T = TP + 2 * LS
PAST = 4096
DFF = 2816
NJ = DFF // 128
MIX = 768
NMEM = 256
EPS = 1e-6
MLA_SCALE = 96 ** -0.5
MEM_SCALE = 64 ** -0.5
TILES = [(0, 512), (512, 512), (1024, 512), (1536, 512), (2048, 64)]
SEQS = [dict(c0=0, L=TP, past=0, s=None), dict(c0=TP, L=LS, past=PAST, s=0), dict(c0=TP + LS, L=LS, past=PAST, s=1)]

H_OFF = 0
XA_OFF = 67584
Z_OFF = 101376
PH_OFF = 135168
PH_SIZE = 46080
PS_OFF = PH_OFF + PH_SIZE
SC_OFF = PS_OFF
CF_OFF = SC_OFF + 4096
CB_OFF = CF_OFF + 128
LATN_OFF = CB_OFF + 5376
KRN_OFF = LATN_OFF + 2 * T * 2
MKT_OFF = KRN_OFF + T * 2
MV_OFF = MKT_OFF + 4096
CONVB_OFF = MV_OFF + 4096
ARENA_BYTES = CONVB_OFF + 1056
assert ARENA_BYTES <= 212800, ARENA_BYTES
NCF = 32
NCB = 2688

ND_SEMS = 24

SCOL = {}
_n = 0
for _name, _w in [("norm_mix", 32), ("norm_ffn", 32), ("mem_norm", 32), ("kv_norm", 8), ("lat_norm", 2),
                  ("krope_g", 1), ("memq_g", 4), ("memk_g", 4), ("q_g", 2), ("knope_g", 1), ("qlat_g", 12),
                  ("ssm_d", 12), ("b_glu", 12), ("ssm_p", 144), ("qscale", 1), ("conv_w", 528), ("conv_b", 176)]:
    SCOL[_name] = _n
    _n += _w
NSC = 1024
assert _n <= NSC


def _foot(ap):
    name = ap.name
    off = int(ap.offset)
    dims = list(ap.ap)
    es = mybir.dt.size(ap.dtype)
    if str(ap.space) == "DRAM":
        if not name.startswith("scr_"):
            return None
        ext = sum((c - 1) * abs(st_) for st_, c in dims) + 1
        return (name, 0, 1, off * es, (off + ext) * es)
    ps, pc = dims[0]
    if ps == 0:
        ps = 1 << 40
    p0 = off // ps
    lo = off % ps
    ext = sum((c - 1) * abs(s) for s, c in dims[1:]) + 1
    if str(ap.space) == "PSUM":
        return ("PSUM", (p0 // 32) * 32, ((p0 + pc + 31) // 32) * 32, (lo * es // 2048) * 2048,
                (((lo + ext) * es + 2047) // 2048) * 2048)
    return (name, p0, p0 + pc, lo * es, (lo + ext) * es)


class Op:
    __slots__ = ("eng", "fn", "deps", "signal", "sigval", "dma", "dsem", "dval", "idx")

    def __init__(self, eng, fn, dma):
        self.eng = eng
        self.fn = fn
        self.deps = []
        self.signal = False
        self.sigval = 0
        self.dma = dma
        self.dsem = -1
        self.dval = 0
        self.idx = 0


PAGE = 2048
SAME_ENG_GAP = 10 ** 9


class Prog:
    def __init__(self, nc):
        self.nc = nc
        self.ops = []
        self.pages = {}
        self.ndma = 0
        self.nq = {"sp": 0, "pool": 0}
        self.ecnt = {}
        self.eseq = {}

    def _recs(self, f):
        seen = set()
        out = []
        for pg in range(f[3] // PAGE, (f[4] - 1) // PAGE + 1):
            lst = self.pages.get((f[0], pg))
            if not lst:
                continue
            alive = [r for r in lst if r[6]]
            if len(alive) != len(lst):
                lst[:] = alive
            for r in alive:
                if id(r) not in seen and r[1] < f[2] and f[1] < r[2] and r[3] < f[4] and f[3] < r[4]:
                    seen.add(id(r))
                    out.append(r)
        return out

    def _put(self, rec, f):
        for pg in range(f[3] // PAGE, (f[4] - 1) // PAGE + 1):
            self.pages.setdefault((f[0], pg), []).append(rec)

    def add(self, eng, fn, reads=(), writes=(), dma=False):
        op = Op(eng, fn, dma)
        op.idx = len(self.ops)
        if dma:
            half = ND_SEMS // 2
            i = self.nq[eng]
            self.nq[eng] += 1
            op.dsem = (i % half) + (0 if eng == "sp" else half)
            op.dval = (i // half + 1) * 16
            self.ndma += 1
        rf = [x for x in (_foot(a) for a in reads) if x is not None]
        wf = [x for x in (_foot(a) for a in writes) if x is not None]
        deps = {}

        myseq = self.ecnt.get(eng, 0)
        self.ecnt[eng] = myseq + 1
        self.eseq[op.idx] = myseq

        def need(o, raw, psum=False):
            if (not o.dma) and (not dma) and o.eng == eng:
                if eng == "pe" or not raw or psum:
                    return
                if myseq - self.eseq[o.idx] >= SAME_ENG_GAP:
                    return
            deps[o.idx] = o

        prf = [f for f in rf if f[0] == "PSUM"]
        rf = [f for f in rf if f[0] != "PSUM"]
        for f in prf:
            for r in self._recs(f):
                need(r[0], True, True)
        for f in rf:
            for r in self._recs(f):
                if r[5]:
                    need(r[0], True)
        for f in wf:
            for r in self._recs(f):
                need(r[0], False, f[0] == "PSUM")
                if f[1] <= r[1] and r[2] <= f[2] and f[3] <= r[3] and r[4] <= f[4]:
                    r[6] = False
        for f in prf:
            for r in self._recs(f):
                if (not r[0].dma) and r[0].eng == eng and r[1] == f[1] and r[2] == f[2] and r[3] == f[3] and r[4] == f[4]:
                    r[6] = False
            self._put([op, f[1], f[2], f[3], f[4], True, True], f)
        best = {}
        dl = []
        for o in deps.values():
            if o.dma:
                dl.append(o)
            elif o.eng not in best or best[o.eng].idx < o.idx:
                best[o.eng] = o
        op.deps = dl + list(best.values())
        for o in op.deps:
            o.signal = True
        for f in wf:
            self._put([op, f[1], f[2], f[3], f[4], True, True], f)
        for f in rf:
            if not dma:
                for r in self._recs(f):
                    if (not r[5]) and (not r[0].dma) and r[0].eng == eng and r[1] == f[1] and r[2] == f[2] \
                            and r[3] == f[3] and r[4] == f[4]:
                        r[6] = False
            self._put([op, f[1], f[2], f[3], f[4], False, True], f)
        self.ops.append(op)
        return op

    def emit(self):
        nc = self.nc
        engs = ["pe", "act", "dve", "pool", "sp"]
        cnt = {e: 0 for e in engs}
        last = {}
        for op in self.ops:
            if not op.dma:
                last[op.eng] = op
        for op in last.values():
            op.signal = True
        for op in self.ops:
            if op.dma:
                op.signal = True
            elif op.signal:
                cnt[op.eng] += 1
                op.sigval = cnt[op.eng]
        per = {e: [o for o in self.ops if o.eng == e] for e in engs}
        ctx = []
        esem = {}
        for e in engs:
            c = nc.semaphore("s_" + e)
            esem[e] = c.__enter__()
            ctx.append(c)
        dsem = []
        for i in range(ND_SEMS):
            c = nc.semaphore("d_%d" % i)
            dsem.append(c.__enter__())
            ctx.append(c)
        dfinal = [0] * ND_SEMS
        for op in self.ops:
            if op.dma:
                dfinal[op.dsem] = max(dfinal[op.dsem], op.dval)

        def run(e, eng):
            known = {}
            for op in per[e]:
                waits = {}
                for d in op.deps:
                    if d.dma:
                        k = ("d", d.dsem)
                        v = d.dval
                    else:
                        k = ("e", d.eng)
                        v = d.sigval
                    if known.get(k, 0) < v:
                        waits[k] = max(waits.get(k, 0), v)
                if op.dma and op.dval > 16:
                    k = ("d", op.dsem)
                    v = op.dval - 16
                    if known.get(k, 0) < v:
                        waits[k] = max(waits.get(k, 0), v)
                for k, v in waits.items():
                    s = dsem[k[1]] if k[0] == "d" else esem[k[1]]
                    eng.wait_ge(s, v)
                    known[k] = v
                ins = op.fn(eng)
                if op.dma:
                    ins.then_inc(dsem[op.dsem], 16)
                elif op.signal:
                    ins.then_inc(esem[e], 1)
            if e == "sp":
                for i in range(ND_SEMS):
                    if dfinal[i] > known.get(("d", i), 0):
                        eng.wait_ge(dsem[i], dfinal[i])
                for e2 in engs:
                    if cnt[e2] > known.get(("e", e2), 0):
                        eng.wait_ge(esem[e2], cnt[e2])
                for s_ in list(esem.values()) + dsem:
                    eng.sem_clear(s_)

        with nc.Block() as block:
            @block.tensor
            def _(eng):
                run("pe", eng)

            @block.scalar
            def _(eng):
                run("act", eng)

            @block.vector
            def _(eng):
                run("dve", eng)

            @block.gpsimd
            def _(eng):
                run("pool", eng)

            @block.sync
            def _(eng):
                run("sp", eng)
        for c in reversed(ctx):
            c.__exit__(None, None, None)

    def dma(self, out, in_, q="sp"):
        return self.add(q, lambda eng: eng.dma_start(out=out, in_=in_), [in_], [out], dma=True)

    def mm(self, out, lhsT, rhs, start=True, stop=True):
        return self.add("pe", lambda eng: eng.matmul(out, lhsT, rhs, start=start, stop=stop), [lhsT, rhs], [out])

    def act(self, out, in_, func, bias=None, scale=1.0):
        reads = [in_]
        kw = {}
        if bias is not None:
            kw["bias"] = bias
            if not isinstance(bias, (int, float)):
                reads.append(bias)
        if not isinstance(scale, (int, float)):
            reads.append(scale)
        return self.add("act", lambda e: e.activation(out, in_, func, scale=scale, **kw), reads, [out])

    def tt(self, out, in0, in1, op, eng="dve"):
        return self.add(eng, lambda e: e.tensor_tensor(out, in0, in1, op), [in0, in1], [out])

    def ts(self, out, in0, s1, s2, op0, op1=None, eng="dve"):
        reads = [in0] + [s for s in (s1, s2) if s is not None and not isinstance(s, (int, float))]
        kw = {}
        if op1 is not None:
            kw["op1"] = op1
        return self.add(eng, lambda e: e.tensor_scalar(out, in0, s1, s2, op0, **kw), reads, [out])

    def stt(self, out, in0, scalar, in1, op0, op1):
        reads = [in0, in1] + ([] if isinstance(scalar, (int, float)) else [scalar])
        return self.add("dve", lambda e: e.scalar_tensor_tensor(out, in0, scalar, in1, op0, op1), reads, [out])

    def copy(self, out, in_, eng="dve"):
        if eng == "act":
            return self.act(out, in_, AF.Copy)
        return self.add(eng, lambda e: e.tensor_copy(out, in_), [in_], [out])

    def memset(self, out, val, eng="dve"):
        return self.add(eng, lambda e: e.memset(out, val), [], [out])

    def recip(self, out, in_):
        return self.add("dve", lambda e: e.reciprocal(out, in_), [in_], [out])


class K:
    def __init__(self, nc):
        self.nc = nc
        self.P = Prog(nc)
        self.ring_i = 0
        self.bank_i = 0
        self.rr = list(range(8))
        self.held = set()
        self.last_bank = 0

    def V(self, off, dt, *shape, p0=0, p1=128):
        es = mybir.dt.size(dt)
        assert off % es == 0
        n = 1
        for s in shape:
            n *= s
        base = self.A if dt == F32 else self.A.bitcast(dt)
        v = base[p0:p1, off // es: off // es + n]
        if len(shape) == 2:
            v = v.rearrange("p (a b) -> p a b", a=shape[0])
        elif len(shape) == 3:
            v = v.rearrange("p (a b c) -> p a b c", a=shape[0], b=shape[1])
        return v

    def sc(self, name, col=0, p0=0, p1=128):
        c = SCOL[name] + col
        return self.V(SC_OFF + 4 * c, F32, 1, p0=p0, p1=p1)

    def bank(self, b=None):
        if b is None:
            while True:
                b = self.rr[self.bank_i % len(self.rr)]
                self.bank_i += 1
                if b not in self.held:
                    break
            self.last_bank = b
        return self.PS[:, b, :]

    def slot(self):
        i = self.ring_i % 4
        self.ring_i += 1
        return PH_OFF + 8192 * i

    def rms_stats(self, srcs, ones_ap, n, scale, sq_offs, rs_off, rstd_off, p0=0, p1=128, K_=None):
        P = self.P
        ps = self.bank()
        for i, s in enumerate(srcs):
            sq = self.V(sq_offs[i % len(sq_offs)], BF16, 512, p0=p0, p1=p1)[:, 0:n]
            P.act(sq, s, AF.Square)
            P.mm(ps[p0:p1, 0:n], ones_ap, sq, start=(i == 0), stop=(i == len(srcs) - 1))
        rs = self.V(rs_off, F32, 512, p0=p0, p1=p1)[:, 0:n]
        P.act(rs, ps[p0:p1, 0:n], AF.Ln, bias=EPS, scale=scale)
        rstd = self.V(rstd_off, F32, 512, p0=p0, p1=p1)[:, 0:n]
        P.act(rstd, rs, AF.Exp, scale=-0.5)
        return rstd

    def rms_fm(self, src, nk, tiles, gname, gcol0, dst, inv_n):
        L = PH_OFF + 32768
        for (c0, n) in tiles:
            rstd = self.rms_stats([src(k, c0, n) for k in range(nk)], self.ones, n, inv_n,
                                  [L, L + 2048], L + 4096, L + 6144)
            for k in range(nk):
                self.P.stt(dst(k, c0, n), src(k, c0, n), self.sc(gname, gcol0 + k), rstd, ALU.mult, ALU.mult)

    def linear(self, W, nk, rhs, mchunks, tiles, evac):
        P = self.P
        blocks = []
        cur = []
        for mc in mchunks:
            if cur and (mc[0] + mc[1] - cur[0][0] > 512):
                blocks.append(cur)
                cur = []
            cur.append(mc)
        if cur:
            blocks.append(cur)
        mi = 0
        for blk in blocks:
            lo = blk[0][0]
            hi = blk[-1][0] + blk[-1][1]
            so = self.slot()
            wv = self.V(so, BF16, nk, hi - lo)
            P.dma(wv, W[:, lo:hi].rearrange("(k p) o -> p k o", p=128), q="pool")
            for (m0, mn) in blk:
                for (c0, n) in tiles:
                    ps = self.bank()
                    for k in range(nk):
                        P.mm(ps[0:mn, 0:n], wv[:, k, m0 - lo:m0 - lo + mn], rhs(k, c0, n), start=(k == 0), stop=(k == nk - 1))
                    evac(mi, c0, n, ps[0:mn, 0:n])
                mi += 1

    def build(self):
        nc = self.nc
        P = self.P

        def din(name, shape):
            return nc.dram_tensor(name, list(shape), F32, kind="ExternalInput").ap()

        def dout(name, shape):
            return nc.dram_tensor(name, list(shape), F32, kind="ExternalOutput").ap()

        self.xT = din("xT", [D, T])
        self.memT = din("memT", [D, NMEM])
        self.latc = din("latc", [2, 256, PAST])
        self.krc = din("krc", [2, 32, PAST])
        self.cmk = din("cmk", [DEPTH, 2, 256, 256])
        self.cmv = din("cmv", [DEPTH, 2, 256, 256])
        self.sst = din("sst", [NA, 2, 2, 128, 24])
        self.cst = din("cst", [DEPTH, 2, 128, 88])
        self.scal = din("scal", [128, NSC])
        self.consf = din("consf", [128, NCF])
        self.consb = din("consb", [128, NCB])
        self.rope = din("rope", [2, 32, T])
        self.w_mix_in = din("w_mix_in", [DEPTH, D, D])
        self.w_mix_out = din("w_mix_out", [DEPTH, D, D])
        self.w_ffn_in = din("w_ffn_in", [DEPTH, D, 2 * DFF])
        self.w_ffn_out = din("w_ffn_out", [DEPTH, DFF, D])
        self.w_mem_kv = din("w_mem_kv", [DEPTH, D, 512])
        self.w_glu = din("w_glu", [NA, MIX, MIX])
        self.w_dkv = din("w_dkv", [D, 288])
        self.w_uk = din("w_uk", [256, MIX])
        self.w_uv = din("w_uv", [256, MIX])
        self.w_uq = din("w_uq", [2, MIX, 12 * 128])
        self.s5b = din("s5b", [NA, 2, 128, 24 * 128])
        self.s5c = din("s5c", [NA, 2, 128, 24 * 128])
        self.kscr = nc.dram_tensor("scr_k", [2, 12, 64, PAST], BF16, kind="Internal").ap()
        self.vscr = nc.dram_tensor("scr_v", [2, 12, PAST // 1024, 128, 8 * 64], BF16, kind="Internal").ap()
        self.yT = dout("yT", [D, T])
        self.memk_o = dout("memk_o", [DEPTH, 256, NMEM])
        self.memv_o = dout("memv_o", [DEPTH, NMEM, 256])
        self.lat_o = dout("lat_o", [256, T])
        self.kr_o = dout("kr_o", [32, T])
        self.ssm_o = dout("ssm_o", [NA, 3, 2, 128, 24])
        self.conv_o = dout("conv_o", [DEPTH, 3, 128, 88])

        with nc.sbuf_tensor("A", [128, ARENA_BYTES // 4], F32) as A_, \
                nc.psum_tensor("PS", [128, 8, 512], F32) as PS_, \
                nc.allow_low_precision("bf16 matmul operands, fp32 accumulation"):
            self.A = A_[:]
            self.PS = PS_
            self.H = self.V(H_OFF, F32, 8, T)
            self.XA = self.V(XA_OFF, BF16, 8, T)
            self.Z = self.V(Z_OFF, BF16, 8, T)
            cf = self.V(CF_OFF, F32, NCF)
            self.rrot = cf[:, 0:32]
            cb = self.V(CB_OFF, BF16, NCB)
            self.ones = cb[:, 0:128]
            self.onesb = cb[:, 0:64]
            self.negI = cb[:, 128:256]
            self.identb = cb[:, 256:384]
            self.bo64 = cb[:, 384:512]
            self.boq = cb[:, 512:640]
            self.masks = cb[:, 640:2688].rearrange("p (a b) -> p a b", a=4)
            self.latn = self.V(LATN_OFF, BF16, 2, T)
            self.krn = self.V(KRN_OFF, BF16, T)
            self.mkt = self.V(MKT_OFF, BF16, 4, 2, 256)
            self.mv = self.V(MV_OFF, BF16, 4, 2, 256)

            P.dma(self.V(SC_OFF, F32, NSC), self.scal, q="sp")
            P.dma(cf, self.consf, q="sp")
            P.dma(cb, self.consb, q="pool")
            xv = self.xT.rearrange("(k p) t -> p k t", p=128)
            for k in range(8):
                P.dma(self.H[:, k, :], xv[:, k, :], q="sp")

            import os
            self.dbg = int(os.environ.get("KDBG", "99"))
            self.mem_kv()
            for l in range(DEPTH):
                if self.dbg >= 2 and (self.dbg >= 6 or l == 0 or (self.dbg == 5 and l < 2)):
                    self.layer(l)
            yv = self.yT.rearrange("(k p) t -> p k t", p=128)
            for k in range(8):
                P.dma(yv[:, k, :], self.H[:, k, :], q="sp")
            P.emit()

    def mem_kv(self):
        P = self.P
        memf = self.V(XA_OFF, F32, 8, NMEM)
        mn = self.V(Z_OFF, BF16, 8, NMEM)
        P.dma(memf, self.memT.rearrange("(k p) t -> p k t", p=128), q="sp")
        L = PH_OFF + 32768
        stage = self.V(L + 8192, F32, 512)
        for l in range(DEPTH):
            self.rms_fm(lambda k, c0, n: memf[:, k, c0:c0 + n], 8, [(0, NMEM)], "mem_norm", 8 * l,
                        lambda k, c0, n: mn[:, k, c0:c0 + n], 1.0 / D)
            so = self.slot()
            wv = self.V(so, BF16, 8, 512)
            P.dma(wv, self.w_mem_kv[l].rearrange("(k p) o -> p k o", p=128), q="pool")
            for mc in range(2):
                ps = self.bank()
                for k in range(8):
                    P.mm(ps[:, 0:NMEM], wv[:, k, mc * 128:(mc + 1) * 128], mn[:, k, :], start=(k == 0), stop=(k == 7))
                rstd = self.rms_stats([ps[:, 0:NMEM]], self.bo64, NMEM, 1.0 / 64, [L], L + 4096, L + 6144)
                P.stt(stage[:, 0:NMEM], ps[:, 0:NMEM], self.sc("memk_g", l), rstd, ALU.mult, ALU.mult)
                P.dma(self.memk_o[l, mc * 128:(mc + 1) * 128, :], stage[:, 0:NMEM], q="sp")
                P.copy(self.mkt[:, l, mc, :], stage[:, 0:NMEM], eng="act")
            for tt in range(2):
                ps = self.bank()
                for k in range(8):
                    P.mm(ps[:, 0:256], mn[:, k, tt * 128:(tt + 1) * 128], wv[:, k, 256:512], start=(k == 0), stop=(k == 7))
                P.copy(stage[:, 256:512], ps[:, 0:256], eng="act")
                P.dma(self.memv_o[l, tt * 128:(tt + 1) * 128, :], stage[:, 256:512], q="sp")
                P.copy(self.mv[:, l, tt, :], stage[:, 256:512])

    def layer(self, l):
        P = self.P
        H, XA, Z = self.H, self.XA, self.Z
        if l == NA:
            self.ckv()
        self.rms_fm(lambda k, c0, n: H[:, k, c0:c0 + n], 8, TILES, "norm_mix", 8 * l,
                    lambda k, c0, n: XA[:, k, c0:c0 + n], 1.0 / D)
        self.linear(self.w_mix_in[l], 8, lambda k, c0, n: XA[:, k, c0:c0 + n], [(m * 128, 128) for m in range(8)], TILES,
                    lambda mi, c0, n, ps: P.copy(Z[:, mi, c0:c0 + n], ps, eng="act"))
        for sq in SEQS:
            self.mem_attend(l, sq)
        if self.dbg < 3:
            return
        if l < NA:
            self.s5(l)
        else:
            self.mla(l - NA)
        if self.dbg < 4:
            return
        self.linear(self.w_mix_out[l], 8, lambda k, c0, n: XA[:, k, c0:c0 + n], [(m * 128, 128) for m in range(8)], TILES,
                    lambda mi, c0, n, ps: P.tt(H[:, mi, c0:c0 + n], H[:, mi, c0:c0 + n], ps, ALU.add))
        self.ffn(l)

    def mem_attend(self, l, sq):
        P = self.P
        Z, XA = self.Z, self.XA
        B0 = PH_OFF
        if sq["s"] is None:
            Kt = self.mkt[:, l]
            Vm = self.mv[:, l]
        else:
            Kt = self.V(B0, BF16, 2, 256)
            Vm = self.V(B0 + 1024, BF16, 2, 256)
            P.dma(Kt, self.cmk[l, sq["s"]].rearrange("(k p) t -> p k t", p=128), q="pool")
            P.dma(Vm, self.cmv[l, sq["s"]].rearrange("(k p) t -> p k t", p=128), q="pool")
        tiles = [(c0, n) for (c0, n) in TILES if c0 < TP] if sq["s"] is None else [(sq["c0"], sq["L"])]
        for (c0, n) in tiles:
            for mc in range(2):
                zc = Z[:, 6 + mc, c0:c0 + n]
                rstd = self.rms_stats([zc], self.bo64, n, 1.0 / 64, [B0 + 3072], B0 + 5120, B0 + 7168)
                qmb = self.V(B0 + 2048, BF16, 512)[:, 0:n]
                P.stt(qmb, zc, self.sc("memq_g", l), rstd, ALU.mult, ALU.mult)
                pso = self.bank()
                psd = self.bank()
                for hh in range(2):
                    r0, r1 = hh * 64, hh * 64 + 64
                    hd = 2 * mc + hh
                    for kc in range(2):
                        pss = self.bank()
                        P.mm(pss[:, 0:n], Kt[r0:r1, mc, kc * 128:(kc + 1) * 128], qmb[r0:r1, :])
                        pT = self.V(B0 + 9216 + 1024 * kc, BF16, 512)[:, 0:n]
                        P.act(pT, pss[:, 0:n], AF.Exp, scale=MEM_SCALE)
                        P.mm(pso[r0:r1, 0:n], Vm[:, kc, hd * 64:(hd + 1) * 64], pT, start=(kc == 0), stop=(kc == 1))
                        P.mm(psd[r0:r1, 0:n], self.onesb, pT, start=(kc == 0), stop=(kc == 1))
                rd = self.V(B0 + 11264, F32, 512)[:, 0:n]
                lg = self.V(B0 + 5120, F32, 512)[:, 0:n]
                P.act(lg, psd[:, 0:n], AF.Ln)
                P.act(rd, lg, AF.Exp, scale=-1.0)
                P.tt(XA[:, 6 + mc, c0:c0 + n], pso[:, 0:n], rd, ALU.mult)

    def s5(self, l):
        P = self.P
        Z, XA = self.Z, self.XA
        B0 = PH_OFF
        NU = 10
        pb = B0 + 34816

        def pv(i):
            return self.V(pb + 96 * i, F32, 24)
        a_re = self.V(SC_OFF + 4 * (SCOL["ssm_p"] + 72 * l), F32, 24)
        a_im = self.V(SC_OFF + 4 * (SCOL["ssm_p"] + 72 * l + 24), F32, 24)
        ldt = self.V(SC_OFF + 4 * (SCOL["ssm_p"] + 72 * l + 48), F32, 24)
        dt, mag, v, vi, fr, t0, t1p, den, xre, f_re, f_im, lam_re, lam_im = [pv(i) for i in range(13)]
        vI = self.V(pb + 96 * 13, I32, 24)
        Uc = self.V(pb + 1344, F32, NU, 24)
        Us = self.V(pb + 2304, F32, NU, 24)
        cin = self.V(pb + 3264, F32, 4)
        P.act(dt, ldt, AF.Exp)
        P.tt(t0, a_re, dt, ALU.mult)
        P.act(mag, t0, AF.Exp)
        P.tt(t1p, a_im, dt, ALU.mult)
        twopi = 2 * math.pi
        for which in range(2):
            P.ts(v, t1p, 1.0 / twopi, 0.25 if which == 0 else 0.0, ALU.mult, ALU.add)
            P.copy(vI, v)
            P.copy(vi, vI)
            P.tt(fr, v, vi, ALU.subtract)
            P.act(Uc[:, 0, :] if which == 0 else Us[:, 0, :], fr, AF.Sin, scale=6.2831845)
        P.tt(lam_re, Uc[:, 0, :], mag, ALU.mult)
        P.tt(lam_im, Us[:, 0, :], mag, ALU.mult)
        P.tt(t0, a_re, a_re, ALU.mult)
        P.tt(t1p, a_im, a_im, ALU.mult)
        P.tt(den, t0, t1p, ALU.add)
        P.recip(den, den)
        P.ts(xre, lam_re, -1.0, None, ALU.add)
        P.tt(t0, xre, a_re, ALU.mult)
        P.tt(t1p, lam_im, a_im, ALU.mult)
        P.tt(t0, t0, t1p, ALU.add)
        P.tt(f_re, t0, den, ALU.mult)
        P.tt(t0, lam_im, a_re, ALU.mult)
        P.tt(t1p, xre, a_im, ALU.mult)
        P.tt(t0, t0, t1p, ALU.subtract)
        P.tt(f_im, t0, den, ALU.mult)
        for k in range(NU - 1):
            P.tt(t0, Uc[:, k, :], Uc[:, k, :], ALU.mult)
            P.tt(t1p, Us[:, k, :], Us[:, k, :], ALU.mult)
            P.tt(Uc[:, k + 1, :], t0, t1p, ALU.subtract)
            P.tt(t0, Uc[:, k, :], Us[:, k, :], ALU.mult)
            P.ts(Us[:, k + 1, :], t0, 2.0, None, ALU.mult)

        cosT = self.V(B0 + 8192, F32, 512)
        sinT = self.V(B0 + 10240, F32, 512)
        Freb = self.V(B0 + 12288, BF16, 512)
        Fimb = self.V(B0 + 13312, BF16, 512)
        cosb = self.V(B0 + 14336, BF16, 512)
        sinb = self.V(B0 + 15360, BF16, 512)
        nsinb = self.V(B0 + 39424, BF16, 512)
        t1 = self.V(B0 + 16384, F32, 512)
        t2 = self.V(B0 + 18432, F32, 512)
        xr = self.V(B0 + 20480, BF16, 512)
        xi = self.V(B0 + 21504, BF16, 512)
        ab = self.V(B0 + 22528, BF16, 512)
        bb = self.V(B0 + 23552, BF16, 512)
        cb_ = self.V(B0 + 44544, BF16, 512)
        rr_ = self.V(B0 + 24576, F32, 512)
        ri_ = self.V(B0 + 26624, F32, 512)
        rawr = self.V(B0 + 40448, BF16, 512)
        rawi = self.V(B0 + 41472, BF16, 512)
        rrbs = [self.V(B0 + 42496, BF16, 512), self.V(B0 + 4096, BF16, 512)]
        ribs = [self.V(B0 + 43520, BF16, 512), self.V(B0 + 5120, BF16, 512)]
        ytmp = self.V(B0 + 32768, F32, 512)
        cars = [self.V(pb + 3296 + 192 * si, F32, 2, 24) for si in range(3)]
        fins = [self.V(pb + 3872 + 192 * si, F32, 2, 24) for si in range(3)]
        for si, sq in enumerate(SEQS):
            if sq["s"] is not None:
                P.dma(cars[si], self.sst[l, sq["s"]].rearrange("r p i -> p r i"), q="sp")
        blocks = []
        for si, sq in enumerate(SEQS):
            TB = min(512, sq["L"])
            for tb in range(sq["L"] // TB):
                blocks.append((si, tb, TB, sq["c0"] + tb * TB))
        assert len(blocks) <= 6
        hb_i = 0
        for cb in range(6):
            wo = B0
            wB = self.V(wo, BF16, 2, 4, 128)
            wC = self.V(wo + 2048, BF16, 2, 4, 128)
            for r in range(2):
                P.dma(wB[:, r], self.s5b[l, r][:, cb * 512:(cb + 1) * 512].rearrange("p (q m) -> p q m", q=4), q="pool")
                P.dma(wC[:, r], self.s5c[l, r][:, cb * 512:(cb + 1) * 512].rearrange("p (q m) -> p q m", q=4), q="pool")
            psy = [self.bank(2 + bi) for bi in range(len(blocks))]
            for q in range(4):
                i = 4 * cb + q
                P.memset(cosT[:, 0:1], 1.0)
                P.memset(sinT[:, 0:1], 0.0)
                for k in range(9):
                    d = 1 << k
                    uc, us = Uc[:, k, i:i + 1], Us[:, k, i:i + 1]
                    if k % 2 == 0:
                        P.ts(t1[:, 0:d], sinT[:, 0:d], us, -1.0, ALU.mult, ALU.mult)
                        P.ts(t2[:, 0:d], cosT[:, 0:d], us, None, ALU.mult)
                        P.stt(cosT[:, d:2 * d], cosT[:, 0:d], uc, t1[:, 0:d], ALU.mult, ALU.add)
                        P.stt(sinT[:, d:2 * d], sinT[:, 0:d], uc, t2[:, 0:d], ALU.mult, ALU.add)
                    else:
                        P.ts(t2[:, 0:d], cosT[:, 0:d], us, None, ALU.mult)
                        P.ts(t1[:, 0:d], sinT[:, 0:d], us, -1.0, ALU.mult, ALU.mult)
                        P.stt(sinT[:, d:2 * d], sinT[:, 0:d], uc, t2[:, 0:d], ALU.mult, ALU.add)
                        P.stt(cosT[:, d:2 * d], cosT[:, 0:d], uc, t1[:, 0:d], ALU.mult, ALU.add)
                P.ts(t1, sinT, f_im[:, i:i + 1], None, ALU.mult)
                P.ts(t2, sinT, f_re[:, i:i + 1], -1.0, ALU.mult, ALU.mult)
                P.stt(Freb, cosT, f_re[:, i:i + 1], t1, ALU.mult, ALU.add)
                P.stt(Fimb, cosT, f_im[:, i:i + 1], t2, ALU.mult, ALU.add)
                P.copy(cosb, cosT, eng="act")
                P.copy(sinb, sinT, eng="act")
                P.act(nsinb, sinT, AF.Copy, scale=-1.0)
                def front_a(bi):
                    si, tb, TB, c0 = blocks[bi]
                    u = Z[:, cb, c0:c0 + TB]
                    psr = self.bank(0)
                    psi = self.bank(1)
                    P.mm(psr[:, 0:TB], wB[:, 0, q, :], u)
                    P.mm(psi[:, 0:TB], wB[:, 1, q, :], u)
                    P.copy(rawr[:, 0:TB], psr[:, 0:TB], eng="act")
                    P.copy(rawi[:, 0:TB], psi[:, 0:TB], eng="act")

                def front_b(bi):
                    si, tb, TB, c0 = blocks[bi]
                    a_, b_, c_ = ab[:, 0:TB], bb[:, 0:TB], cb_[:, 0:TB]
                    P.tt(a_, rawr[:, 0:TB], Freb[:, 0:TB], ALU.mult)
                    P.tt(b_, rawi[:, 0:TB], Fimb[:, 0:TB], ALU.mult)
                    P.tt(c_, rawr[:, 0:TB], Fimb[:, 0:TB], ALU.mult)
                    P.tt(xi[:, 0:TB], rawi[:, 0:TB], Freb[:, 0:TB], ALU.mult)
                    P.tt(xr[:, 0:TB], a_, b_, ALU.subtract)
                    P.tt(xi[:, 0:TB], c_, xi[:, 0:TB], ALU.add)

                front_a(0)
                front_b(0)
                for bi, (si, tb, TB, c0) in enumerate(blocks):
                    sq = SEQS[si]
                    car = cars[si]
                    a_, b_ = ab[:, 0:TB], bb[:, 0:TB]
                    rrb, rib = rrbs[hb_i % 2], ribs[hb_i % 2]
                    if tb == 0 and sq["s"] is None:
                        ini_r, ini_i = 0.0, 0.0
                        rd_ = []
                    else:
                        kk = 0 if tb == 0 else int(math.log2(TB))
                        uc, us = Uc[:, kk, i:i + 1], Us[:, kk, i:i + 1]
                        cr, ci = car[:, 0, i:i + 1], car[:, 1, i:i + 1]
                        P.ts(cin[:, 0:1], ci, us, -1.0, ALU.mult, ALU.mult)
                        P.ts(cin[:, 1:2], cr, us, None, ALU.mult)
                        P.stt(cin[:, 0:1], cr, uc, cin[:, 0:1], ALU.mult, ALU.add)
                        P.stt(cin[:, 1:2], ci, uc, cin[:, 1:2], ALU.mult, ALU.add)
                        ini_r, ini_i = cin[:, 0:1], cin[:, 1:2]
                        rd_ = [cin]
                    if bi + 1 < len(blocks):
                        front_a(bi + 1)
                    mbc = mag[:, i:i + 1].to_broadcast([128, TB])
                    for (o_, x_, in_) in ((rr_, xr, ini_r), (ri_, xi, ini_i)):
                        P.add("dve", (lambda o_=o_, x_=x_, in_=in_, mbc=mbc, TB=TB:
                                      (lambda e: e.tensor_tensor_scan(o_[:, 0:TB], mbc, x_[:, 0:TB], in_, ALU.mult, ALU.add)))(),
                              [x_[:, 0:TB], mag[:, i:i + 1]] + rd_, [o_[:, 0:TB]])
                    P.copy(car[:, 0, i:i + 1], rr_[:, TB - 1:TB], eng="act")
                    P.copy(car[:, 1, i:i + 1], ri_[:, TB - 1:TB], eng="act")
                    P.copy(rrb[:, 0:TB], rr_[:, 0:TB], eng="act")
                    P.copy(rib[:, 0:TB], ri_[:, 0:TB], eng="act")
                    if tb == sq["L"] // TB - 1:
                        fin = fins[si]
                        cc, ss = cosT[:, TB - 1:TB], sinT[:, TB - 1:TB]
                        P.ts(cin[:, 2:3], ri_[:, TB - 1:TB], ss, -1.0, ALU.mult, ALU.mult)
                        P.ts(cin[:, 3:4], rr_[:, TB - 1:TB], ss, None, ALU.mult)
                        P.stt(fin[:, 0, i:i + 1], rr_[:, TB - 1:TB], cc, cin[:, 2:3], ALU.mult, ALU.add)
                        P.stt(fin[:, 1, i:i + 1], ri_[:, TB - 1:TB], cc, cin[:, 3:4], ALU.mult, ALU.add)
                    if bi + 1 < len(blocks):
                        front_b(bi + 1)
                    hb = self.V(B0 + 28672 + 2048 * (hb_i % 2), BF16, 2, 512)
                    hb_i += 1
                    c_ = cb_[:, 0:TB]
                    P.tt(a_, rrb[:, 0:TB], cosb[:, 0:TB], ALU.mult)
                    P.tt(b_, rib[:, 0:TB], sinb[:, 0:TB], ALU.mult)
                    P.tt(c_, rrb[:, 0:TB], nsinb[:, 0:TB], ALU.mult)
                    P.tt(hb[:, 1, 0:TB], rib[:, 0:TB], cosb[:, 0:TB], ALU.mult)
                    P.tt(hb[:, 0, 0:TB], a_, b_, ALU.subtract)
                    P.tt(hb[:, 1, 0:TB], c_, hb[:, 1, 0:TB], ALU.subtract)
                    P.mm(psy[bi][:, 0:TB], wC[:, 0, q, :], hb[:, 0, 0:TB], start=(q == 0), stop=False)
                    P.mm(psy[bi][:, 0:TB], wC[:, 1, q, :], hb[:, 1, 0:TB], start=False, stop=(q == 3))
            for bi, (si, tb, TB, c0) in enumerate(blocks):
                yt = ytmp[:, 0:TB]
                P.stt(yt, Z[:, cb, c0:c0 + TB], self.sc("ssm_d", 6 * l + cb), psy[bi][:, 0:TB], ALU.mult, ALU.add)
                P.act(XA[:, cb, c0:c0 + TB], yt, AF.Gelu_apprx_tanh)
        for si in range(3):
            P.dma(self.ssm_o[l, si].rearrange("r p i -> p r i"), fins[si], q="sp")
        self.linear(self.w_glu[l], 6, lambda k, c0, n: XA[:, k, c0:c0 + n], [(m * 128, 128) for m in range(6)], TILES,
                    lambda mi, c0, n, ps: P.act(Z[:, mi, c0:c0 + n], ps, AF.Sigmoid, bias=self.sc("b_glu", 6 * l + mi)))
        for m in range(6):
            P.tt(XA[:, m, :], XA[:, m, :], Z[:, m, :], ALU.mult)

    def ckv(self):
        P = self.P
        H, XA = self.H, self.XA
        B0 = PH_OFF
        self.rms_fm(lambda k, c0, n: H[:, k, c0:c0 + n], 8, TILES, "kv_norm", 0,
                    lambda k, c0, n: XA[:, k, c0:c0 + n], 1.0 / D)
        wv = self.V(B0, BF16, 8, 288)
        P.dma(wv, self.w_dkv.rearrange("(k p) o -> p k o", p=128), q="pool")
        S0 = B0 + 8192
        lst = self.V(B0 + 16384, F32, 2, 512)
        rC = self.V(B0 + 20480, F32, 512)
        rS = self.V(B0 + 22528, F32, 512)
        knf = self.V(B0 + 24576, F32, 512)
        t1 = self.V(B0 + 26624, F32, 512)
        t2 = self.V(B0 + 28672, F32, 512)
        krs = self.V(B0 + 30720, F32, 512)
        for (c0, n) in TILES:
            pss = [self.bank(), self.bank(), self.bank()]
            for mc, (m0, mn) in enumerate([(0, 128), (128, 128), (256, 32)]):
                o = pss[mc][0:128, 0:n] if mc < 2 else pss[mc][64:96, 0:n]
                for k in range(8):
                    P.mm(o, wv[:, k, m0:m0 + mn], XA[:, k, c0:c0 + n], start=(k == 0), stop=(k == 7))
            rstd = self.rms_stats([pss[0][:, 0:n], pss[1][:, 0:n]], self.ones, n, 1.0 / 256, [S0, S0 + 2048], S0 + 4096, S0 + 6144)
            for mc in range(2):
                P.stt(lst[:, mc, 0:n], pss[mc][:, 0:n], self.sc("lat_norm", mc), rstd, ALU.mult, ALU.mult)
                P.dma(self.lat_o[mc * 128:(mc + 1) * 128, c0:c0 + n], lst[:, mc, 0:n], q="sp")
                P.copy(self.latn[:, mc, c0:c0 + n], lst[:, mc, 0:n], eng="act")
            pk = pss[2][64:96, 0:n]
            rstd2 = self.rms_stats([pk], self.boq[64:96, 64:96], n, 1.0 / 32, [S0], S0 + 4096, S0 + 6144, p0=64, p1=96)
            P.stt(knf[64:96, 0:n], pk, self.sc("krope_g", 0, 64, 96), rstd2, ALU.mult, ALU.mult)
            self.rope_apply(knf, krs, c0, n, rC, rS, t1, t2)
            P.dma(self.kr_o[:, c0:c0 + n], krs[64:96, 0:n], q="sp")
            P.copy(self.krn[64:96, c0:c0 + n], krs[64:96, 0:n], eng="act")

    def rope_apply(self, src, dst, c0, n, rC, rS, t1, t2):
        P = self.P
        P.dma(rC[64:96, 0:n], self.rope[0][:, c0:c0 + n], q="sp")
        P.dma(rS[64:96, 0:n], self.rope[1][:, c0:c0 + n], q="sp")
        pr = self.bank()
        P.mm(pr[64:96, 0:n], self.rrot[64:96, :], src[64:96, 0:n])
        P.tt(t1[64:96, 0:n], pr[64:96, 0:n], rS[64:96, 0:n], ALU.mult)
        P.tt(t2[64:96, 0:n], src[64:96, 0:n], rC[64:96, 0:n], ALU.mult)
        P.tt(dst[64:96, 0:n], t1[64:96, 0:n], t2[64:96, 0:n], ALU.add)

    def mla(self, j):
        P = self.P
        Z, XA = self.Z, self.XA
        B0 = PH_OFF
        self.rms_fm(lambda k, c0, n: Z[:, k, c0:c0 + n], 6, TILES, "qlat_g", 6 * j,
                    lambda k, c0, n: Z[:, k, c0:c0 + n], 1.0 / MIX)
        wuk = self.V(B0, BF16, 2, MIX)
        wuv = self.V(B0 + 3072, BF16, 2, MIX)
        P.dma(wuk, self.w_uk.rearrange("(k p) o -> p k o", p=128), q="pool")
        P.dma(wuv, self.w_uv.rearrange("(k p) o -> p k o", p=128), q="pool")
        wqs = [self.V(B0 + 6144 + 1536 * i, BF16, 6, 128) for i in range(2)]
        qnf = self.V(B0 + 28672, F32, 512)
        SQ, RS, RSTD = B0 + 30720, B0 + 31744, B0 + 33792
        rC = self.V(B0 + 38912, F32, 512)
        rS = self.V(B0 + 40960, F32, 512)
        t1 = self.V(B0 + 43008, F32, 512)
        t2 = self.V(RS, F32, 512)
        acc = self.V(B0 + 25600, F32, 12, 32)
        fz = self.V(B0 + 43008, F32, 512)
        PT0 = B0 + 35840
        qscale = self.sc("qscale", 0)
        self.rr = [0, 1, 2, 3, 4, 5]
        st = dict(pt=0, u=0)

        def head_build(h, lat, nkeys, Kcat, Vh):
            stages = []
            nkc = (nkeys + 127) // 128
            for kt in range(0, nkeys, 512):
                kn = min(512, nkeys - kt)
                box = {}

                def s1(kt=kt, kn=kn, box=box):
                    ps = self.bank()
                    box["ps"] = ps
                    box["b"] = self.last_bank
                    self.held.add(self.last_bank)
                    for kc2 in range(2):
                        P.mm(ps[0:64, 0:kn], wuk[:, kc2, h * 64:(h + 1) * 64], lat[:, kc2, kt:kt + kn], start=(kc2 == 0), stop=(kc2 == 1))
                    sq = self.V(SQ, BF16, 512, p0=0, p1=64)[:, 0:kn]
                    P.act(sq, ps[0:64, 0:kn], AF.Square)

                def s2(kt=kt, kn=kn, box=box):
                    sq = self.V(SQ, BF16, 512, p0=0, p1=64)[:, 0:kn]
                    p2 = self.bank()
                    box["p2"] = p2
                    P.mm(p2[0:64, 0:kn], self.bo64[0:64, 0:64], sq)
                    rs = self.V(RS, F32, 512, p0=0, p1=64)[:, 0:kn]
                    P.act(rs, p2[0:64, 0:kn], AF.Ln, bias=EPS, scale=1.0 / 64)

                def s3(kt=kt, kn=kn, box=box):
                    rs = self.V(RS, F32, 512, p0=0, p1=64)[:, 0:kn]
                    rstd = self.V(RSTD, F32, 512, p0=0, p1=64)[:, 0:kn]
                    P.act(rstd, rs, AF.Exp, scale=-0.5)
                    P.stt(Kcat[0:64, kt:kt + kn], box["ps"][0:64, 0:kn], self.sc("knope_g", 0, 0, 64), rstd, ALU.mult, ALU.mult)
                    self.held.discard(box["b"])
                stages += [s1, s2, s3]
            for g0 in range(0, nkc, 8):
                def sv(g0=g0):
                    ps = self.bank()
                    g1 = min(nkc, g0 + 8)
                    for kc in range(g0, g1):
                        kn = min(128, nkeys - kc * 128)
                        for kc2 in range(2):
                            P.mm(ps[0:kn, (kc - g0) * 64:(kc - g0 + 1) * 64], lat[:, kc2, kc * 128:kc * 128 + kn],
                                 wuv[:, kc2, h * 64:(h + 1) * 64], start=(kc2 == 0), stop=(kc2 == 1))
                    knl = min(128, nkeys - (g1 - 1) * 128)
                    vo = (h % 2) * 64
                    if knl == 128:
                        P.copy(Vh[:, g0:g1, vo:vo + 64], ps[:, 0:(g1 - g0) * 64].rearrange("p (a b) -> p a b", b=64))
                    else:
                        if g1 - 1 > g0:
                            P.copy(Vh[:, g0:g1 - 1, vo:vo + 64], ps[:, 0:(g1 - 1 - g0) * 64].rearrange("p (a b) -> p a b", b=64))
                        P.copy(Vh[0:knl, g1 - 1, vo:vo + 64], ps[0:knl, (g1 - 1 - g0) * 64:(g1 - g0) * 64])
                stages.append(sv)
            return stages

        def load_wq(h):
            P.dma(wqs[h % 2], self.w_uq[j][:, h * 128:(h + 1) * 128].rearrange("(k p) o -> p k o", p=128), q="pool")

        def q_build(h, c0, n, Qdst):
            wq = wqs[h % 2]
            box = {}

            def s1():
                psq = self.bank()
                box["psq"] = psq
                box["b"] = self.last_bank
                self.held.add(self.last_bank)
                for k in range(6):
                    P.mm(psq[:, 0:n], wq[:, k, :], Z[:, k, c0:c0 + n], start=(k == 0), stop=(k == 5))
                sq = self.V(SQ, BF16, 512)[:, 0:n]
                P.act(sq, psq[:, 0:n], AF.Square)
                P.dma(rC[64:96, 0:n], self.rope[0][:, c0:c0 + n], q="sp")
                P.dma(rS[64:96, 0:n], self.rope[1][:, c0:c0 + n], q="sp")

            def s2():
                sq = self.V(SQ, BF16, 512)[:, 0:n]
                p2 = self.bank()
                P.mm(p2[:, 0:n], self.boq, sq)
                rs = self.V(RS, F32, 512)[:, 0:n]
                P.act(rs, p2[:, 0:n], AF.Ln, bias=EPS, scale=qscale)

            def s3():
                rs = self.V(RS, F32, 512)[:, 0:n]
                rstd = self.V(RSTD, F32, 512)[:, 0:n]
                P.act(rstd, rs, AF.Exp, scale=-0.5)
                P.stt(qnf[:, 0:n], box["psq"][:, 0:n], self.sc("q_g", j), rstd, ALU.mult, ALU.mult)
                P.copy(Qdst[0:64, :], qnf[0:64, 0:n])
                self.held.discard(box["b"])

            def s4():
                pr = self.bank()
                box["pr"] = pr
                box["b2"] = self.last_bank
                self.held.add(self.last_bank)
                P.mm(pr[64:96, 0:n], self.rrot[64:96, :], qnf[64:96, 0:n])
                P.tt(t2[64:96, 0:n], qnf[64:96, 0:n], rC[64:96, 0:n], ALU.mult)

            def s5_():
                P.tt(t1[64:96, 0:n], box["pr"][64:96, 0:n], rS[64:96, 0:n], ALU.mult)
                P.tt(Qdst[64:96, :], t1[64:96, 0:n], t2[64:96, 0:n], ALU.add)
                self.held.discard(box["b2"])
            return [s1, s2, s3, s4, s5_]

        def run_all(stages):
            for f in stages:
                f()

        def core(Kcat, Vh, Q, n, kcs, nkeys, diag0, pso, psd, hp, fillers=()):
            G = 512 // n if n < 512 else 1
            groups = [kcs[gi:gi + G] for gi in range(0, len(kcs), G)]

            def S(grp):
                pss = self.bank()
                pb_ = self.last_bank
                self.held.add(pb_)
                pT = self.V(PT0 + 1024 * (st["pt"] % 3), BF16, 512)
                st["pt"] += 1
                cl = 0
                for gj, kc in enumerate(grp):
                    kn = min(128, nkeys - kc * 128)
                    diag = diag0 is not None and kc >= diag0
                    if diag:
                        cl = min(128 * (kc - diag0), n - 128)
                    P.mm(pss[0:kn, gj * n + cl:(gj + 1) * n], Kcat[0:96, kc * 128:kc * 128 + kn], Q[0:96, cl:n], start=True, stop=not diag)
                    if diag:
                        P.mm(pss[0:kn, cl:n], self.negI, self.masks[:, kc - diag0, cl:n], start=False, stop=True)
                return (grp, pss, pT, pb_, cl)

            def E(item):
                grp, pss, pT, pb_, cl = item
                self.held.discard(pb_)
                full = [kc for kc in grp if min(128, nkeys - kc * 128) == 128]
                if full:
                    P.act(pT[:, cl:len(full) * n], pss[:, cl:len(full) * n], AF.Exp, scale=MLA_SCALE)
                if len(full) < len(grp):
                    kn = nkeys - grp[-1] * 128
                    gj = len(grp) - 1
                    P.act(pT[0:kn, gj * n:(gj + 1) * n], pss[0:kn, gj * n:(gj + 1) * n], AF.Exp, scale=MLA_SCALE)
                for gj, kc in enumerate(grp):
                    kn = min(128, nkeys - kc * 128)
                    first = (kc == kcs[0])
                    last = (kc == kcs[-1])
                    P.mm(pso[:, cl:n], Vh[0:kn, kc, :], pT[0:kn, gj * n + cl:(gj + 1) * n], start=first, stop=last)

            fillers = list(fillers)
            per = -(-len(fillers) // max(1, len(groups)))
            pend = []
            for grp in groups:
                pend.append(S(grp))
                if len(pend) > 2:
                    E(pend.pop(0))
                for _ in range(per):
                    if fillers:
                        fillers.pop(0)()
            while pend:
                E(pend.pop(0))
            run_all(fillers)

        def acc_banks():
            u = st["u"]
            st["u"] += 1
            return self.bank(6 + (u % 2)), None

        Kc = [self.V(B0 + 9216, BF16, 2048), self.V(B0 + 17408, BF16, 2048)]
        Vhs = [self.V(B0 + 13312, BF16, 16, 128), self.V(B0 + 21504, BF16, 16, 128)]
        Qcs = [self.V(B0 + 25600 + 1024 * i, BF16, 512) for i in range(3)]
        P.memset(Vhs[0][:, :, 64:128], 1.0)
        P.memset(Vhs[1][:, :, 0:64], 1.0)
        ptiles = [(c0, n) for (c0, n) in TILES if c0 < TP]
        latp = self.latn[:, :, 0:TP]
        for b_ in range(2):
            P.copy(Kc[b_][64:96, 0:TP], self.krn[64:96, 0:TP], eng="act")
        units = [(h, c0, n) for h in range(12) for (c0, n) in ptiles]
        load_wq(0)
        run_all(head_build(0, latp, TP, Kc[0], Vhs[0]))
        for u0 in range(2):
            h0, c0_, n0_ = units[u0]
            run_all(q_build(h0, c0_, n0_, Qcs[u0 % 3][:, 0:n0_]))
        for ui, (h, c0, n) in enumerate(units):
            fill = []
            if ui + 1 < len(units) and units[ui + 1][0] != h:
                fill += head_build(units[ui + 1][0], latp, TP, Kc[units[ui + 1][0] % 2], Vhs[units[ui + 1][0] % 2])
            if ui + 2 < len(units):
                h2, c2, n2 = units[ui + 2]
                if h2 != units[ui + 1][0] or (ui == 0 and False):
                    load_wq(h2)
                fill += q_build(h2, c2, n2, Qcs[(ui + 2) % 3][:, 0:n2])
            hp = (h % 2) * 64
            qt = c0 // 512
            pso, psd = acc_banks()
            core(Kc[h % 2], Vhs[h % 2], Qcs[ui % 3][:, 0:n], n, list(range(4 * (qt + 1))), TP, 4 * qt, pso, psd, hp, fill)
            dp = 64 - hp
            lg = fz[dp:dp + 64, 0:n]
            P.recip(lg, pso[dp:dp + 64, 0:n])
            P.tt(XA[hp:hp + 64, h // 2, c0:c0 + n], pso[hp:hp + 64, 0:n], lg, ALU.mult)

        SBK = 1024
        latsb = self.V(B0 + 9216, BF16, 2, SBK)
        KcS = [self.V(B0 + 13312, BF16, SBK), self.V(B0 + 17408, BF16, SBK)]
        VhS = [self.V(B0 + 15360, BF16, 8, 128), self.V(B0 + 19456, BF16, 8, 128)]
        Qs = self.V(B0 + 21504, BF16, 12, LS)
        P.memset(VhS[0][:, :, 64:128], 1.0)
        P.memset(VhS[1][:, :, 0:64], 1.0)
        for sq in SEQS[1:]:
            c0, n = sq["c0"], sq["L"]
            for h in range(12):
                if h % 2 == 0:
                    load_wq(h)
                    if h + 1 < 12:
                        load_wq(h + 1)
                run_all(q_build(h, c0, n, Qs[:, h, :]))
            sbs = [("cache", k0_, SBK) for k0_ in range(0, PAST, SBK)] + [("new", c0, n)]
            for sbi, (kind, k0, nkeys) in enumerate(sbs):
                reuse = (kind == "cache" and j == 1)
                spill = (kind == "cache" and j == 0)
                if kind == "new":
                    lat = self.latn[:, :, k0:k0 + nkeys]
                    for b_ in range(2):
                        P.copy(KcS[b_][64:96, 0:nkeys], self.krn[64:96, k0:k0 + nkeys], eng="act")
                else:
                    lat = latsb[:, :, 0:nkeys]
                    if not reuse:
                        for kc2 in range(2):
                            P.dma(latsb[:, kc2, 0:nkeys], self.latc[sq["s"], kc2 * 128:(kc2 + 1) * 128, k0:k0 + nkeys], q="pool")
                    for b_ in range(2):
                        P.dma(KcS[b_][64:96, 0:nkeys], self.krc[sq["s"], :, k0:k0 + nkeys], q="pool")
                nkc = (nkeys + 127) // 128

                def hb_(h):
                    b_ = h % 2
                    vo = b_ * 64
                    if reuse:
                        P.dma(KcS[b_][0:64, 0:nkeys], self.kscr[sq["s"], h, :, k0:k0 + nkeys], q="sp")
                        P.dma(VhS[b_][:, :, vo:vo + 64], self.vscr[sq["s"], h, sbi].rearrange("p (a b) -> p a b", b=64), q="sp")
                        return
                    run_all(head_build(h, lat, nkeys, KcS[b_], VhS[b_]))
                    if spill:
                        P.dma(self.kscr[sq["s"], h, :, k0:k0 + nkeys], KcS[b_][0:64, 0:nkeys], q="sp")
                        P.dma(self.vscr[sq["s"], h, sbi].rearrange("p (a b) -> p a b", b=64), VhS[b_][:, :, vo:vo + 64], q="sp")

                hb_(0)
                for h in range(12):
                    if h + 1 < 12:
                        hb_(h + 1)
                    hp = (h % 2) * 64
                    pso, psd = acc_banks()
                    core(KcS[h % 2], VhS[h % 2], Qs[:, h, :], n, list(range(nkc)), nkeys, None, pso, psd, hp)
                    ah = acc[:, h, :]
                    if sbi == 0:
                        P.copy(ah, pso[:, 0:n])
                    else:
                        P.tt(ah, ah, pso[:, 0:n], ALU.add)
                    if sbi == len(sbs) - 1:
                        dp = 64 - hp
                        rdh = fz[hp:hp + 64, 0:n]
                        P.recip(rdh, acc[dp:dp + 64, h, :])
                        P.tt(XA[hp:hp + 64, h // 2, c0:c0 + n], acc[hp:hp + 64, h, :], rdh, ALU.mult)
        self.rr = list(range(8))

    def ffn(self, l):
        P = self.P
        H, XA, Z = self.H, self.XA, self.Z
        B0 = PH_OFF
        self.rms_fm(lambda k, c0, n: H[:, k, c0:c0 + n], 8, TILES, "norm_ffn", 8 * l,
                    lambda k, c0, n: XA[:, k, c0:c0 + n], 1.0 / D)
        UW = T + 6
        ub = [self.V(B0 + 32768 + 4352 * i, BF16, 2176)[:, 0:UW] for i in range(2)]
        sg = self.V(B0 + 41472, F32, 512)
        Dm = self.V(B0 + 43520, BF16, 6, 128)
        cbufL = self.V(CONVB_OFF, F32, 3, 88)
        ubase = [sq["c0"] + 2 * si for si, sq in enumerate(SEQS)]
        groups = [list(range(0, 8)), list(range(8, 16)), list(range(16, 22))]
        Wi = self.w_ffn_in[l]
        Wo = self.w_ffn_out[l]
        for grp in groups:
            wblk = {}
            for jj, jc in enumerate(grp):
                if jj % 4 == 0:
                    nb_ = min(4, len(grp) - jj) * 128
                    for part in range(2):
                        so = self.slot()
                        wv = self.V(so, BF16, 8, 512)[:, :, 0:nb_]
                        P.dma(wv, Wi[:, part * DFF + jc * 128: part * DFF + jc * 128 + nb_].rearrange("(k p) o -> p k o", p=128), q="pool")
                        wblk[part] = wv
                for part in range(2):
                    wv = wblk[part]
                    col = part * NJ + jc
                    for si, sq in enumerate(SEQS):
                        if sq["s"] is None:
                            P.memset(ub[part][:, ubase[si]:ubase[si] + 2], 0.0)
                        else:
                            P.dma(ub[part][:, ubase[si]:ubase[si] + 2], self.cst[l, sq["s"]][:, 2 * col:2 * col + 2], q="pool")
                    for k3 in range(3):
                        P.ts(Dm[:, part * 3 + k3, :], self.identb, self.sc("conv_w", l * 132 + k3 * 44 + col), None, ALU.mult)
                    for (c0, n) in TILES:
                        ps = self.bank()
                        for k in range(8):
                            P.mm(ps[:, 0:n], wv[:, k, (jj % 4) * 128:(jj % 4 + 1) * 128], XA[:, k, c0:c0 + n], start=(k == 0), stop=(k == 7))
                        if c0 < TP:
                            P.copy(ub[part][:, 2 + c0:2 + c0 + n], ps[:, 0:n], eng="act")
                            if c0 + n == TP:
                                P.copy(cbufL[:, 0, 2 * col:2 * col + 2], ps[:, n - 2:n])
                        else:
                            for si in (1, 2):
                                o = (si - 1) * LS
                                P.copy(ub[part][:, ubase[si] + 2:ubase[si] + 2 + LS], ps[:, o:o + LS], eng="act")
                                P.copy(cbufL[:, si, 2 * col:2 * col + 2], ps[:, o + LS - 2:o + LS])
                for si, sq in enumerate(SEQS):
                    tl = [(c0, n) for (c0, n) in TILES if c0 < TP] if sq["s"] is None else [(sq["c0"], sq["L"])]
                    for (c0, n) in tl:
                        pa = self.bank()
                        pg = self.bank()
                        u0 = ubase[si] + (c0 - sq["c0"])
                        for part, pp in ((0, pa), (1, pg)):
                            for k3 in range(3):
                                P.mm(pp[:, 0:n], Dm[:, part * 3 + k3, :], ub[part][:, u0 + k3:u0 + k3 + n], start=(k3 == 0), stop=(k3 == 2))
                        P.act(sg[:, 0:n], pg[:, 0:n], AF.Silu, bias=self.sc("conv_b", l * 44 + NJ + jc))
                        P.stt(Z[:, jj, c0:c0 + n], pa[:, 0:n], self.sc("conv_b", l * 44 + jc), sg[:, 0:n], ALU.add, ALU.mult)
            ng = len(grp)
            for half in range(2):
                so = self.slot()
                wv = self.V(so, BF16, 8, 512)[:, 0:ng, :]
                P.dma(wv, Wo[grp[0] * 128:(grp[-1] + 1) * 128, half * 512:(half + 1) * 512].rearrange("(k p) o -> p k o", p=128), q="pool")
                for mo in range(4):
                    m = half * 4 + mo
                    for (c0, n) in TILES:
                        ps = self.bank()
                        for jj in range(ng):
                            P.mm(ps[:, 0:n], wv[:, jj, mo * 128:(mo + 1) * 128], Z[:, jj, c0:c0 + n], start=(jj == 0), stop=(jj == ng - 1))
                        P.tt(H[:, m, c0:c0 + n], H[:, m, c0:c0 + n], ps[:, 0:n], ALU.add)
        P.dma(self.conv_o[l].rearrange("s p c -> p s c"), cbufL, q="sp")


_CACHE = {}


def _get_nc():
    if "nc" not in _CACHE:
        nc = bass.Bass("TRN2", target_bir_lowering=False)
        K(nc).build()
        _CACHE["nc"] = nc
    return _CACHE["nc"]


def _pp(v):
    v = np.asarray(v, np.float32)
    return np.ascontiguousarray(v.reshape(-1, 128).T)


def _consts():
    cf = np.zeros((128, NCF), np.float32)
    R = np.zeros((32, 32), np.float32)
    for m in range(16):
        R[16 + m, m] = -1.0
        R[m, 16 + m] = 1.0
    cf[64:96, 0:32] = R
    cb = np.zeros((128, NCB), np.float32)
    cb[:, 0:128] = 1.0
    cb[:, 128:256] = -30000.0 * np.eye(128, dtype=np.float32)
    cb[:, 256:384] = np.eye(128, dtype=np.float32)
    cb[0:64, 384:448] = 1.0
    cb[64:128, 448:512] = 1.0
    cb[0:64, 512:576] = 1.0
    cb[64:96, 576:608] = 1.0
    p = np.arange(128)[:, None]
    c = np.arange(512)[None, :]
    for i in range(4):
        cb[:, 640 + 512 * i: 640 + 512 * (i + 1)] = (((128 * i + p) // 64) > (c // 64)).astype(np.float32)
    return cf, cb


def _rope_tables():
    inv = (1.0 / (np.float32(10000.0) ** (np.arange(0, 32, 2, dtype=np.float32) / np.float32(32)))).astype(np.float32)
    pos = np.concatenate([np.arange(TP), PAST + np.arange(LS), PAST + np.arange(LS)]).astype(np.float32)
    ang = (pos[:, None] * inv[None, :]).astype(np.float32)
    c = np.cos(ang).astype(np.float32).T
    s = np.sin(ang).astype(np.float32).T
    return np.ascontiguousarray(np.stack([np.concatenate([c, c], 0), np.concatenate([s, s], 0)], 0))


def _pack_shared(inp):
    f = lambda k: np.asarray(inp[k], np.float32)
    sc = np.zeros((128, NSC), np.float32)

    def put(name, col, arr):
        arr = np.asarray(arr, np.float32)
        if arr.ndim == 1:
            arr = arr[:, None]
        sc[:arr.shape[0], SCOL[name] + col: SCOL[name] + col + arr.shape[1]] = arr

    for l in range(DEPTH):
        put("norm_mix", 8 * l, _pp(f("norm_mix_g")[l]))
        put("norm_ffn", 8 * l, _pp(f("norm_ffn_g")[l]))
        put("mem_norm", 8 * l, _pp(f("mem_norm_g")[l]))
        put("memq_g", l, np.tile(f("mem_q_norm_g")[l], 2))
        put("memk_g", l, np.tile(f("mem_k_norm_g")[l], 2))
        cw = f("ffn_conv_w")[l]
        for k3 in range(3):
            put("conv_w", l * 132 + k3 * 44, _pp(cw[k3]))
        put("conv_b", l * 44, _pp(f("ffn_conv_b")[l]))
    put("kv_norm", 0, _pp(f("kv_norm_g")))
    put("lat_norm", 0, _pp(f("latent_norm_g")))
    kg = np.zeros(128, np.float32)
    kg[64:96] = f("krope_norm_g")
    put("krope_g", 0, kg)
    kn = np.zeros(128, np.float32)
    kn[0:64] = f("k_nope_norm_g")
    put("knope_g", 0, kn)
    qs = np.zeros(128, np.float32)
    qs[0:64] = 1.0 / 64
    qs[64:96] = 1.0 / 32
    put("qscale", 0, qs)
    for j in range(2):
        qg = np.zeros(128, np.float32)
        qg[0:64] = f("q_nope_norm_g")[j]
        qg[64:96] = f("q_rope_norm_g")[j]
        put("q_g", j, qg)
        put("qlat_g", 6 * j, _pp(f("q_latent_norm_g")[j]))
        put("ssm_d", 6 * j, _pp(f("ssm_d")[j]))
        put("b_glu", 6 * j, _pp(f("b_glu")[j]))
        put("ssm_p", 72 * j, _pp(f("ssm_a_re")[j].reshape(-1)))
        put("ssm_p", 72 * j + 24, _pp(f("ssm_a_im")[j].reshape(-1)))
        put("ssm_p", 72 * j + 48, _pp(np.repeat(f("ssm_log_dt")[j], 64)))
    s5b = np.zeros((NA, 2, 128, 24, 128), np.float32)
    s5c = np.zeros((NA, 2, 128, 24, 128), np.float32)
    for l in range(NA):
        for r, (bk, ck) in enumerate((("ssm_b_re", "ssm_c_re"), ("ssm_b_im", "ssm_c_im"))):
            b = f(bk)[l]
            c = f(ck)[l]
            for i in range(24):
                q = i % 4
                for gg in range(2):
                    g = 2 * i + gg
                    rows = slice(32 * q + 16 * gg, 32 * q + 16 * gg + 16)
                    cols = slice(64 * gg, 64 * gg + 64)
                    s5b[l, r, rows, i, cols] = b[g].T
                    s5c[l, r, cols, i, rows] = c[g].T
    wuq = np.zeros((2, MIX, 12, 128), np.float32)
    wuq[:, :, :, 0:96] = f("w_uq").reshape(2, MIX, 12, 96)
    cf, cb = _consts()
    return dict(scal=sc, consf=cf, consb=cb, rope=_rope_tables(),
                w_mix_in=f("w_mix_in"), w_mix_out=f("w_mix_out"), w_ffn_in=f("w_ffn_in"), w_ffn_out=f("w_ffn_out"),
                w_mem_kv=f("w_mem_kv"), w_glu=f("w_glu"), w_dkv=f("w_dkv"), w_uk=f("w_uk"), w_uv=f("w_uv"),
                w_uq=np.ascontiguousarray(wuq.reshape(2, MIX, 12 * 128)),
                s5b=np.ascontiguousarray(s5b.reshape(NA, 2, 128, 24 * 128)),
                s5c=np.ascontiguousarray(s5c.reshape(NA, 2, 128, 24 * 128)))


def _pack_core(inp, c):
    f = lambda k: np.asarray(inp[k], np.float32)
    s0, s1 = 2 * c, 2 * c + 1
    xT = np.concatenate([f("x_prompt")[c].T, f("x_sample")[s0].T, f("x_sample")[s1].T], axis=1)
    d = dict(xT=np.ascontiguousarray(xT), memT=np.ascontiguousarray(f("mem_prompt")[c].T))
    d["latc"] = np.ascontiguousarray(np.stack([f("cache_mla_latent")[s].T for s in (s0, s1)]))
    d["krc"] = np.ascontiguousarray(np.stack([f("cache_mla_krope")[s].T for s in (s0, s1)]))
    d["cmk"] = np.ascontiguousarray(np.stack([np.stack([f("cache_mem_k")[l, s].reshape(256, 256).T for s in (s0, s1)]) for l in range(DEPTH)]))
    d["cmv"] = np.ascontiguousarray(np.stack([np.stack([f("cache_mem_v")[l, s].reshape(256, 256) for s in (s0, s1)]) for l in range(DEPTH)]))
    d["sst"] = np.ascontiguousarray(np.stack([np.stack([np.stack([_pp(f(k)[l, s].reshape(-1)) for k in ("state_ssm_re", "state_ssm_im")])
                                                        for s in (s0, s1)]) for l in range(NA)]))
    cst = np.zeros((DEPTH, 2, 128, 44, 2), np.float32)
    for l in range(DEPTH):
        for si, s in enumerate((s0, s1)):
            sc_ = f("state_conv")[l, s]
            cst[l, si] = sc_.reshape(2, 44, 128).transpose(2, 1, 0)
    d["cst"] = np.ascontiguousarray(cst.reshape(DEPTH, 2, 128, 88))
    return d


def kernel(**inputs):
    nc = _get_nc()
    shared = _pack_shared(inputs)
    in_maps = []
    for c in range(8):
        d = dict(shared)
        d.update(_pack_core(inputs, c))
        in_maps.append(d)
    res = run_bass_kernel_spmd(nc, in_maps, core_ids=list(range(8)))
    R = res.results
    B, DB = 8, 16
    y_p = np.stack([R[c]["yT"][:, :TP].T for c in range(B)])
    y_s = np.zeros((DB, LS, D), np.float32)
    lat_p = np.stack([R[c]["lat_o"][:, :TP].T for c in range(B)])
    kr_p = np.stack([R[c]["kr_o"][:, :TP].T for c in range(B)])
    lat_s = np.zeros((DB, LS, 256), np.float32)
    kr_s = np.zeros((DB, LS, 32), np.float32)
    memk = np.zeros((DEPTH, B, NMEM, 4, 64), np.float32)
    memv = np.zeros((DEPTH, B, NMEM, 4, 64), np.float32)
    ssm_p = np.zeros((2, NA, B, 48, 64), np.float32)
    ssm_s = np.zeros((2, NA, DB, 48, 64), np.float32)
    conv_p = np.zeros((DEPTH, B, 2, 2 * DFF), np.float32)
    conv_s = np.zeros((DEPTH, DB, 2, 2 * DFF), np.float32)
    for c in range(B):
        r = R[c]
        for l in range(DEPTH):
            memk[l, c] = r["memk_o"][l].T.reshape(NMEM, 4, 64)
            memv[l, c] = r["memv_o"][l].reshape(NMEM, 4, 64)
            cv = r["conv_o"][l].reshape(3, 128, 44, 2)
            for si in range(3):
                arr = cv[si].transpose(2, 1, 0).reshape(2, 2 * DFF)
                if si == 0:
                    conv_p[l, c] = arr
                else:
                    conv_s[l, 2 * c + si - 1] = arr
        for l in range(NA):
            for si in range(3):
                for ri in range(2):
                    arr = r["ssm_o"][l, si, ri].T.reshape(48, 64)
                    if si == 0:
                        ssm_p[ri, l, c] = arr
                    else:
                        ssm_s[ri, l, 2 * c + si - 1] = arr
        for si in range(2):
            cs = slice(TP + si * LS, TP + (si + 1) * LS)
            y_s[2 * c + si] = r["yT"][:, cs].T
            lat_s[2 * c + si] = r["lat_o"][:, cs].T
            kr_s[2 * c + si] = r["kr_o"][:, cs].T
    return (y_p, y_s, memk, memv, lat_p, kr_p, ssm_p[0], ssm_p[1], conv_p, lat_s, kr_s, ssm_s[0], ssm_s[1], conv_s)
```

```python
import math
import numpy as np
import concourse.bass as bass
import concourse.mybir as mybir
from concourse.bass_utils import run_bass_kernel_spmd

F32 = mybir.dt.float32
BF16 = mybir.dt.bfloat16
I32 = mybir.dt.int32
AF = mybir.ActivationFunctionType
ALU = mybir.AluOpType

D = 1024
DEPTH = 4
NA = 2
TP = 2048
LS = 32
T = TP + 2 * LS
PAST = 4096
DFF = 2816
NJ = DFF // 128
MIX = 768
NMEM = 256
EPS = 1e-6
MLA_SCALE = 96 ** -0.5
MEM_SCALE = 64 ** -0.5
TILES = [(0, 512), (512, 512), (1024, 512), (1536, 512), (2048, 64)]
SEQS = [dict(c0=0, L=TP, past=0, s=None), dict(c0=TP, L=LS, past=PAST, s=0), dict(c0=TP + LS, L=LS, past=PAST, s=1)]

H_OFF = 0
XA_OFF = 67584
Z_OFF = 101376
PH_OFF = 135168
PH_SIZE = 46080
PS_OFF = PH_OFF + PH_SIZE
SC_OFF = PS_OFF
CF_OFF = SC_OFF + 4096
CB_OFF = CF_OFF + 128
LATN_OFF = CB_OFF + 5376
KRN_OFF = LATN_OFF + 2 * T * 2
MKT_OFF = KRN_OFF + T * 2
MV_OFF = MKT_OFF + 4096
CONVB_OFF = MV_OFF + 4096
ARENA_BYTES = CONVB_OFF + 1056
assert ARENA_BYTES <= 212800, ARENA_BYTES
NCF = 32
NCB = 2688

ND_SEMS = 24

SCOL = {}
_n = 0
for _name, _w in [("norm_mix", 32), ("norm_ffn", 32), ("mem_norm", 32), ("kv_norm", 8), ("lat_norm", 2),
                  ("krope_g", 1), ("memq_g", 4), ("memk_g", 4), ("q_g", 2), ("knope_g", 1), ("qlat_g", 12),
                  ("ssm_d", 12), ("b_glu", 12), ("ssm_p", 144), ("qscale", 1), ("conv_w", 528), ("conv_b", 176)]:
    SCOL[_name] = _n
    _n += _w
NSC = 1024
assert _n <= NSC


def _foot(ap):
    name = ap.name
    off = int(ap.offset)
    dims = list(ap.ap)
    es = mybir.dt.size(ap.dtype)
    if str(ap.space) == "DRAM":
        if not name.startswith("scr_"):
            return None
        ext = sum((c - 1) * abs(st_) for st_, c in dims) + 1
        return (name, 0, 1, off * es, (off + ext) * es)
    ps, pc = dims[0]
    if ps == 0:
        ps = 1 << 40
    p0 = off // ps
    lo = off % ps
    ext = sum((c - 1) * abs(s) for s, c in dims[1:]) + 1
    if str(ap.space) == "PSUM":
        return ("PSUM", (p0 // 32) * 32, ((p0 + pc + 31) // 32) * 32, (lo * es // 2048) * 2048,
                (((lo + ext) * es + 2047) // 2048) * 2048)
    return (name, p0, p0 + pc, lo * es, (lo + ext) * es)


class Op:
    __slots__ = ("eng", "fn", "deps", "signal", "sigval", "dma", "dsem", "dval", "idx")

    def __init__(self, eng, fn, dma):
        self.eng = eng
        self.fn = fn
        self.deps = []
        self.signal = False
        self.sigval = 0
        self.dma = dma
        self.dsem = -1
        self.dval = 0
        self.idx = 0


PAGE = 2048
SAME_ENG_GAP = 10 ** 9


class Prog:
    def __init__(self, nc):
        self.nc = nc
        self.ops = []
        self.pages = {}
        self.ndma = 0
        self.nq = {"sp": 0, "pool": 0}
        self.ecnt = {}
        self.eseq = {}

    def _recs(self, f):
        seen = set()
        out = []
        for pg in range(f[3] // PAGE, (f[4] - 1) // PAGE + 1):
            lst = self.pages.get((f[0], pg))
            if not lst:
                continue
            alive = [r for r in lst if r[6]]
            if len(alive) != len(lst):
                lst[:] = alive
            for r in alive:
                if id(r) not in seen and r[1] < f[2] and f[1] < r[2] and r[3] < f[4] and f[3] < r[4]:
                    seen.add(id(r))
                    out.append(r)
        return out

    def _put(self, rec, f):
        for pg in range(f[3] // PAGE, (f[4] - 1) // PAGE + 1):
            self.pages.setdefault((f[0], pg), []).append(rec)

    def add(self, eng, fn, reads=(), writes=(), dma=False):
        op = Op(eng, fn, dma)
        op.idx = len(self.ops)
        if dma:
            half = ND_SEMS // 2
            i = self.nq[eng]
            self.nq[eng] += 1
            op.dsem = (i % half) + (0 if eng == "sp" else half)
            op.dval = (i // half + 1) * 16
            self.ndma += 1
        rf = [x for x in (_foot(a) for a in reads) if x is not None]
        wf = [x for x in (_foot(a) for a in writes) if x is not None]
        deps = {}

        myseq = self.ecnt.get(eng, 0)
        self.ecnt[eng] = myseq + 1
        self.eseq[op.idx] = myseq

        def need(o, raw, psum=False):
            if (not o.dma) and (not dma) and o.eng == eng:
                if eng == "pe" or not raw or psum:
                    return
                if myseq - self.eseq[o.idx] >= SAME_ENG_GAP:
                    return
            deps[o.idx] = o

        prf = [f for f in rf if f[0] == "PSUM"]
        rf = [f for f in rf if f[0] != "PSUM"]
        for f in prf:
            for r in self._recs(f):
                need(r[0], True, True)
        for f in rf:
            for r in self._recs(f):
                if r[5]:
                    need(r[0], True)
        for f in wf:
            for r in self._recs(f):
                need(r[0], False, f[0] == "PSUM")
                if f[1] <= r[1] and r[2] <= f[2] and f[3] <= r[3] and r[4] <= f[4]:
                    r[6] = False
        for f in prf:
            for r in self._recs(f):
                if (not r[0].dma) and r[0].eng == eng and r[1] == f[1] and r[2] == f[2] and r[3] == f[3] and r[4] == f[4]:
                    r[6] = False
            self._put([op, f[1], f[2], f[3], f[4], True, True], f)
        best = {}
        dl = []
        for o in deps.values():
            if o.dma:
                dl.append(o)
            elif o.eng not in best or best[o.eng].idx < o.idx:
                best[o.eng] = o
        op.deps = dl + list(best.values())
        for o in op.deps:
            o.signal = True
        for f in wf:
            self._put([op, f[1], f[2], f[3], f[4], True, True], f)
        for f in rf:
            if not dma:
                for r in self._recs(f):
                    if (not r[5]) and (not r[0].dma) and r[0].eng == eng and r[1] == f[1] and r[2] == f[2] \
                            and r[3] == f[3] and r[4] == f[4]:
                        r[6] = False
            self._put([op, f[1], f[2], f[3], f[4], False, True], f)
        self.ops.append(op)
        return op

    def emit(self):
        nc = self.nc
        engs = ["pe", "act", "dve", "pool", "sp"]
        cnt = {e: 0 for e in engs}
        last = {}
        for op in self.ops:
            if not op.dma:
                last[op.eng] = op
        for op in last.values():
            op.signal = True
        for op in self.ops:
            if op.dma:
                op.signal = True
            elif op.signal:
                cnt[op.eng] += 1
                op.sigval = cnt[op.eng]
        per = {e: [o for o in self.ops if o.eng == e] for e in engs}
        ctx = []
        esem = {}
        for e in engs:
            c = nc.semaphore("s_" + e)
            esem[e] = c.__enter__()
            ctx.append(c)
        dsem = []
        for i in range(ND_SEMS):
            c = nc.semaphore("d_%d" % i)
            dsem.append(c.__enter__())
            ctx.append(c)
        dfinal = [0] * ND_SEMS
        for op in self.ops:
            if op.dma:
                dfinal[op.dsem] = max(dfinal[op.dsem], op.dval)

        def run(e, eng):
            known = {}
            for op in per[e]:
                waits = {}
                for d in op.deps:
                    if d.dma:
                        k = ("d", d.dsem)
                        v = d.dval
                    else:
                        k = ("e", d.eng)
                        v = d.sigval
                    if known.get(k, 0) < v:
                        waits[k] = max(waits.get(k, 0), v)
                if op.dma and op.dval > 16:
                    k = ("d", op.dsem)
                    v = op.dval - 16
                    if known.get(k, 0) < v:
                        waits[k] = max(waits.get(k, 0), v)
                for k, v in waits.items():
                    s = dsem[k[1]] if k[0] == "d" else esem[k[1]]
                    eng.wait_ge(s, v)
                    known[k] = v
                ins = op.fn(eng)
                if op.dma:
                    ins.then_inc(dsem[op.dsem], 16)
                elif op.signal:
                    ins.then_inc(esem[e], 1)
            if e == "sp":
                for i in range(ND_SEMS):
                    if dfinal[i] > known.get(("d", i), 0):
                        eng.wait_ge(dsem[i], dfinal[i])
                for e2 in engs:
                    if cnt[e2] > known.get(("e", e2), 0):
                        eng.wait_ge(esem[e2], cnt[e2])
                for s_ in list(esem.values()) + dsem:
                    eng.sem_clear(s_)

        with nc.Block() as block:
            @block.tensor
            def _(eng):
                run("pe", eng)

            @block.scalar
            def _(eng):
                run("act", eng)

            @block.vector
            def _(eng):
                run("dve", eng)

            @block.gpsimd
            def _(eng):
                run("pool", eng)

            @block.sync
            def _(eng):
                run("sp", eng)
        for c in reversed(ctx):
            c.__exit__(None, None, None)

    def dma(self, out, in_, q="sp"):
        return self.add(q, lambda eng: eng.dma_start(out=out, in_=in_), [in_], [out], dma=True)

    def mm(self, out, lhsT, rhs, start=True, stop=True):
        return self.add("pe", lambda eng: eng.matmul(out, lhsT, rhs, start=start, stop=stop), [lhsT, rhs], [out])

    def act(self, out, in_, func, bias=None, scale=1.0):
        reads = [in_]
        kw = {}
        if bias is not None:
            kw["bias"] = bias
            if not isinstance(bias, (int, float)):
                reads.append(bias)
        if not isinstance(scale, (int, float)):
            reads.append(scale)
        return self.add("act", lambda e: e.activation(out, in_, func, scale=scale, **kw), reads, [out])

    def tt(self, out, in0, in1, op, eng="dve"):
        return self.add(eng, lambda e: e.tensor_tensor(out, in0, in1, op), [in0, in1], [out])

    def ts(self, out, in0, s1, s2, op0, op1=None, eng="dve"):
        reads = [in0] + [s for s in (s1, s2) if s is not None and not isinstance(s, (int, float))]
        kw = {}
        if op1 is not None:
            kw["op1"] = op1
        return self.add(eng, lambda e: e.tensor_scalar(out, in0, s1, s2, op0, **kw), reads, [out])

    def stt(self, out, in0, scalar, in1, op0, op1):
        reads = [in0, in1] + ([] if isinstance(scalar, (int, float)) else [scalar])
        return self.add("dve", lambda e: e.scalar_tensor_tensor(out, in0, scalar, in1, op0, op1), reads, [out])

    def copy(self, out, in_, eng="dve"):
        if eng == "act":
            return self.act(out, in_, AF.Copy)
        return self.add(eng, lambda e: e.tensor_copy(out, in_), [in_], [out])

    def memset(self, out, val, eng="dve"):
        return self.add(eng, lambda e: e.memset(out, val), [], [out])

    def recip(self, out, in_):
        return self.add("dve", lambda e: e.reciprocal(out, in_), [in_], [out])


class K:
    def __init__(self, nc):
        self.nc = nc
        self.P = Prog(nc)
        self.ring_i = 0
        self.bank_i = 0
        self.rr = list(range(8))
        self.held = set()
        self.last_bank = 0

    def V(self, off, dt, *shape, p0=0, p1=128):
        es = mybir.dt.size(dt)
        assert off % es == 0
        n = 1
        for s in shape:
            n *= s
        base = self.A if dt == F32 else self.A.bitcast(dt)
        v = base[p0:p1, off // es: off // es + n]
        if len(shape) == 2:
            v = v.rearrange("p (a b) -> p a b", a=shape[0])
        elif len(shape) == 3:
            v = v.rearrange("p (a b c) -> p a b c", a=shape[0], b=shape[1])
        return v

    def sc(self, name, col=0, p0=0, p1=128):
        c = SCOL[name] + col
        return self.V(SC_OFF + 4 * c, F32, 1, p0=p0, p1=p1)

    def bank(self, b=None):
        if b is None:
            while True:
                b = self.rr[self.bank_i % len(self.rr)]
                self.bank_i += 1
                if b not in self.held:
                    break
            self.last_bank = b
        return self.PS[:, b, :]

    def slot(self):
        i = self.ring_i % 4
        self.ring_i += 1
        return PH_OFF + 8192 * i

    def rms_stats(self, srcs, ones_ap, n, scale, sq_offs, rs_off, rstd_off, p0=0, p1=128, K_=None):
        P = self.P
        ps = self.bank()
        for i, s in enumerate(srcs):
            sq = self.V(sq_offs[i % len(sq_offs)], BF16, 512, p0=p0, p1=p1)[:, 0:n]
            P.act(sq, s, AF.Square)
            P.mm(ps[p0:p1, 0:n], ones_ap, sq, start=(i == 0), stop=(i == len(srcs) - 1))
        rs = self.V(rs_off, F32, 512, p0=p0, p1=p1)[:, 0:n]
        P.act(rs, ps[p0:p1, 0:n], AF.Ln, bias=EPS, scale=scale)
        rstd = self.V(rstd_off, F32, 512, p0=p0, p1=p1)[:, 0:n]
        P.act(rstd, rs, AF.Exp, scale=-0.5)
        return rstd

    def rms_fm(self, src, nk, tiles, gname, gcol0, dst, inv_n):
        L = PH_OFF + 32768
        for (c0, n) in tiles:
            rstd = self.rms_stats([src(k, c0, n) for k in range(nk)], self.ones, n, inv_n,
                                  [L, L + 2048], L + 4096, L + 6144)
            for k in range(nk):
                self.P.stt(dst(k, c0, n), src(k, c0, n), self.sc(gname, gcol0 + k), rstd, ALU.mult, ALU.mult)

    def linear(self, W, nk, rhs, mchunks, tiles, evac):
        P = self.P
        blocks = []
        cur = []
        for mc in mchunks:
            if cur and (mc[0] + mc[1] - cur[0][0] > 512):
                blocks.append(cur)
                cur = []
            cur.append(mc)
        if cur:
            blocks.append(cur)
        mi = 0
        for blk in blocks:
            lo = blk[0][0]
            hi = blk[-1][0] + blk[-1][1]
            so = self.slot()
            wv = self.V(so, BF16, nk, hi - lo)
            P.dma(wv, W[:, lo:hi].rearrange("(k p) o -> p k o", p=128), q="pool")
            for (m0, mn) in blk:
                for (c0, n) in tiles:
                    ps = self.bank()
                    for k in range(nk):
                        P.mm(ps[0:mn, 0:n], wv[:, k, m0 - lo:m0 - lo + mn], rhs(k, c0, n), start=(k == 0), stop=(k == nk - 1))
                    evac(mi, c0, n, ps[0:mn, 0:n])
                mi += 1

    def build(self):
        nc = self.nc
        P = self.P

        def din(name, shape):
            return nc.dram_tensor(name, list(shape), F32, kind="ExternalInput").ap()

        def dout(name, shape):
            return nc.dram_tensor(name, list(shape), F32, kind="ExternalOutput").ap()

        self.xT = din("xT", [D, T])
        self.memT = din("memT", [D, NMEM])
        self.latc = din("latc", [2, 256, PAST])
        self.krc = din("krc", [2, 32, PAST])
        self.cmk = din("cmk", [DEPTH, 2, 256, 256])
        self.cmv = din("cmv", [DEPTH, 2, 256, 256])
        self.sst = din("sst", [NA, 2, 2, 128, 24])
        self.cst = din("cst", [DEPTH, 2, 128, 88])
        self.scal = din("scal", [128, NSC])
        self.consf = din("consf", [128, NCF])
        self.consb = din("consb", [128, NCB])
        self.rope = din("rope", [2, 32, T])
        self.w_mix_in = din("w_mix_in", [DEPTH, D, D])
        self.w_mix_out = din("w_mix_out", [DEPTH, D, D])
        self.w_ffn_in = din("w_ffn_in", [DEPTH, D, 2 * DFF])
        self.w_ffn_out = din("w_ffn_out", [DEPTH, DFF, D])
        self.w_mem_kv = din("w_mem_kv", [DEPTH, D, 512])
        self.w_glu = din("w_glu", [NA, MIX, MIX])
        self.w_dkv = din("w_dkv", [D, 288])
        self.w_uk = din("w_uk", [256, MIX])
        self.w_uv = din("w_uv", [256, MIX])
        self.w_uq = din("w_uq", [2, MIX, 12 * 128])
        self.s5b = din("s5b", [NA, 2, 128, 24 * 128])
        self.s5c = din("s5c", [NA, 2, 128, 24 * 128])
        self.kscr = nc.dram_tensor("scr_k", [2, 12, 64, PAST], BF16, kind="Internal").ap()
        self.vscr = nc.dram_tensor("scr_v", [2, 12, PAST // 1024, 128, 8 * 64], BF16, kind="Internal").ap()
        self.yT = dout("yT", [D, T])
        self.memk_o = dout("memk_o", [DEPTH, 256, NMEM])
        self.memv_o = dout("memv_o", [DEPTH, NMEM, 256])
        self.lat_o = dout("lat_o", [256, T])
        self.kr_o = dout("kr_o", [32, T])
        self.ssm_o = dout("ssm_o", [NA, 3, 2, 128, 24])
        self.conv_o = dout("conv_o", [DEPTH, 3, 128, 88])

        with nc.sbuf_tensor("A", [128, ARENA_BYTES // 4], F32) as A_, \
                nc.psum_tensor("PS", [128, 8, 512], F32) as PS_, \
                nc.allow_low_precision("bf16 matmul operands, fp32 accumulation"):
            self.A = A_[:]
            self.PS = PS_
            self.H = self.V(H_OFF, F32, 8, T)
            self.XA = self.V(XA_OFF, BF16, 8, T)
            self.Z = self.V(Z_OFF, BF16, 8, T)
            cf = self.V(CF_OFF, F32, NCF)
            self.rrot = cf[:, 0:32]
            cb = self.V(CB_OFF, BF16, NCB)
            self.ones = cb[:, 0:128]
            self.onesb = cb[:, 0:64]
            self.negI = cb[:, 128:256]
            self.identb = cb[:, 256:384]
            self.bo64 = cb[:, 384:512]
            self.boq = cb[:, 512:640]
            self.masks = cb[:, 640:2688].rearrange("p (a b) -> p a b", a=4)
            self.latn = self.V(LATN_OFF, BF16, 2, T)
            self.krn = self.V(KRN_OFF, BF16, T)
            self.mkt = self.V(MKT_OFF, BF16, 4, 2, 256)
            self.mv = self.V(MV_OFF, BF16, 4, 2, 256)

            P.dma(self.V(SC_OFF, F32, NSC), self.scal, q="sp")
            P.dma(cf, self.consf, q="sp")
            P.dma(cb, self.consb, q="pool")
            xv = self.xT.rearrange("(k p) t -> p k t", p=128)
            for k in range(8):
                P.dma(self.H[:, k, :], xv[:, k, :], q="sp")

            import os
            self.dbg = int(os.environ.get("KDBG", "99"))
            self.mem_kv()
            for l in range(DEPTH):
                if self.dbg >= 2 and (self.dbg >= 6 or l == 0 or (self.dbg == 5 and l < 2)):
                    self.layer(l)
            yv = self.yT.rearrange("(k p) t -> p k t", p=128)
            for k in range(8):
                P.dma(yv[:, k, :], self.H[:, k, :], q="sp")
            P.emit()

    def mem_kv(self):
        P = self.P
        memf = self.V(XA_OFF, F32, 8, NMEM)
        mn = self.V(Z_OFF, BF16, 8, NMEM)
        P.dma(memf, self.memT.rearrange("(k p) t -> p k t", p=128), q="sp")
        L = PH_OFF + 32768
        stage = self.V(L + 8192, F32, 512)
        for l in range(DEPTH):
            self.rms_fm(lambda k, c0, n: memf[:, k, c0:c0 + n], 8, [(0, NMEM)], "mem_norm", 8 * l,
                        lambda k, c0, n: mn[:, k, c0:c0 + n], 1.0 / D)
            so = self.slot()
            wv = self.V(so, BF16, 8, 512)
            P.dma(wv, self.w_mem_kv[l].rearrange("(k p) o -> p k o", p=128), q="pool")
            for mc in range(2):
                ps = self.bank()
                for k in range(8):
                    P.mm(ps[:, 0:NMEM], wv[:, k, mc * 128:(mc + 1) * 128], mn[:, k, :], start=(k == 0), stop=(k == 7))
                rstd = self.rms_stats([ps[:, 0:NMEM]], self.bo64, NMEM, 1.0 / 64, [L], L + 4096, L + 6144)
                P.stt(stage[:, 0:NMEM], ps[:, 0:NMEM], self.sc("memk_g", l), rstd, ALU.mult, ALU.mult)
                P.dma(self.memk_o[l, mc * 128:(mc + 1) * 128, :], stage[:, 0:NMEM], q="sp")
                P.copy(self.mkt[:, l, mc, :], stage[:, 0:NMEM], eng="act")
            for tt in range(2):
                ps = self.bank()
                for k in range(8):
                    P.mm(ps[:, 0:256], mn[:, k, tt * 128:(tt + 1) * 128], wv[:, k, 256:512], start=(k == 0), stop=(k == 7))
                P.copy(stage[:, 256:512], ps[:, 0:256], eng="act")
                P.dma(self.memv_o[l, tt * 128:(tt + 1) * 128, :], stage[:, 256:512], q="sp")
                P.copy(self.mv[:, l, tt, :], stage[:, 256:512])

    def layer(self, l):
        P = self.P
        H, XA, Z = self.H, self.XA, self.Z
        if l == NA:
            self.ckv()
        self.rms_fm(lambda k, c0, n: H[:, k, c0:c0 + n], 8, TILES, "norm_mix", 8 * l,
                    lambda k, c0, n: XA[:, k, c0:c0 + n], 1.0 / D)
        self.linear(self.w_mix_in[l], 8, lambda k, c0, n: XA[:, k, c0:c0 + n], [(m * 128, 128) for m in range(8)], TILES,
                    lambda mi, c0, n, ps: P.copy(Z[:, mi, c0:c0 + n], ps, eng="act"))
        self.mem_attend_all(l)
        if self.dbg < 3:
            return
        if l < NA:
            self.s5(l)
        else:
            self.mla(l - NA)
        if self.dbg < 4:
            return
        self.linear(self.w_mix_out[l], 8, lambda k, c0, n: XA[:, k, c0:c0 + n], [(m * 128, 128) for m in range(8)], TILES,
                    lambda mi, c0, n, ps: P.tt(H[:, mi, c0:c0 + n], H[:, mi, c0:c0 + n], ps, ALU.add))
        self.ffn(l)

    def mem_attend_all(self, l):
        P = self.P
        Z, XA = self.Z, self.XA
        B0 = PH_OFF
        qm = self.V(B0 + 4096, BF16, 2, T)
        sqs = [self.V(B0 + 12544 + 1024 * i, BF16, 512) for i in range(3)]
        rss = [self.V(B0 + 15616 + 2048 * i, F32, 512) for i in range(3)]
        units = []
        for (c0, n) in TILES:
            for mc in range(2):
                units.append((c0, n, mc))
        box = {}

        def A(u):
            c0, n, mc = units[u]
            ps = self.bank()
            box[u] = ps
            sq = sqs[u % 3][:, 0:n]
            P.act(sq, Z[:, 6 + mc, c0:c0 + n], AF.Square)
            P.mm(ps[:, 0:n], self.bo64, sq)

        def Bq(u):
            c0, n, mc = units[u]
            P.act(rss[u % 3][:, 0:n], box[u][:, 0:n], AF.Ln, bias=EPS, scale=1.0 / 64)

        def C(u):
            c0, n, mc = units[u]
            r = rss[u % 3][:, 0:n]
            P.act(r, r, AF.Exp, scale=-0.5)
            P.stt(qm[:, mc, c0:c0 + n], Z[:, 6 + mc, c0:c0 + n], self.sc("memq_g", l), r, ALU.mult, ALU.mult)
        nu = len(units)
        for step in range(nu + 2):
            if step < nu:
                A(step)
            if 0 <= step - 1 < nu:
                Bq(step - 1)
            if 0 <= step - 2 < nu:
                C(step - 2)
        KtS = self.V(B0, BF16, 2, 256)
        VmS = self.V(B0 + 1024, BF16, 2, 256)
        pi = 0
        for sq_ in SEQS:
            if sq_["s"] is None:
                Kt = self.mkt[:, l]
                Vm = self.mv[:, l]
                tiles = [(c0, n) for (c0, n) in TILES if c0 < TP]
            else:
                Kt, Vm = KtS, VmS
                P.dma(Kt, self.cmk[l, sq_["s"]].rearrange("(k p) t -> p k t", p=128), q="pool")
                P.dma(Vm, self.cmv[l, sq_["s"]].rearrange("(k p) t -> p k t", p=128), q="pool")
                tiles = [(sq_["c0"], sq_["L"])]
            for (c0, n) in tiles:
                for mc in range(2):
                    q_ = qm[:, mc, c0:c0 + n]
                    pso = self.bank()
                    psd = self.bank()
                    sc_ = []
                    for hh in range(2):
                        r0, r1 = hh * 64, hh * 64 + 64
                        for kc in range(2):
                            pss = self.bank()
                            P.mm(pss[:, 0:n], Kt[r0:r1, mc, kc * 128:(kc + 1) * 128], q_[r0:r1, :])
                            sc_.append((hh, kc, pss))
                    for (hh, kc, pss) in sc_:
                        r0, r1 = hh * 64, hh * 64 + 64
                        hd = 2 * mc + hh
                        pT = self.V(B0 + 21760 + 1024 * (pi % 8), BF16, 512)[:, 0:n]
                        pi += 1
                        P.act(pT, pss[:, 0:n], AF.Exp, scale=MEM_SCALE)
                        P.mm(pso[r0:r1, 0:n], Vm[:, kc, hd * 64:(hd + 1) * 64], pT, start=(kc == 0), stop=(kc == 1))
                        P.mm(psd[r0:r1, 0:n], self.onesb, pT, start=(kc == 0), stop=(kc == 1))
                    rd = self.V(B0 + 29952 + 2048 * ((pi // 4) % 2), F32, 512)[:, 0:n]
                    P.recip(rd, psd[:, 0:n])
                    P.tt(XA[:, 6 + mc, c0:c0 + n], pso[:, 0:n], rd, ALU.mult)

    def s5(self, l):
        P = self.P
        Z, XA = self.Z, self.XA
        B0 = PH_OFF
        NU = 10
        pb = B0 + 34816

        def pv(i):
            return self.V(pb + 96 * i, F32, 24)
        a_re = self.V(SC_OFF + 4 * (SCOL["ssm_p"] + 72 * l), F32, 24)
        a_im = self.V(SC_OFF + 4 * (SCOL["ssm_p"] + 72 * l + 24), F32, 24)
        ldt = self.V(SC_OFF + 4 * (SCOL["ssm_p"] + 72 * l + 48), F32, 24)
        dt, mag, v, vi, fr, t0, t1p, den, xre, f_re, f_im, lam_re, lam_im = [pv(i) for i in range(13)]
        vI = self.V(pb + 96 * 13, I32, 24)
        Uc = self.V(pb + 1344, F32, NU, 24)
        Us = self.V(pb + 2304, F32, NU, 24)
        cin = self.V(pb + 3264, F32, 4)
        P.act(dt, ldt, AF.Exp)
        P.tt(t0, a_re, dt, ALU.mult)
        P.act(mag, t0, AF.Exp)
        P.tt(t1p, a_im, dt, ALU.mult)
        twopi = 2 * math.pi
        for which in range(2):
            P.ts(v, t1p, 1.0 / twopi, 0.25 if which == 0 else 0.0, ALU.mult, ALU.add)
            P.copy(vI, v)
            P.copy(vi, vI)
            P.tt(fr, v, vi, ALU.subtract)
            P.act(Uc[:, 0, :] if which == 0 else Us[:, 0, :], fr, AF.Sin, scale=6.2831845)
        P.tt(lam_re, Uc[:, 0, :], mag, ALU.mult)
        P.tt(lam_im, Us[:, 0, :], mag, ALU.mult)
        P.tt(t0, a_re, a_re, ALU.mult)
        P.tt(t1p, a_im, a_im, ALU.mult)
        P.tt(den, t0, t1p, ALU.add)
        P.recip(den, den)
        P.ts(xre, lam_re, -1.0, None, ALU.add)
        P.tt(t0, xre, a_re, ALU.mult)
        P.tt(t1p, lam_im, a_im, ALU.mult)
        P.tt(t0, t0, t1p, ALU.add)
        P.tt(f_re, t0, den, ALU.mult)
        P.tt(t0, lam_im, a_re, ALU.mult)
        P.tt(t1p, xre, a_im, ALU.mult)
        P.tt(t0, t0, t1p, ALU.subtract)
        P.tt(f_im, t0, den, ALU.mult)
        for k in range(NU - 1):
            P.tt(t0, Uc[:, k, :], Uc[:, k, :], ALU.mult)
            P.tt(t1p, Us[:, k, :], Us[:, k, :], ALU.mult)
            P.tt(Uc[:, k + 1, :], t0, t1p, ALU.subtract)
            P.tt(t0, Uc[:, k, :], Us[:, k, :], ALU.mult)
            P.ts(Us[:, k + 1, :], t0, 2.0, None, ALU.mult)

        cosT = self.V(B0 + 8192, F32, 512)
        sinT = self.V(B0 + 10240, F32, 512)
        Freb = self.V(B0 + 12288, BF16, 512)
        Fimb = self.V(B0 + 13312, BF16, 512)
        cosb = self.V(B0 + 14336, BF16, 512)
        sinb = self.V(B0 + 15360, BF16, 512)
        nsinb = self.V(B0 + 39424, BF16, 512)
        t1 = self.V(B0 + 16384, F32, 512)
        t2 = self.V(B0 + 18432, F32, 512)
        xr = self.V(B0 + 20480, BF16, 512)
        xi = self.V(B0 + 21504, BF16, 512)
        ab = self.V(B0 + 22528, BF16, 512)
        bb = self.V(B0 + 23552, BF16, 512)
        cb_ = self.V(B0 + 44544, BF16, 512)
        rr_ = self.V(B0 + 24576, F32, 512)
        ri_ = self.V(B0 + 26624, F32, 512)
        rawr = self.V(B0 + 40448, BF16, 512)
        rawi = self.V(B0 + 41472, BF16, 512)
        rrbs = [self.V(B0 + 42496, BF16, 512), self.V(B0 + 4096, BF16, 512)]
        ribs = [self.V(B0 + 43520, BF16, 512), self.V(B0 + 5120, BF16, 512)]
        ytmp = self.V(B0 + 32768, F32, 512)
        cars = [self.V(pb + 3296 + 192 * si, F32, 2, 24) for si in range(3)]
        fins = [self.V(pb + 3872 + 192 * si, F32, 2, 24) for si in range(3)]
        for si, sq in enumerate(SEQS):
            if sq["s"] is not None:
                P.dma(cars[si], self.sst[l, sq["s"]].rearrange("r p i -> p r i"), q="sp")
        blocks = []
        for si, sq in enumerate(SEQS):
            TB = min(512, sq["L"])
            for tb in range(sq["L"] // TB):
                blocks.append((si, tb, TB, sq["c0"] + tb * TB))
        assert len(blocks) <= 6
        hb_i = 0
        for cb in range(6):
            wo = B0
            wB = self.V(wo, BF16, 2, 4, 128)
            wC = self.V(wo + 2048, BF16, 2, 4, 128)
            for r in range(2):
                P.dma(wB[:, r], self.s5b[l, r][:, cb * 512:(cb + 1) * 512].rearrange("p (q m) -> p q m", q=4), q="pool")
                P.dma(wC[:, r], self.s5c[l, r][:, cb * 512:(cb + 1) * 512].rearrange("p (q m) -> p q m", q=4), q="pool")
            psy = [self.bank(2 + bi) for bi in range(len(blocks))]
            for q in range(4):
                i = 4 * cb + q
                P.memset(cosT[:, 0:1], 1.0)
                P.memset(sinT[:, 0:1], 0.0)
                for k in range(9):
                    d = 1 << k
                    uc, us = Uc[:, k, i:i + 1], Us[:, k, i:i + 1]
                    if k % 2 == 0:
                        P.ts(t1[:, 0:d], sinT[:, 0:d], us, -1.0, ALU.mult, ALU.mult)
                        P.ts(t2[:, 0:d], cosT[:, 0:d], us, None, ALU.mult)
                        P.stt(cosT[:, d:2 * d], cosT[:, 0:d], uc, t1[:, 0:d], ALU.mult, ALU.add)
                        P.stt(sinT[:, d:2 * d], sinT[:, 0:d], uc, t2[:, 0:d], ALU.mult, ALU.add)
                    else:
                        P.ts(t2[:, 0:d], cosT[:, 0:d], us, None, ALU.mult)
                        P.ts(t1[:, 0:d], sinT[:, 0:d], us, -1.0, ALU.mult, ALU.mult)
                        P.stt(sinT[:, d:2 * d], sinT[:, 0:d], uc, t2[:, 0:d], ALU.mult, ALU.add)
                        P.stt(cosT[:, d:2 * d], cosT[:, 0:d], uc, t1[:, 0:d], ALU.mult, ALU.add)
                P.ts(t1, sinT, f_im[:, i:i + 1], None, ALU.mult)
                P.ts(t2, sinT, f_re[:, i:i + 1], -1.0, ALU.mult, ALU.mult)
                P.stt(Freb, cosT, f_re[:, i:i + 1], t1, ALU.mult, ALU.add)
                P.stt(Fimb, cosT, f_im[:, i:i + 1], t2, ALU.mult, ALU.add)
                P.copy(cosb, cosT, eng="act")
                P.copy(sinb, sinT, eng="act")
                P.act(nsinb, sinT, AF.Copy, scale=-1.0)
                def front_a(bi):
                    si, tb, TB, c0 = blocks[bi]
                    u = Z[:, cb, c0:c0 + TB]
                    psr = self.bank(0)
                    psi = self.bank(1)
                    P.mm(psr[:, 0:TB], wB[:, 0, q, :], u)
                    P.mm(psi[:, 0:TB], wB[:, 1, q, :], u)
                    P.copy(rawr[:, 0:TB], psr[:, 0:TB], eng="act")
                    P.copy(rawi[:, 0:TB], psi[:, 0:TB], eng="act")

                def front_b(bi):
                    si, tb, TB, c0 = blocks[bi]
                    a_, b_, c_ = ab[:, 0:TB], bb[:, 0:TB], cb_[:, 0:TB]
                    P.tt(a_, rawr[:, 0:TB], Freb[:, 0:TB], ALU.mult)
                    P.tt(b_, rawi[:, 0:TB], Fimb[:, 0:TB], ALU.mult)
                    P.tt(c_, rawr[:, 0:TB], Fimb[:, 0:TB], ALU.mult)
                    P.tt(xi[:, 0:TB], rawi[:, 0:TB], Freb[:, 0:TB], ALU.mult)
                    P.tt(xr[:, 0:TB], a_, b_, ALU.subtract)
                    P.tt(xi[:, 0:TB], c_, xi[:, 0:TB], ALU.add)

                front_a(0)
                front_b(0)
                for bi, (si, tb, TB, c0) in enumerate(blocks):
                    sq = SEQS[si]
                    car = cars[si]
                    a_, b_ = ab[:, 0:TB], bb[:, 0:TB]
                    rrb, rib = rrbs[hb_i % 2], ribs[hb_i % 2]
                    if tb == 0 and sq["s"] is None:
                        ini_r, ini_i = 0.0, 0.0
                        rd_ = []
                    else:
                        kk = 0 if tb == 0 else int(math.log2(TB))
                        uc, us = Uc[:, kk, i:i + 1], Us[:, kk, i:i + 1]
                        cr, ci = car[:, 0, i:i + 1], car[:, 1, i:i + 1]
                        P.ts(cin[:, 0:1], ci, us, -1.0, ALU.mult, ALU.mult)
                        P.ts(cin[:, 1:2], cr, us, None, ALU.mult)
                        P.stt(cin[:, 0:1], cr, uc, cin[:, 0:1], ALU.mult, ALU.add)
                        P.stt(cin[:, 1:2], ci, uc, cin[:, 1:2], ALU.mult, ALU.add)
                        ini_r, ini_i = cin[:, 0:1], cin[:, 1:2]
                        rd_ = [cin]
                    if bi + 1 < len(blocks):
                        front_a(bi + 1)
                    mbc = mag[:, i:i + 1].to_broadcast([128, TB])
                    for (o_, x_, in_) in ((rr_, xr, ini_r), (ri_, xi, ini_i)):
                        P.add("dve", (lambda o_=o_, x_=x_, in_=in_, mbc=mbc, TB=TB:
                                      (lambda e: e.tensor_tensor_scan(o_[:, 0:TB], mbc, x_[:, 0:TB], in_, ALU.mult, ALU.add)))(),
                              [x_[:, 0:TB], mag[:, i:i + 1]] + rd_, [o_[:, 0:TB]])
                    P.copy(car[:, 0, i:i + 1], rr_[:, TB - 1:TB], eng="act")
                    P.copy(car[:, 1, i:i + 1], ri_[:, TB - 1:TB], eng="act")
                    P.copy(rrb[:, 0:TB], rr_[:, 0:TB], eng="act")
                    P.copy(rib[:, 0:TB], ri_[:, 0:TB], eng="act")
                    if tb == sq["L"] // TB - 1:
                        fin = fins[si]
                        cc, ss = cosT[:, TB - 1:TB], sinT[:, TB - 1:TB]
                        P.ts(cin[:, 2:3], ri_[:, TB - 1:TB], ss, -1.0, ALU.mult, ALU.mult)
                        P.ts(cin[:, 3:4], rr_[:, TB - 1:TB], ss, None, ALU.mult)
                        P.stt(fin[:, 0, i:i + 1], rr_[:, TB - 1:TB], cc, cin[:, 2:3], ALU.mult, ALU.add)
                        P.stt(fin[:, 1, i:i + 1], ri_[:, TB - 1:TB], cc, cin[:, 3:4], ALU.mult, ALU.add)
                    if bi + 1 < len(blocks):
                        front_b(bi + 1)
                    hb = self.V(B0 + 28672 + 2048 * (hb_i % 2), BF16, 2, 512)
                    hb_i += 1
                    c_ = cb_[:, 0:TB]
                    P.tt(a_, rrb[:, 0:TB], cosb[:, 0:TB], ALU.mult)
                    P.tt(b_, rib[:, 0:TB], sinb[:, 0:TB], ALU.mult)
                    P.tt(c_, rrb[:, 0:TB], nsinb[:, 0:TB], ALU.mult)
                    P.tt(hb[:, 1, 0:TB], rib[:, 0:TB], cosb[:, 0:TB], ALU.mult)
                    P.tt(hb[:, 0, 0:TB], a_, b_, ALU.subtract)
                    P.tt(hb[:, 1, 0:TB], c_, hb[:, 1, 0:TB], ALU.subtract)
                    P.mm(psy[bi][:, 0:TB], wC[:, 0, q, :], hb[:, 0, 0:TB], start=(q == 0), stop=False)
                    P.mm(psy[bi][:, 0:TB], wC[:, 1, q, :], hb[:, 1, 0:TB], start=False, stop=(q == 3))
            for bi, (si, tb, TB, c0) in enumerate(blocks):
                yt = ytmp[:, 0:TB]
                P.stt(yt, Z[:, cb, c0:c0 + TB], self.sc("ssm_d", 6 * l + cb), psy[bi][:, 0:TB], ALU.mult, ALU.add)
                P.act(XA[:, cb, c0:c0 + TB], yt, AF.Gelu_apprx_tanh)
        for si in range(3):
            P.dma(self.ssm_o[l, si].rearrange("r p i -> p r i"), fins[si], q="sp")
        self.linear(self.w_glu[l], 6, lambda k, c0, n: XA[:, k, c0:c0 + n], [(m * 128, 128) for m in range(6)], TILES,
                    lambda mi, c0, n, ps: P.act(Z[:, mi, c0:c0 + n], ps, AF.Sigmoid, bias=self.sc("b_glu", 6 * l + mi)))
        for m in range(6):
            P.tt(XA[:, m, :], XA[:, m, :], Z[:, m, :], ALU.mult)

    def ckv(self):
        P = self.P
        H, XA = self.H, self.XA
        B0 = PH_OFF
        self.rms_fm(lambda k, c0, n: H[:, k, c0:c0 + n], 8, TILES, "kv_norm", 0,
                    lambda k, c0, n: XA[:, k, c0:c0 + n], 1.0 / D)
        wv = self.V(B0, BF16, 8, 288)
        P.dma(wv, self.w_dkv.rearrange("(k p) o -> p k o", p=128), q="pool")
        S0 = B0 + 8192
        lst = self.V(B0 + 16384, F32, 2, 512)
        rC = self.V(B0 + 20480, F32, 512)
        rS = self.V(B0 + 22528, F32, 512)
        knf = self.V(B0 + 24576, F32, 512)
        t1 = self.V(B0 + 26624, F32, 512)
        t2 = self.V(B0 + 28672, F32, 512)
        krs = self.V(B0 + 30720, F32, 512)
        for (c0, n) in TILES:
            pss = [self.bank(), self.bank(), self.bank()]
            for mc, (m0, mn) in enumerate([(0, 128), (128, 128), (256, 32)]):
                o = pss[mc][0:128, 0:n] if mc < 2 else pss[mc][64:96, 0:n]
                for k in range(8):
                    P.mm(o, wv[:, k, m0:m0 + mn], XA[:, k, c0:c0 + n], start=(k == 0), stop=(k == 7))
            rstd = self.rms_stats([pss[0][:, 0:n], pss[1][:, 0:n]], self.ones, n, 1.0 / 256, [S0, S0 + 2048], S0 + 4096, S0 + 6144)
            for mc in range(2):
                P.stt(lst[:, mc, 0:n], pss[mc][:, 0:n], self.sc("lat_norm", mc), rstd, ALU.mult, ALU.mult)
                P.dma(self.lat_o[mc * 128:(mc + 1) * 128, c0:c0 + n], lst[:, mc, 0:n], q="sp")
                P.copy(self.latn[:, mc, c0:c0 + n], lst[:, mc, 0:n], eng="act")
            pk = pss[2][64:96, 0:n]
            rstd2 = self.rms_stats([pk], self.boq[64:96, 64:96], n, 1.0 / 32, [S0], S0 + 4096, S0 + 6144, p0=64, p1=96)
            P.stt(knf[64:96, 0:n], pk, self.sc("krope_g", 0, 64, 96), rstd2, ALU.mult, ALU.mult)
            self.rope_apply(knf, krs, c0, n, rC, rS, t1, t2)
            P.dma(self.kr_o[:, c0:c0 + n], krs[64:96, 0:n], q="sp")
            P.copy(self.krn[64:96, c0:c0 + n], krs[64:96, 0:n], eng="act")

    def rope_apply(self, src, dst, c0, n, rC, rS, t1, t2):
        P = self.P
        P.dma(rC[64:96, 0:n], self.rope[0][:, c0:c0 + n], q="sp")
        P.dma(rS[64:96, 0:n], self.rope[1][:, c0:c0 + n], q="sp")
        pr = self.bank()
        P.mm(pr[64:96, 0:n], self.rrot[64:96, :], src[64:96, 0:n])
        P.tt(t1[64:96, 0:n], pr[64:96, 0:n], rS[64:96, 0:n], ALU.mult)
        P.tt(t2[64:96, 0:n], src[64:96, 0:n], rC[64:96, 0:n], ALU.mult)
        P.tt(dst[64:96, 0:n], t1[64:96, 0:n], t2[64:96, 0:n], ALU.add)

    def mla(self, j):
        P = self.P
        Z, XA = self.Z, self.XA
        B0 = PH_OFF
        self.rms_fm(lambda k, c0, n: Z[:, k, c0:c0 + n], 6, TILES, "qlat_g", 6 * j,
                    lambda k, c0, n: Z[:, k, c0:c0 + n], 1.0 / MIX)
        wuk = self.V(B0, BF16, 2, MIX)
        wuv = self.V(B0 + 3072, BF16, 2, MIX)
        P.dma(wuk, self.w_uk.rearrange("(k p) o -> p k o", p=128), q="pool")
        P.dma(wuv, self.w_uv.rearrange("(k p) o -> p k o", p=128), q="pool")
        wqs = [self.V(B0 + 6144 + 1536 * i, BF16, 6, 128) for i in range(2)]
        qnf = self.V(B0 + 28672, F32, 512)
        SQ, RS, RSTD = B0 + 30720, B0 + 31744, B0 + 33792
        rC = self.V(B0 + 38912, F32, 512)
        rS = self.V(B0 + 40960, F32, 512)
        t1 = self.V(B0 + 43008, F32, 512)
        t2 = self.V(RS, F32, 512)
        acc = self.V(B0 + 25600, F32, 12, 32)
        fz = self.V(B0 + 43008, F32, 512)
        PT0 = B0 + 35840
        qscale = self.sc("qscale", 0)
        self.rr = [0, 1, 2, 3, 4, 5]
        st = dict(pt=0, u=0)

        def head_build(h, lat, nkeys, Kcat, Vh):
            stages = []
            nkc = (nkeys + 127) // 128
            for kt in range(0, nkeys, 512):
                kn = min(512, nkeys - kt)
                box = {}

                def s1(kt=kt, kn=kn, box=box):
                    ps = self.bank()
                    box["ps"] = ps
                    box["b"] = self.last_bank
                    self.held.add(self.last_bank)
                    for kc2 in range(2):
                        P.mm(ps[0:64, 0:kn], wuk[:, kc2, h * 64:(h + 1) * 64], lat[:, kc2, kt:kt + kn], start=(kc2 == 0), stop=(kc2 == 1))
                    sq = self.V(SQ, BF16, 512, p0=0, p1=64)[:, 0:kn]
                    P.act(sq, ps[0:64, 0:kn], AF.Square)

                def s2(kt=kt, kn=kn, box=box):
                    sq = self.V(SQ, BF16, 512, p0=0, p1=64)[:, 0:kn]
                    p2 = self.bank()
                    box["p2"] = p2
                    P.mm(p2[0:64, 0:kn], self.bo64[0:64, 0:64], sq)
                    rs = self.V(RS, F32, 512, p0=0, p1=64)[:, 0:kn]
                    P.act(rs, p2[0:64, 0:kn], AF.Ln, bias=EPS, scale=1.0 / 64)

                def s3(kt=kt, kn=kn, box=box):
                    rs = self.V(RS, F32, 512, p0=0, p1=64)[:, 0:kn]
                    rstd = self.V(RSTD, F32, 512, p0=0, p1=64)[:, 0:kn]
                    P.act(rstd, rs, AF.Exp, scale=-0.5)
                    P.stt(Kcat[0:64, kt:kt + kn], box["ps"][0:64, 0:kn], self.sc("knope_g", 0, 0, 64), rstd, ALU.mult, ALU.mult)
                    self.held.discard(box["b"])
                stages += [s1, s2, s3]
            for g0 in range(0, nkc, 8):
                def sv(g0=g0):
                    ps = self.bank()
                    g1 = min(nkc, g0 + 8)
                    for kc in range(g0, g1):
                        kn = min(128, nkeys - kc * 128)
                        for kc2 in range(2):
                            P.mm(ps[0:kn, (kc - g0) * 64:(kc - g0 + 1) * 64], lat[:, kc2, kc * 128:kc * 128 + kn],
                                 wuv[:, kc2, h * 64:(h + 1) * 64], start=(kc2 == 0), stop=(kc2 == 1))
                    knl = min(128, nkeys - (g1 - 1) * 128)
                    vo = (h % 2) * 64
                    if knl == 128:
                        P.copy(Vh[:, g0:g1, vo:vo + 64], ps[:, 0:(g1 - g0) * 64].rearrange("p (a b) -> p a b", b=64))
                    else:
                        if g1 - 1 > g0:
                            P.copy(Vh[:, g0:g1 - 1, vo:vo + 64], ps[:, 0:(g1 - 1 - g0) * 64].rearrange("p (a b) -> p a b", b=64))
                        P.copy(Vh[0:knl, g1 - 1, vo:vo + 64], ps[0:knl, (g1 - 1 - g0) * 64:(g1 - g0) * 64])
                stages.append(sv)
            return stages

        def load_wq(h):
            P.dma(wqs[h % 2], self.w_uq[j][:, h * 128:(h + 1) * 128].rearrange("(k p) o -> p k o", p=128), q="pool")

        def q_build(h, c0, n, Qdst):
            wq = wqs[h % 2]
            box = {}

            def s1():
                psq = self.bank()
                box["psq"] = psq
                box["b"] = self.last_bank
                self.held.add(self.last_bank)
                for k in range(6):
                    P.mm(psq[:, 0:n], wq[:, k, :], Z[:, k, c0:c0 + n], start=(k == 0), stop=(k == 5))
                sq = self.V(SQ, BF16, 512)[:, 0:n]
                P.act(sq, psq[:, 0:n], AF.Square)
                P.dma(rC[64:96, 0:n], self.rope[0][:, c0:c0 + n], q="sp")
                P.dma(rS[64:96, 0:n], self.rope[1][:, c0:c0 + n], q="sp")

            def s2():
                sq = self.V(SQ, BF16, 512)[:, 0:n]
                p2 = self.bank()
                P.mm(p2[:, 0:n], self.boq, sq)
                rs = self.V(RS, F32, 512)[:, 0:n]
                P.act(rs, p2[:, 0:n], AF.Ln, bias=EPS, scale=qscale)

            def s3():
                rs = self.V(RS, F32, 512)[:, 0:n]
                rstd = self.V(RSTD, F32, 512)[:, 0:n]
                P.act(rstd, rs, AF.Exp, scale=-0.5)
                P.stt(qnf[:, 0:n], box["psq"][:, 0:n], self.sc("q_g", j), rstd, ALU.mult, ALU.mult)
                P.copy(Qdst[0:64, :], qnf[0:64, 0:n])
                self.held.discard(box["b"])

            def s4():
                pr = self.bank()
                box["pr"] = pr
                box["b2"] = self.last_bank
                self.held.add(self.last_bank)
                P.mm(pr[64:96, 0:n], self.rrot[64:96, :], qnf[64:96, 0:n])
                P.tt(t2[64:96, 0:n], qnf[64:96, 0:n], rC[64:96, 0:n], ALU.mult)

            def s5_():
                P.tt(t1[64:96, 0:n], box["pr"][64:96, 0:n], rS[64:96, 0:n], ALU.mult)
                P.tt(Qdst[64:96, :], t1[64:96, 0:n], t2[64:96, 0:n], ALU.add)
                self.held.discard(box["b2"])
            return [s1, s2, s3, s4, s5_]

        def run_all(stages):
            for f in stages:
                f()

        def core(Kcat, Vh, Q, n, kcs, nkeys, diag0, pso, psd, hp, fillers=()):
            G = 512 // n if n < 512 else 1
            groups = [kcs[gi:gi + G] for gi in range(0, len(kcs), G)]

            def S(grp):
                pss = self.bank()
                pb_ = self.last_bank
                self.held.add(pb_)
                pT = self.V(PT0 + 1024 * (st["pt"] % 3), BF16, 512)
                st["pt"] += 1
                cl = 0
                for gj, kc in enumerate(grp):
                    kn = min(128, nkeys - kc * 128)
                    diag = diag0 is not None and kc >= diag0
                    if diag:
                        cl = min(128 * (kc - diag0), n - 128)
                    P.mm(pss[0:kn, gj * n + cl:(gj + 1) * n], Kcat[0:96, kc * 128:kc * 128 + kn], Q[0:96, cl:n], start=True, stop=not diag)
                    if diag:
                        P.mm(pss[0:kn, cl:n], self.negI, self.masks[:, kc - diag0, cl:n], start=False, stop=True)
                return (grp, pss, pT, pb_, cl)

            def E(item):
                grp, pss, pT, pb_, cl = item
                self.held.discard(pb_)
                full = [kc for kc in grp if min(128, nkeys - kc * 128) == 128]
                if full:
                    P.act(pT[:, cl:len(full) * n], pss[:, cl:len(full) * n], AF.Exp, scale=MLA_SCALE)
                if len(full) < len(grp):
                    kn = nkeys - grp[-1] * 128
                    gj = len(grp) - 1
                    P.act(pT[0:kn, gj * n:(gj + 1) * n], pss[0:kn, gj * n:(gj + 1) * n], AF.Exp, scale=MLA_SCALE)
                for gj, kc in enumerate(grp):
                    kn = min(128, nkeys - kc * 128)
                    first = (kc == kcs[0])
                    last = (kc == kcs[-1])
                    P.mm(pso[:, cl:n], Vh[0:kn, kc, :], pT[0:kn, gj * n + cl:(gj + 1) * n], start=first, stop=last)

            fillers = list(fillers)
            per = -(-len(fillers) // max(1, len(groups)))
            pend = []
            for grp in groups:
                pend.append(S(grp))
                if len(pend) > 2:
                    E(pend.pop(0))
                for _ in range(per):
                    if fillers:
                        fillers.pop(0)()
            while pend:
                E(pend.pop(0))
            run_all(fillers)

        def acc_banks():
            u = st["u"]
            st["u"] += 1
            return self.bank(6 + (u % 2)), None

        Kc = [self.V(B0 + 9216, BF16, 2048), self.V(B0 + 17408, BF16, 2048)]
        Vhs = [self.V(B0 + 13312, BF16, 16, 128), self.V(B0 + 21504, BF16, 16, 128)]
        Qcs = [self.V(B0 + 25600 + 1024 * i, BF16, 512) for i in range(3)]
        P.memset(Vhs[0][:, :, 64:128], 1.0)
        P.memset(Vhs[1][:, :, 0:64], 1.0)
        ptiles = [(c0, n) for (c0, n) in TILES if c0 < TP]
        latp = self.latn[:, :, 0:TP]
        for b_ in range(2):
            P.copy(Kc[b_][64:96, 0:TP], self.krn[64:96, 0:TP], eng="act")
        units = [(h, c0, n) for h in range(12) for (c0, n) in ptiles]
        load_wq(0)
        run_all(head_build(0, latp, TP, Kc[0], Vhs[0]))
        for u0 in range(2):
            h0, c0_, n0_ = units[u0]
            run_all(q_build(h0, c0_, n0_, Qcs[u0 % 3][:, 0:n0_]))
        for ui, (h, c0, n) in enumerate(units):
            fill = []
            if ui + 1 < len(units) and units[ui + 1][0] != h:
                fill += head_build(units[ui + 1][0], latp, TP, Kc[units[ui + 1][0] % 2], Vhs[units[ui + 1][0] % 2])
            if ui + 2 < len(units):
                h2, c2, n2 = units[ui + 2]
                if h2 != units[ui + 1][0] or (ui == 0 and False):
                    load_wq(h2)
                fill += q_build(h2, c2, n2, Qcs[(ui + 2) % 3][:, 0:n2])
            hp = (h % 2) * 64
            qt = c0 // 512
            pso, psd = acc_banks()
            core(Kc[h % 2], Vhs[h % 2], Qcs[ui % 3][:, 0:n], n, list(range(4 * (qt + 1))), TP, 4 * qt, pso, psd, hp, fill)
            dp = 64 - hp
            lg = fz[dp:dp + 64, 0:n]
            P.recip(lg, pso[dp:dp + 64, 0:n])
            P.tt(XA[hp:hp + 64, h // 2, c0:c0 + n], pso[hp:hp + 64, 0:n], lg, ALU.mult)

        SBK = 1024
        latsb = self.V(B0 + 9216, BF16, 2, SBK)
        KcS = [self.V(B0 + 13312, BF16, SBK), self.V(B0 + 17408, BF16, SBK)]
        VhS = [self.V(B0 + 15360, BF16, 8, 128), self.V(B0 + 19456, BF16, 8, 128)]
        Qs = self.V(B0 + 21504, BF16, 12, LS)
        P.memset(VhS[0][:, :, 64:128], 1.0)
        P.memset(VhS[1][:, :, 0:64], 1.0)
        for sq in SEQS[1:]:
            c0, n = sq["c0"], sq["L"]
            for h in range(12):
                if h % 2 == 0:
                    load_wq(h)
                    if h + 1 < 12:
                        load_wq(h + 1)
                run_all(q_build(h, c0, n, Qs[:, h, :]))
            sbs = [("cache", k0_, SBK) for k0_ in range(0, PAST, SBK)] + [("new", c0, n)]
            for sbi, (kind, k0, nkeys) in enumerate(sbs):
                reuse = (kind == "cache" and j == 1)
                spill = (kind == "cache" and j == 0)
                if kind == "new":
                    lat = self.latn[:, :, k0:k0 + nkeys]
                    for b_ in range(2):
                        P.copy(KcS[b_][64:96, 0:nkeys], self.krn[64:96, k0:k0 + nkeys], eng="act")
                else:
                    lat = latsb[:, :, 0:nkeys]
                    if not reuse:
                        for kc2 in range(2):
                            P.dma(latsb[:, kc2, 0:nkeys], self.latc[sq["s"], kc2 * 128:(kc2 + 1) * 128, k0:k0 + nkeys], q="pool")
                    for b_ in range(2):
                        P.dma(KcS[b_][64:96, 0:nkeys], self.krc[sq["s"], :, k0:k0 + nkeys], q="pool")
                nkc = (nkeys + 127) // 128

                def hb_(h):
                    b_ = h % 2
                    vo = b_ * 64
                    if reuse:
                        P.dma(KcS[b_][0:64, 0:nkeys], self.kscr[sq["s"], h, :, k0:k0 + nkeys], q="sp")
                        P.dma(VhS[b_][:, :, vo:vo + 64], self.vscr[sq["s"], h, sbi].rearrange("p (a b) -> p a b", b=64), q="sp")
                        return
                    run_all(head_build(h, lat, nkeys, KcS[b_], VhS[b_]))
                    if spill:
                        P.dma(self.kscr[sq["s"], h, :, k0:k0 + nkeys], KcS[b_][0:64, 0:nkeys], q="sp")
                        P.dma(self.vscr[sq["s"], h, sbi].rearrange("p (a b) -> p a b", b=64), VhS[b_][:, :, vo:vo + 64], q="sp")

                hb_(0)
                for h in range(12):
                    if h + 1 < 12:
                        hb_(h + 1)
                    hp = (h % 2) * 64
                    pso, psd = acc_banks()
                    core(KcS[h % 2], VhS[h % 2], Qs[:, h, :], n, list(range(nkc)), nkeys, None, pso, psd, hp)
                    ah = acc[:, h, :]
                    if sbi == 0:
                        P.copy(ah, pso[:, 0:n])
                    else:
                        P.tt(ah, ah, pso[:, 0:n], ALU.add)
                    if sbi == len(sbs) - 1:
                        dp = 64 - hp
                        rdh = fz[hp:hp + 64, 0:n]
                        P.recip(rdh, acc[dp:dp + 64, h, :])
                        P.tt(XA[hp:hp + 64, h // 2, c0:c0 + n], acc[hp:hp + 64, h, :], rdh, ALU.mult)
        self.rr = list(range(8))

    def ffn(self, l):
        P = self.P
        H, XA, Z = self.H, self.XA, self.Z
        B0 = PH_OFF
        self.rms_fm(lambda k, c0, n: H[:, k, c0:c0 + n], 8, TILES, "norm_ffn", 8 * l,
                    lambda k, c0, n: XA[:, k, c0:c0 + n], 1.0 / D)
        UW = T + 6
        ub = [self.V(B0 + 32768 + 4352 * i, BF16, 2176)[:, 0:UW] for i in range(2)]
        sg = self.V(B0 + 41472, F32, 512)
        Dm = self.V(B0 + 43520, BF16, 6, 128)
        cbufL = self.V(CONVB_OFF, F32, 3, 88)
        ubase = [sq["c0"] + 2 * si for si, sq in enumerate(SEQS)]
        groups = [list(range(0, 8)), list(range(8, 16)), list(range(16, 22))]
        Wi = self.w_ffn_in[l]
        Wo = self.w_ffn_out[l]
        for grp in groups:
            wblk = {}
            for jj, jc in enumerate(grp):
                if jj % 4 == 0:
                    nb_ = min(4, len(grp) - jj) * 128
                    for part in range(2):
                        so = self.slot()
                        wv = self.V(so, BF16, 8, 512)[:, :, 0:nb_]
                        P.dma(wv, Wi[:, part * DFF + jc * 128: part * DFF + jc * 128 + nb_].rearrange("(k p) o -> p k o", p=128), q="pool")
                        wblk[part] = wv
                for part in range(2):
                    wv = wblk[part]
                    col = part * NJ + jc
                    for si, sq in enumerate(SEQS):
                        if sq["s"] is None:
                            P.memset(ub[part][:, ubase[si]:ubase[si] + 2], 0.0)
                        else:
                            P.dma(ub[part][:, ubase[si]:ubase[si] + 2], self.cst[l, sq["s"]][:, 2 * col:2 * col + 2], q="pool")
                    for k3 in range(3):
                        P.ts(Dm[:, part * 3 + k3, :], self.identb, self.sc("conv_w", l * 132 + k3 * 44 + col), None, ALU.mult)
                    for (c0, n) in TILES:
                        ps = self.bank()
                        for k in range(8):
                            P.mm(ps[:, 0:n], wv[:, k, (jj % 4) * 128:(jj % 4 + 1) * 128], XA[:, k, c0:c0 + n], start=(k == 0), stop=(k == 7))
                        if c0 < TP:
                            P.copy(ub[part][:, 2 + c0:2 + c0 + n], ps[:, 0:n], eng="act")
                            if c0 + n == TP:
                                P.copy(cbufL[:, 0, 2 * col:2 * col + 2], ps[:, n - 2:n])
                        else:
                            for si in (1, 2):
                                o = (si - 1) * LS
                                P.copy(ub[part][:, ubase[si] + 2:ubase[si] + 2 + LS], ps[:, o:o + LS], eng="act")
                                P.copy(cbufL[:, si, 2 * col:2 * col + 2], ps[:, o + LS - 2:o + LS])
                for si, sq in enumerate(SEQS):
                    tl = [(c0, n) for (c0, n) in TILES if c0 < TP] if sq["s"] is None else [(sq["c0"], sq["L"])]
                    for (c0, n) in tl:
                        pa = self.bank()
                        pg = self.bank()
                        u0 = ubase[si] + (c0 - sq["c0"])
                        for part, pp in ((0, pa), (1, pg)):
                            for k3 in range(3):
                                P.mm(pp[:, 0:n], Dm[:, part * 3 + k3, :], ub[part][:, u0 + k3:u0 + k3 + n], start=(k3 == 0), stop=(k3 == 2))
                        P.act(sg[:, 0:n], pg[:, 0:n], AF.Silu, bias=self.sc("conv_b", l * 44 + NJ + jc))
                        P.stt(Z[:, jj, c0:c0 + n], pa[:, 0:n], self.sc("conv_b", l * 44 + jc), sg[:, 0:n], ALU.add, ALU.mult)
            ng = len(grp)
            for half in range(2):
                so = self.slot()
                wv = self.V(so, BF16, 8, 512)[:, 0:ng, :]
                P.dma(wv, Wo[grp[0] * 128:(grp[-1] + 1) * 128, half * 512:(half + 1) * 512].rearrange("(k p) o -> p k o", p=128), q="pool")
                for mo in range(4):
                    m = half * 4 + mo
                    for (c0, n) in TILES:
                        ps = self.bank()
                        for jj in range(ng):
                            P.mm(ps[:, 0:n], wv[:, jj, mo * 128:(mo + 1) * 128], Z[:, jj, c0:c0 + n], start=(jj == 0), stop=(jj == ng - 1))
                        P.tt(H[:, m, c0:c0 + n], H[:, m, c0:c0 + n], ps[:, 0:n], ALU.add)
        P.dma(self.conv_o[l].rearrange("s p c -> p s c"), cbufL, q="sp")


_CACHE = {}


def _get_nc():
    if "nc" not in _CACHE:
        nc = bass.Bass("TRN2", target_bir_lowering=False)
        K(nc).build()
        _CACHE["nc"] = nc
    return _CACHE["nc"]


def _pp(v):
    v = np.asarray(v, np.float32)
    return np.ascontiguousarray(v.reshape(-1, 128).T)


def _consts():
    cf = np.zeros((128, NCF), np.float32)
    R = np.zeros((32, 32), np.float32)
    for m in range(16):
        R[16 + m, m] = -1.0
        R[m, 16 + m] = 1.0
    cf[64:96, 0:32] = R
    cb = np.zeros((128, NCB), np.float32)
    cb[:, 0:128] = 1.0
    cb[:, 128:256] = -30000.0 * np.eye(128, dtype=np.float32)
    cb[:, 256:384] = np.eye(128, dtype=np.float32)
    cb[0:64, 384:448] = 1.0
    cb[64:128, 448:512] = 1.0
    cb[0:64, 512:576] = 1.0
    cb[64:96, 576:608] = 1.0
    p = np.arange(128)[:, None]
    c = np.arange(512)[None, :]
    for i in range(4):
        cb[:, 640 + 512 * i: 640 + 512 * (i + 1)] = (((128 * i + p) // 64) > (c // 64)).astype(np.float32)
    return cf, cb


def _rope_tables():
    inv = (1.0 / (np.float32(10000.0) ** (np.arange(0, 32, 2, dtype=np.float32) / np.float32(32)))).astype(np.float32)
    pos = np.concatenate([np.arange(TP), PAST + np.arange(LS), PAST + np.arange(LS)]).astype(np.float32)
    ang = (pos[:, None] * inv[None, :]).astype(np.float32)
    c = np.cos(ang).astype(np.float32).T
    s = np.sin(ang).astype(np.float32).T
    return np.ascontiguousarray(np.stack([np.concatenate([c, c], 0), np.concatenate([s, s], 0)], 0))


def _pack_shared(inp):
    f = lambda k: np.asarray(inp[k], np.float32)
    sc = np.zeros((128, NSC), np.float32)

    def put(name, col, arr):
        arr = np.asarray(arr, np.float32)
        if arr.ndim == 1:
            arr = arr[:, None]
        sc[:arr.shape[0], SCOL[name] + col: SCOL[name] + col + arr.shape[1]] = arr

    for l in range(DEPTH):
        put("norm_mix", 8 * l, _pp(f("norm_mix_g")[l]))
        put("norm_ffn", 8 * l, _pp(f("norm_ffn_g")[l]))
        put("mem_norm", 8 * l, _pp(f("mem_norm_g")[l]))
        put("memq_g", l, np.tile(f("mem_q_norm_g")[l], 2))
        put("memk_g", l, np.tile(f("mem_k_norm_g")[l], 2))
        cw = f("ffn_conv_w")[l]
        for k3 in range(3):
            put("conv_w", l * 132 + k3 * 44, _pp(cw[k3]))
        put("conv_b", l * 44, _pp(f("ffn_conv_b")[l]))
    put("kv_norm", 0, _pp(f("kv_norm_g")))
    put("lat_norm", 0, _pp(f("latent_norm_g")))
    kg = np.zeros(128, np.float32)
    kg[64:96] = f("krope_norm_g")
    put("krope_g", 0, kg)
    kn = np.zeros(128, np.float32)
    kn[0:64] = f("k_nope_norm_g")
    put("knope_g", 0, kn)
    qs = np.zeros(128, np.float32)
    qs[0:64] = 1.0 / 64
    qs[64:96] = 1.0 / 32
    put("qscale", 0, qs)
    for j in range(2):
        qg = np.zeros(128, np.float32)
        qg[0:64] = f("q_nope_norm_g")[j]
        qg[64:96] = f("q_rope_norm_g")[j]
        put("q_g", j, qg)
        put("qlat_g", 6 * j, _pp(f("q_latent_norm_g")[j]))
        put("ssm_d", 6 * j, _pp(f("ssm_d")[j]))
        put("b_glu", 6 * j, _pp(f("b_glu")[j]))
        put("ssm_p", 72 * j, _pp(f("ssm_a_re")[j].reshape(-1)))
        put("ssm_p", 72 * j + 24, _pp(f("ssm_a_im")[j].reshape(-1)))
        put("ssm_p", 72 * j + 48, _pp(np.repeat(f("ssm_log_dt")[j], 64)))
    s5b = np.zeros((NA, 2, 128, 24, 128), np.float32)
    s5c = np.zeros((NA, 2, 128, 24, 128), np.float32)
    for l in range(NA):
        for r, (bk, ck) in enumerate((("ssm_b_re", "ssm_c_re"), ("ssm_b_im", "ssm_c_im"))):
            b = f(bk)[l]
            c = f(ck)[l]
            for i in range(24):
                q = i % 4
                for gg in range(2):
                    g = 2 * i + gg
                    rows = slice(32 * q + 16 * gg, 32 * q + 16 * gg + 16)
                    cols = slice(64 * gg, 64 * gg + 64)
                    s5b[l, r, rows, i, cols] = b[g].T
                    s5c[l, r, cols, i, rows] = c[g].T
    wuq = np.zeros((2, MIX, 12, 128), np.float32)
    wuq[:, :, :, 0:96] = f("w_uq").reshape(2, MIX, 12, 96)
    cf, cb = _consts()
    return dict(scal=sc, consf=cf, consb=cb, rope=_rope_tables(),
                w_mix_in=f("w_mix_in"), w_mix_out=f("w_mix_out"), w_ffn_in=f("w_ffn_in"), w_ffn_out=f("w_ffn_out"),
                w_mem_kv=f("w_mem_kv"), w_glu=f("w_glu"), w_dkv=f("w_dkv"), w_uk=f("w_uk"), w_uv=f("w_uv"),
                w_uq=np.ascontiguousarray(wuq.reshape(2, MIX, 12 * 128)),
                s5b=np.ascontiguousarray(s5b.reshape(NA, 2, 128, 24 * 128)),
                s5c=np.ascontiguousarray(s5c.reshape(NA, 2, 128, 24 * 128)))


def _pack_core(inp, c):
    f = lambda k: np.asarray(inp[k], np.float32)
    s0, s1 = 2 * c, 2 * c + 1
    xT = np.concatenate([f("x_prompt")[c].T, f("x_sample")[s0].T, f("x_sample")[s1].T], axis=1)
    d = dict(xT=np.ascontiguousarray(xT), memT=np.ascontiguousarray(f("mem_prompt")[c].T))
    d["latc"] = np.ascontiguousarray(np.stack([f("cache_mla_latent")[s].T for s in (s0, s1)]))
    d["krc"] = np.ascontiguousarray(np.stack([f("cache_mla_krope")[s].T for s in (s0, s1)]))
    d["cmk"] = np.ascontiguousarray(np.stack([np.stack([f("cache_mem_k")[l, s].reshape(256, 256).T for s in (s0, s1)]) for l in range(DEPTH)]))
    d["cmv"] = np.ascontiguousarray(np.stack([np.stack([f("cache_mem_v")[l, s].reshape(256, 256) for s in (s0, s1)]) for l in range(DEPTH)]))
    d["sst"] = np.ascontiguousarray(np.stack([np.stack([np.stack([_pp(f(k)[l, s].reshape(-1)) for k in ("state_ssm_re", "state_ssm_im")])
                                                        for s in (s0, s1)]) for l in range(NA)]))
    cst = np.zeros((DEPTH, 2, 128, 44, 2), np.float32)
    for l in range(DEPTH):
        for si, s in enumerate((s0, s1)):
            sc_ = f("state_conv")[l, s]
            cst[l, si] = sc_.reshape(2, 44, 128).transpose(2, 1, 0)
    d["cst"] = np.ascontiguousarray(cst.reshape(DEPTH, 2, 128, 88))
    return d


def kernel(**inputs):
    nc = _get_nc()
    shared = _pack_shared(inputs)
    in_maps = []
    for c in range(8):
        d = dict(shared)
        d.update(_pack_core(inputs, c))
        in_maps.append(d)
    res = run_bass_kernel_spmd(nc, in_maps, core_ids=list(range(8)))
    R = res.results
    B, DB = 8, 16
    y_p = np.stack([R[c]["yT"][:, :TP].T for c in range(B)])
    y_s = np.zeros((DB, LS, D), np.float32)
    lat_p = np.stack([R[c]["lat_o"][:, :TP].T for c in range(B)])
    kr_p = np.stack([R[c]["kr_o"][:, :TP].T for c in range(B)])
    lat_s = np.zeros((DB, LS, 256), np.float32)
    kr_s = np.zeros((DB, LS, 32), np.float32)
    memk = np.zeros((DEPTH, B, NMEM, 4, 64), np.float32)
    memv = np.zeros((DEPTH, B, NMEM, 4, 64), np.float32)
    ssm_p = np.zeros((2, NA, B, 48, 64), np.float32)
    ssm_s = np.zeros((2, NA, DB, 48, 64), np.float32)
    conv_p = np.zeros((DEPTH, B, 2, 2 * DFF), np.float32)
    conv_s = np.zeros((DEPTH, DB, 2, 2 * DFF), np.float32)
    for c in range(B):
        r = R[c]
        for l in range(DEPTH):
            memk[l, c] = r["memk_o"][l].T.reshape(NMEM, 4, 64)
            memv[l, c] = r["memv_o"][l].reshape(NMEM, 4, 64)
            cv = r["conv_o"][l].reshape(3, 128, 44, 2)
            for si in range(3):
                arr = cv[si].transpose(2, 1, 0).reshape(2, 2 * DFF)
                if si == 0:
                    conv_p[l, c] = arr
                else:
                    conv_s[l, 2 * c + si - 1] = arr
        for l in range(NA):
            for si in range(3):
                for ri in range(2):
                    arr = r["ssm_o"][l, si, ri].T.reshape(48, 64)
                    if si == 0:
                        ssm_p[ri, l, c] = arr
                    else:
                        ssm_s[ri, l, 2 * c + si - 1] = arr
        for si in range(2):
            cs = slice(TP + si * LS, TP + (si + 1) * LS)
            y_s[2 * c + si] = r["yT"][:, cs].T
            lat_s[2 * c + si] = r["lat_o"][:, cs].T
            kr_s[2 * c + si] = r["kr_o"][:, cs].T
    return (y_p, y_s, memk, memv, lat_p, kr_p, ssm_p[0], ssm_p[1], conv_p, lat_s, kr_s, ssm_s[0], ssm_s[1], conv_s)
```

```python
import math
import numpy as np
import concourse.bass as bass
import concourse.mybir as mybir
from concourse.bass_utils import run_bass_kernel_spmd

F32 = mybir.dt.float32
BF16 = mybir.dt.bfloat16
I32 = mybir.dt.int32
AF = mybir.ActivationFunctionType
ALU = mybir.AluOpType

D = 1024
DEPTH = 4
NA = 2
TP = 2048
LS = 32
T = TP + 2 * LS
PAST = 4096
DFF = 2816
NJ = DFF // 128
MIX = 768
NMEM = 256
EPS = 1e-6
MLA_SCALE = 96 ** -0.5
MEM_SCALE = 64 ** -0.5
TILES = [(0, 512), (512, 512), (1024, 512), (1536, 512), (2048, 64)]
SEQS = [dict(c0=0, L=TP, past=0, s=None), dict(c0=TP, L=LS, past=PAST, s=0), dict(c0=TP + LS, L=LS, past=PAST, s=1)]

H_OFF = 0
XA_OFF = 67584
Z_OFF = 101376
PH_OFF = 135168
PH_SIZE = 46080
PS_OFF = PH_OFF + PH_SIZE
SC_OFF = PS_OFF
CF_OFF = SC_OFF + 4096
CB_OFF = CF_OFF + 128
LATN_OFF = CB_OFF + 5376
KRN_OFF = LATN_OFF + 2 * T * 2
MKT_OFF = KRN_OFF + T * 2
MV_OFF = MKT_OFF + 4096
CONVB_OFF = MV_OFF + 4096
ARENA_BYTES = CONVB_OFF + 1056
assert ARENA_BYTES <= 212800, ARENA_BYTES
NCF = 32
NCB = 2688

ND_SEMS = 24

SCOL = {}
_n = 0
for _name, _w in [("norm_mix", 32), ("norm_ffn", 32), ("mem_norm", 32), ("kv_norm", 8), ("lat_norm", 2),
                  ("krope_g", 1), ("memq_g", 4), ("memk_g", 4), ("q_g", 2), ("knope_g", 1), ("qlat_g", 12),
                  ("ssm_d", 12), ("b_glu", 12), ("ssm_p", 144), ("qscale", 1), ("conv_w", 528), ("conv_b", 176)]:
    SCOL[_name] = _n
    _n += _w
NSC = 1024
assert _n <= NSC


def _foot(ap):
    name = ap.name
    off = int(ap.offset)
    dims = list(ap.ap)
    es = mybir.dt.size(ap.dtype)
    if str(ap.space) == "DRAM":
        if not name.startswith("scr_"):
            return None
        ext = sum((c - 1) * abs(st_) for st_, c in dims) + 1
        return (name, 0, 1, off * es, (off + ext) * es)
    ps, pc = dims[0]
    if ps == 0:
        ps = 1 << 40
    p0 = off // ps
    lo = off % ps
    ext = sum((c - 1) * abs(s) for s, c in dims[1:]) + 1
    if str(ap.space) == "PSUM":
        return ("PSUM", (p0 // 32) * 32, ((p0 + pc + 31) // 32) * 32, (lo * es // 2048) * 2048,
                (((lo + ext) * es + 2047) // 2048) * 2048)
    return (name, p0, p0 + pc, lo * es, (lo + ext) * es)


class Op:
    __slots__ = ("eng", "fn", "deps", "signal", "sigval", "dma", "dsem", "dval", "idx")

    def __init__(self, eng, fn, dma):
        self.eng = eng
        self.fn = fn
        self.deps = []
        self.signal = False
        self.sigval = 0
        self.dma = dma
        self.dsem = -1
        self.dval = 0
        self.idx = 0


PAGE = 2048
SAME_ENG_GAP = 10 ** 9


class Prog:
    def __init__(self, nc):
        self.nc = nc
        self.ops = []
        self.pages = {}
        self.ndma = 0
        self.nq = {"sp": 0, "pool": 0}
        self.ecnt = {}
        self.eseq = {}

    def _recs(self, f):
        seen = set()
        out = []
        for pg in range(f[3] // PAGE, (f[4] - 1) // PAGE + 1):
            lst = self.pages.get((f[0], pg))
            if not lst:
                continue
            alive = [r for r in lst if r[6]]
            if len(alive) != len(lst):
                lst[:] = alive
            for r in alive:
                if id(r) not in seen and r[1] < f[2] and f[1] < r[2] and r[3] < f[4] and f[3] < r[4]:
                    seen.add(id(r))
                    out.append(r)
        return out

    def _put(self, rec, f):
        for pg in range(f[3] // PAGE, (f[4] - 1) // PAGE + 1):
            self.pages.setdefault((f[0], pg), []).append(rec)

    def add(self, eng, fn, reads=(), writes=(), dma=False):
        op = Op(eng, fn, dma)
        op.idx = len(self.ops)
        if dma:
            half = ND_SEMS // 2
            i = self.nq[eng]
            self.nq[eng] += 1
            op.dsem = (i % half) + (0 if eng == "sp" else half)
            op.dval = (i // half + 1) * 16
            self.ndma += 1
        rf = [x for x in (_foot(a) for a in reads) if x is not None]
        wf = [x for x in (_foot(a) for a in writes) if x is not None]
        deps = {}

        myseq = self.ecnt.get(eng, 0)
        self.ecnt[eng] = myseq + 1
        self.eseq[op.idx] = myseq

        def need(o, raw, psum=False):
            if (not o.dma) and (not dma) and o.eng == eng:
                if eng == "pe" or not raw or psum:
                    return
                if myseq - self.eseq[o.idx] >= SAME_ENG_GAP:
                    return
            deps[o.idx] = o

        prf = [f for f in rf if f[0] == "PSUM"]
        rf = [f for f in rf if f[0] != "PSUM"]
        for f in prf:
            for r in self._recs(f):
                need(r[0], True, True)
        for f in rf:
            for r in self._recs(f):
                if r[5]:
                    need(r[0], True)
        for f in wf:
            for r in self._recs(f):
                need(r[0], False, f[0] == "PSUM")
                if f[1] <= r[1] and r[2] <= f[2] and f[3] <= r[3] and r[4] <= f[4]:
                    r[6] = False
        for f in prf:
            for r in self._recs(f):
                if (not r[0].dma) and r[0].eng == eng and r[1] == f[1] and r[2] == f[2] and r[3] == f[3] and r[4] == f[4]:
                    r[6] = False
            self._put([op, f[1], f[2], f[3], f[4], True, True], f)
        best = {}
        dl = []
        for o in deps.values():
            if o.dma:
                dl.append(o)
            elif o.eng not in best or best[o.eng].idx < o.idx:
                best[o.eng] = o
        op.deps = dl + list(best.values())
        for o in op.deps:
            o.signal = True
        for f in wf:
            self._put([op, f[1], f[2], f[3], f[4], True, True], f)
        for f in rf:
            if not dma:
                for r in self._recs(f):
                    if (not r[5]) and (not r[0].dma) and r[0].eng == eng and r[1] == f[1] and r[2] == f[2] \
                            and r[3] == f[3] and r[4] == f[4]:
                        r[6] = False
            self._put([op, f[1], f[2], f[3], f[4], False, True], f)
        self.ops.append(op)
        return op

    def emit(self):
        nc = self.nc
        engs = ["pe", "act", "dve", "pool", "sp"]
        cnt = {e: 0 for e in engs}
        last = {}
        for op in self.ops:
            if not op.dma:
                last[op.eng] = op
        for op in last.values():
            op.signal = True
        for op in self.ops:
            if op.dma:
                op.signal = True
            elif op.signal:
                cnt[op.eng] += 1
                op.sigval = cnt[op.eng]
        per = {e: [o for o in self.ops if o.eng == e] for e in engs}
        ctx = []
        esem = {}
        for e in engs:
            c = nc.semaphore("s_" + e)
            esem[e] = c.__enter__()
            ctx.append(c)
        dsem = []
        for i in range(ND_SEMS):
            c = nc.semaphore("d_%d" % i)
            dsem.append(c.__enter__())
            ctx.append(c)
        dfinal = [0] * ND_SEMS
        for op in self.ops:
            if op.dma:
                dfinal[op.dsem] = max(dfinal[op.dsem], op.dval)

        def run(e, eng):
            known = {}
            for op in per[e]:
                waits = {}
                for d in op.deps:
                    if d.dma:
                        k = ("d", d.dsem)
                        v = d.dval
                    else:
                        k = ("e", d.eng)
                        v = d.sigval
                    if known.get(k, 0) < v:
                        waits[k] = max(waits.get(k, 0), v)
                if op.dma and op.dval > 16:
                    k = ("d", op.dsem)
                    v = op.dval - 16
                    if known.get(k, 0) < v:
                        waits[k] = max(waits.get(k, 0), v)
                for k, v in waits.items():
                    s = dsem[k[1]] if k[0] == "d" else esem[k[1]]
                    eng.wait_ge(s, v)
                    known[k] = v
                ins = op.fn(eng)
                if op.dma:
                    ins.then_inc(dsem[op.dsem], 16)
                elif op.signal:
                    ins.then_inc(esem[e], 1)
            if e == "sp":
                for i in range(ND_SEMS):
                    if dfinal[i] > known.get(("d", i), 0):
                        eng.wait_ge(dsem[i], dfinal[i])
                for e2 in engs:
                    if cnt[e2] > known.get(("e", e2), 0):
                        eng.wait_ge(esem[e2], cnt[e2])
                for s_ in list(esem.values()) + dsem:
                    eng.sem_clear(s_)

        with nc.Block() as block:
            @block.tensor
            def _(eng):
                run("pe", eng)

            @block.scalar
            def _(eng):
                run("act", eng)

            @block.vector
            def _(eng):
                run("dve", eng)

            @block.gpsimd
            def _(eng):
                run("pool", eng)

            @block.sync
            def _(eng):
                run("sp", eng)
        for c in reversed(ctx):
            c.__exit__(None, None, None)

    def dma(self, out, in_, q="sp"):
        return self.add(q, lambda eng: eng.dma_start(out=out, in_=in_), [in_], [out], dma=True)

    def mm(self, out, lhsT, rhs, start=True, stop=True):
        return self.add("pe", lambda eng: eng.matmul(out, lhsT, rhs, start=start, stop=stop), [lhsT, rhs], [out])

    def act(self, out, in_, func, bias=None, scale=1.0):
        reads = [in_]
        kw = {}
        if bias is not None:
            kw["bias"] = bias
            if not isinstance(bias, (int, float)):
                reads.append(bias)
        if not isinstance(scale, (int, float)):
            reads.append(scale)
        return self.add("act", lambda e: e.activation(out, in_, func, scale=scale, **kw), reads, [out])

    def tt(self, out, in0, in1, op, eng="dve"):
        return self.add(eng, lambda e: e.tensor_tensor(out, in0, in1, op), [in0, in1], [out])

    def ts(self, out, in0, s1, s2, op0, op1=None, eng="dve"):
        reads = [in0] + [s for s in (s1, s2) if s is not None and not isinstance(s, (int, float))]
        kw = {}
        if op1 is not None:
            kw["op1"] = op1
        return self.add(eng, lambda e: e.tensor_scalar(out, in0, s1, s2, op0, **kw), reads, [out])

    def stt(self, out, in0, scalar, in1, op0, op1):
        reads = [in0, in1] + ([] if isinstance(scalar, (int, float)) else [scalar])
        return self.add("dve", lambda e: e.scalar_tensor_tensor(out, in0, scalar, in1, op0, op1), reads, [out])

    def copy(self, out, in_, eng="dve"):
        if eng == "act":
            return self.act(out, in_, AF.Copy)
        return self.add(eng, lambda e: e.tensor_copy(out, in_), [in_], [out])

    def memset(self, out, val, eng="dve"):
        return self.add(eng, lambda e: e.memset(out, val), [], [out])

    def recip(self, out, in_):
        return self.add("dve", lambda e: e.reciprocal(out, in_), [in_], [out])


class K:
    def __init__(self, nc):
        self.nc = nc
        self.P = Prog(nc)
        self.ring_i = 0
        self.bank_i = 0
        self.rr = list(range(8))
        self.held = set()
        self.last_bank = 0

    def V(self, off, dt, *shape, p0=0, p1=128):
        es = mybir.dt.size(dt)
        assert off % es == 0
        n = 1
        for s in shape:
            n *= s
        base = self.A if dt == F32 else self.A.bitcast(dt)
        v = base[p0:p1, off // es: off // es + n]
        if len(shape) == 2:
            v = v.rearrange("p (a b) -> p a b", a=shape[0])
        elif len(shape) == 3:
            v = v.rearrange("p (a b c) -> p a b c", a=shape[0], b=shape[1])
        return v

    def sc(self, name, col=0, p0=0, p1=128):
        c = SCOL[name] + col
        return self.V(SC_OFF + 4 * c, F32, 1, p0=p0, p1=p1)

    def bank(self, b=None):
        if b is None:
            while True:
                b = self.rr[self.bank_i % len(self.rr)]
                self.bank_i += 1
                if b not in self.held:
                    break
            self.last_bank = b
        return self.PS[:, b, :]

    def slot(self):
        i = self.ring_i % 4
        self.ring_i += 1
        return PH_OFF + 8192 * i

    def rms_stats(self, srcs, ones_ap, n, scale, sq_offs, rs_off, rstd_off, p0=0, p1=128, K_=None):
        P = self.P
        ps = self.bank()
        for i, s in enumerate(srcs):
            sq = self.V(sq_offs[i % len(sq_offs)], BF16, 512, p0=p0, p1=p1)[:, 0:n]
            P.act(sq, s, AF.Square)
            P.mm(ps[p0:p1, 0:n], ones_ap, sq, start=(i == 0), stop=(i == len(srcs) - 1))
        rs = self.V(rs_off, F32, 512, p0=p0, p1=p1)[:, 0:n]
        P.act(rs, ps[p0:p1, 0:n], AF.Ln, bias=EPS, scale=scale)
        rstd = self.V(rstd_off, F32, 512, p0=p0, p1=p1)[:, 0:n]
        P.act(rstd, rs, AF.Exp, scale=-0.5)
        return rstd

    def rms_fm(self, src, nk, tiles, gname, gcol0, dst, inv_n):
        P = self.P
        L = PH_OFF + 32768
        rss = [self.V(L + o, F32, 512) for o in (4096, 6144, 10240)]
        box = {}

        def A(t):
            c0, n = tiles[t]
            ps = self.bank()
            box[t] = ps
            for k in range(nk):
                sq = self.V(L + 2048 * (k % 2), BF16, 512)[:, 0:n]
                P.act(sq, src(k, c0, n), AF.Square)
                P.mm(ps[:, 0:n], self.ones, sq, start=(k == 0), stop=(k == nk - 1))

        def B_(t):
            c0, n = tiles[t]
            P.act(rss[t % 3][:, 0:n], box[t][:, 0:n], AF.Ln, bias=EPS, scale=inv_n)

        def C(t):
            c0, n = tiles[t]
            r = rss[t % 3][:, 0:n]
            P.act(r, r, AF.Exp, scale=-0.5)
            for k in range(nk):
                P.stt(dst(k, c0, n), src(k, c0, n), self.sc(gname, gcol0 + k), r, ALU.mult, ALU.mult)
        nt = len(tiles)
        for step in range(nt + 2):
            if step < nt:
                A(step)
            if 0 <= step - 1 < nt:
                B_(step - 1)
            if 0 <= step - 2 < nt:
                C(step - 2)

    def linear(self, W, nk, rhs, mchunks, tiles, evac):
        P = self.P
        blocks = []
        cur = []
        for mc in mchunks:
            if cur and (mc[0] + mc[1] - cur[0][0] > 512):
                blocks.append(cur)
                cur = []
            cur.append(mc)
        if cur:
            blocks.append(cur)
        mi = 0
        for blk in blocks:
            lo = blk[0][0]
            hi = blk[-1][0] + blk[-1][1]
            so = self.slot()
            wv = self.V(so, BF16, nk, hi - lo)
            P.dma(wv, W[:, lo:hi].rearrange("(k p) o -> p k o", p=128), q="pool")
            for (m0, mn) in blk:
                for (c0, n) in tiles:
                    ps = self.bank()
                    for k in range(nk):
                        P.mm(ps[0:mn, 0:n], wv[:, k, m0 - lo:m0 - lo + mn], rhs(k, c0, n), start=(k == 0), stop=(k == nk - 1))
                    evac(mi, c0, n, ps[0:mn, 0:n])
                mi += 1

    def build(self):
        nc = self.nc
        P = self.P

        def din(name, shape):
            return nc.dram_tensor(name, list(shape), F32, kind="ExternalInput").ap()

        def dout(name, shape):
            return nc.dram_tensor(name, list(shape), F32, kind="ExternalOutput").ap()

        self.xT = din("xT", [D, T])
        self.memT = din("memT", [D, NMEM])
        self.latc = din("latc", [2, 256, PAST])
        self.krc = din("krc", [2, 32, PAST])
        self.cmk = din("cmk", [DEPTH, 2, 256, 256])
        self.cmv = din("cmv", [DEPTH, 2, 256, 256])
        self.sst = din("sst", [NA, 2, 2, 128, 24])
        self.cst = din("cst", [DEPTH, 2, 128, 88])
        self.scal = din("scal", [128, NSC])
        self.consf = din("consf", [128, NCF])
        self.consb = din("consb", [128, NCB])
        self.rope = din("rope", [2, 32, T])
        self.w_mix_in = din("w_mix_in", [DEPTH, D, D])
        self.w_mix_out = din("w_mix_out", [DEPTH, D, D])
        self.w_ffn_in = din("w_ffn_in", [DEPTH, D, 2 * DFF])
        self.w_ffn_out = din("w_ffn_out", [DEPTH, DFF, D])
        self.w_mem_kv = din("w_mem_kv", [DEPTH, D, 512])
        self.w_glu = din("w_glu", [NA, MIX, MIX])
        self.w_dkv = din("w_dkv", [D, 288])
        self.w_uk = din("w_uk", [256, MIX])
        self.w_uv = din("w_uv", [256, MIX])
        self.w_uq = din("w_uq", [2, MIX, 12 * 128])
        self.s5b = din("s5b", [NA, 2, 128, 24 * 128])
        self.s5c = din("s5c", [NA, 2, 128, 24 * 128])
        self.kscr = nc.dram_tensor("scr_k", [2, 12, 64, PAST], BF16, kind="Internal").ap()
        self.vscr = nc.dram_tensor("scr_v", [2, 12, PAST // 1024, 128, 8 * 64], BF16, kind="Internal").ap()
        self.yT = dout("yT", [D, T])
        self.memk_o = dout("memk_o", [DEPTH, 256, NMEM])
        self.memv_o = dout("memv_o", [DEPTH, NMEM, 256])
        self.lat_o = dout("lat_o", [256, T])
        self.kr_o = dout("kr_o", [32, T])
        self.ssm_o = dout("ssm_o", [NA, 3, 2, 128, 24])
        self.conv_o = dout("conv_o", [DEPTH, 3, 128, 88])

        with nc.sbuf_tensor("A", [128, ARENA_BYTES // 4], F32) as A_, \
                nc.psum_tensor("PS", [128, 8, 512], F32) as PS_, \
                nc.allow_low_precision("bf16 matmul operands, fp32 accumulation"):
            self.A = A_[:]
            self.PS = PS_
            self.H = self.V(H_OFF, F32, 8, T)
            self.XA = self.V(XA_OFF, BF16, 8, T)
            self.Z = self.V(Z_OFF, BF16, 8, T)
            cf = self.V(CF_OFF, F32, NCF)
            self.rrot = cf[:, 0:32]
            cb = self.V(CB_OFF, BF16, NCB)
            self.ones = cb[:, 0:128]
            self.onesb = cb[:, 0:64]
            self.negI = cb[:, 128:256]
            self.identb = cb[:, 256:384]
            self.bo64 = cb[:, 384:512]
            self.boq = cb[:, 512:640]
            self.masks = cb[:, 640:2688].rearrange("p (a b) -> p a b", a=4)
            self.latn = self.V(LATN_OFF, BF16, 2, T)
            self.krn = self.V(KRN_OFF, BF16, T)
            self.mkt = self.V(MKT_OFF, BF16, 4, 2, 256)
            self.mv = self.V(MV_OFF, BF16, 4, 2, 256)

            P.dma(self.V(SC_OFF, F32, NSC), self.scal, q="sp")
            P.dma(cf, self.consf, q="sp")
            P.dma(cb, self.consb, q="pool")
            xv = self.xT.rearrange("(k p) t -> p k t", p=128)
            for k in range(8):
                P.dma(self.H[:, k, :], xv[:, k, :], q="sp")

            import os
            self.dbg = int(os.environ.get("KDBG", "99"))
            self.mem_kv()
            for l in range(DEPTH):
                if self.dbg >= 2 and (self.dbg >= 6 or l == 0 or (self.dbg == 5 and l < 2)):
                    self.layer(l)
            yv = self.yT.rearrange("(k p) t -> p k t", p=128)
            for k in range(8):
                P.dma(yv[:, k, :], self.H[:, k, :], q="sp")
            P.emit()

    def mem_kv(self):
        P = self.P
        memf = self.V(XA_OFF, F32, 8, NMEM)
        mn = self.V(Z_OFF, BF16, 8, NMEM)
        P.dma(memf, self.memT.rearrange("(k p) t -> p k t", p=128), q="sp")
        L = PH_OFF + 32768
        stage = self.V(L + 8192, F32, 512)
        for l in range(DEPTH):
            self.rms_fm(lambda k, c0, n: memf[:, k, c0:c0 + n], 8, [(0, NMEM)], "mem_norm", 8 * l,
                        lambda k, c0, n: mn[:, k, c0:c0 + n], 1.0 / D)
            so = self.slot()
            wv = self.V(so, BF16, 8, 512)
            P.dma(wv, self.w_mem_kv[l].rearrange("(k p) o -> p k o", p=128), q="pool")
            for mc in range(2):
                ps = self.bank()
                for k in range(8):
                    P.mm(ps[:, 0:NMEM], wv[:, k, mc * 128:(mc + 1) * 128], mn[:, k, :], start=(k == 0), stop=(k == 7))
                rstd = self.rms_stats([ps[:, 0:NMEM]], self.bo64, NMEM, 1.0 / 64, [L], L + 4096, L + 6144)
                P.stt(stage[:, 0:NMEM], ps[:, 0:NMEM], self.sc("memk_g", l), rstd, ALU.mult, ALU.mult)
                P.dma(self.memk_o[l, mc * 128:(mc + 1) * 128, :], stage[:, 0:NMEM], q="sp")
                P.copy(self.mkt[:, l, mc, :], stage[:, 0:NMEM], eng="act")
            for tt in range(2):
                ps = self.bank()
                for k in range(8):
                    P.mm(ps[:, 0:256], mn[:, k, tt * 128:(tt + 1) * 128], wv[:, k, 256:512], start=(k == 0), stop=(k == 7))
                P.copy(stage[:, 256:512], ps[:, 0:256], eng="act")
                P.dma(self.memv_o[l, tt * 128:(tt + 1) * 128, :], stage[:, 256:512], q="sp")
                P.copy(self.mv[:, l, tt, :], stage[:, 256:512])

    def layer(self, l):
        P = self.P
        H, XA, Z = self.H, self.XA, self.Z
        if l == NA:
            self.ckv()
        self.rms_fm(lambda k, c0, n: H[:, k, c0:c0 + n], 8, TILES, "norm_mix", 8 * l,
                    lambda k, c0, n: XA[:, k, c0:c0 + n], 1.0 / D)
        self.linear(self.w_mix_in[l], 8, lambda k, c0, n: XA[:, k, c0:c0 + n], [(m * 128, 128) for m in range(8)], TILES,
                    lambda mi, c0, n, ps: P.copy(Z[:, mi, c0:c0 + n], ps, eng="act"))
        self.mem_attend_all(l)
        if self.dbg < 3:
            return
        if l < NA:
            self.s5(l)
        else:
            self.mla(l - NA)
        if self.dbg < 4:
            return
        self.linear(self.w_mix_out[l], 8, lambda k, c0, n: XA[:, k, c0:c0 + n], [(m * 128, 128) for m in range(8)], TILES,
                    lambda mi, c0, n, ps: P.tt(H[:, mi, c0:c0 + n], H[:, mi, c0:c0 + n], ps, ALU.add))
        self.ffn(l)

    def mem_attend_all(self, l):
        P = self.P
        Z, XA = self.Z, self.XA
        B0 = PH_OFF
        qm = self.V(B0 + 4096, BF16, 2, T)
        sqs = [self.V(B0 + 12544 + 1024 * i, BF16, 512) for i in range(3)]
        rss = [self.V(B0 + 15616 + 2048 * i, F32, 512) for i in range(3)]
        units = []
        for (c0, n) in TILES:
            for mc in range(2):
                units.append((c0, n, mc))
        box = {}

        def A(u):
            c0, n, mc = units[u]
            ps = self.bank()
            box[u] = ps
            sq = sqs[u % 3][:, 0:n]
            P.act(sq, Z[:, 6 + mc, c0:c0 + n], AF.Square)
            P.mm(ps[:, 0:n], self.bo64, sq)

        def Bq(u):
            c0, n, mc = units[u]
            P.act(rss[u % 3][:, 0:n], box[u][:, 0:n], AF.Ln, bias=EPS, scale=1.0 / 64)

        def C(u):
            c0, n, mc = units[u]
            r = rss[u % 3][:, 0:n]
            P.act(r, r, AF.Exp, scale=-0.5)
            P.stt(qm[:, mc, c0:c0 + n], Z[:, 6 + mc, c0:c0 + n], self.sc("memq_g", l), r, ALU.mult, ALU.mult)
        nu = len(units)
        for step in range(nu + 2):
            if step < nu:
                A(step)
            if 0 <= step - 1 < nu:
                Bq(step - 1)
            if 0 <= step - 2 < nu:
                C(step - 2)
        KtS = self.V(B0, BF16, 2, 256)
        VmS = self.V(B0 + 1024, BF16, 2, 256)
        pi = 0
        for sq_ in SEQS:
            if sq_["s"] is None:
                Kt = self.mkt[:, l]
                Vm = self.mv[:, l]
                tiles = [(c0, n) for (c0, n) in TILES if c0 < TP]
            else:
                Kt, Vm = KtS, VmS
                P.dma(Kt, self.cmk[l, sq_["s"]].rearrange("(k p) t -> p k t", p=128), q="pool")
                P.dma(Vm, self.cmv[l, sq_["s"]].rearrange("(k p) t -> p k t", p=128), q="pool")
                tiles = [(sq_["c0"], sq_["L"])]
            for (c0, n) in tiles:
                for mc in range(2):
                    q_ = qm[:, mc, c0:c0 + n]
                    pso = self.bank()
                    psd = self.bank()
                    sc_ = []
                    for hh in range(2):
                        r0, r1 = hh * 64, hh * 64 + 64
                        for kc in range(2):
                            pss = self.bank()
                            P.mm(pss[:, 0:n], Kt[r0:r1, mc, kc * 128:(kc + 1) * 128], q_[r0:r1, :])
                            sc_.append((hh, kc, pss))
                    for (hh, kc, pss) in sc_:
                        r0, r1 = hh * 64, hh * 64 + 64
                        hd = 2 * mc + hh
                        pT = self.V(B0 + 21760 + 1024 * (pi % 8), BF16, 512)[:, 0:n]
                        pi += 1
                        P.act(pT, pss[:, 0:n], AF.Exp, scale=MEM_SCALE)
                        P.mm(pso[r0:r1, 0:n], Vm[:, kc, hd * 64:(hd + 1) * 64], pT, start=(kc == 0), stop=(kc == 1))
                        P.mm(psd[r0:r1, 0:n], self.onesb, pT, start=(kc == 0), stop=(kc == 1))
                    rd = self.V(B0 + 29952 + 2048 * ((pi // 4) % 2), F32, 512)[:, 0:n]
                    P.recip(rd, psd[:, 0:n])
                    P.tt(XA[:, 6 + mc, c0:c0 + n], pso[:, 0:n], rd, ALU.mult)

    def s5(self, l):
        P = self.P
        Z, XA = self.Z, self.XA
        B0 = PH_OFF
        NU = 10
        pb = B0 + 34816

        def pv(i):
            return self.V(pb + 96 * i, F32, 24)
        a_re = self.V(SC_OFF + 4 * (SCOL["ssm_p"] + 72 * l), F32, 24)
        a_im = self.V(SC_OFF + 4 * (SCOL["ssm_p"] + 72 * l + 24), F32, 24)
        ldt = self.V(SC_OFF + 4 * (SCOL["ssm_p"] + 72 * l + 48), F32, 24)
        dt, mag, v, vi, fr, t0, t1p, den, xre, f_re, f_im, lam_re, lam_im = [pv(i) for i in range(13)]
        vI = self.V(pb + 96 * 13, I32, 24)
        Uc = self.V(pb + 1344, F32, NU, 24)
        Us = self.V(pb + 2304, F32, NU, 24)
        cin = self.V(pb + 3264, F32, 4)
        P.act(dt, ldt, AF.Exp)
        P.tt(t0, a_re, dt, ALU.mult)
        P.act(mag, t0, AF.Exp)
        P.tt(t1p, a_im, dt, ALU.mult)
        twopi = 2 * math.pi
        for which in range(2):
            P.ts(v, t1p, 1.0 / twopi, 0.25 if which == 0 else 0.0, ALU.mult, ALU.add)
            P.copy(vI, v)
            P.copy(vi, vI)
            P.tt(fr, v, vi, ALU.subtract)
            P.act(Uc[:, 0, :] if which == 0 else Us[:, 0, :], fr, AF.Sin, scale=6.2831845)
        P.tt(lam_re, Uc[:, 0, :], mag, ALU.mult)
        P.tt(lam_im, Us[:, 0, :], mag, ALU.mult)
        P.tt(t0, a_re, a_re, ALU.mult)
        P.tt(t1p, a_im, a_im, ALU.mult)
        P.tt(den, t0, t1p, ALU.add)
        P.recip(den, den)
        P.ts(xre, lam_re, -1.0, None, ALU.add)
        P.tt(t0, xre, a_re, ALU.mult)
        P.tt(t1p, lam_im, a_im, ALU.mult)
        P.tt(t0, t0, t1p, ALU.add)
        P.tt(f_re, t0, den, ALU.mult)
        P.tt(t0, lam_im, a_re, ALU.mult)
        P.tt(t1p, xre, a_im, ALU.mult)
        P.tt(t0, t0, t1p, ALU.subtract)
        P.tt(f_im, t0, den, ALU.mult)
        for k in range(NU - 1):
            P.tt(t0, Uc[:, k, :], Uc[:, k, :], ALU.mult)
            P.tt(t1p, Us[:, k, :], Us[:, k, :], ALU.mult)
            P.tt(Uc[:, k + 1, :], t0, t1p, ALU.subtract)
            P.tt(t0, Uc[:, k, :], Us[:, k, :], ALU.mult)
            P.ts(Us[:, k + 1, :], t0, 2.0, None, ALU.mult)

        cosT = self.V(B0 + 8192, F32, 512)
        sinT = self.V(B0 + 10240, F32, 512)
        Freb = self.V(B0 + 12288, BF16, 512)
        Fimb = self.V(B0 + 13312, BF16, 512)
        cosb = self.V(B0 + 14336, BF16, 512)
        sinb = self.V(B0 + 15360, BF16, 512)
        nsinb = self.V(B0 + 39424, BF16, 512)
        t1 = self.V(B0 + 16384, F32, 512)
        t2 = self.V(B0 + 18432, F32, 512)
        xr = self.V(B0 + 20480, BF16, 512)
        xi = self.V(B0 + 21504, BF16, 512)
        ab = self.V(B0 + 22528, BF16, 512)
        bb = self.V(B0 + 23552, BF16, 512)
        cb_ = self.V(B0 + 44544, BF16, 512)
        rr_ = self.V(B0 + 24576, F32, 512)
        ri_ = self.V(B0 + 26624, F32, 512)
        rawr = self.V(B0 + 40448, BF16, 512)
        rawi = self.V(B0 + 41472, BF16, 512)
        rrbs = [self.V(B0 + 42496, BF16, 512), self.V(B0 + 4096, BF16, 512)]
        ribs = [self.V(B0 + 43520, BF16, 512), self.V(B0 + 5120, BF16, 512)]
        ytmp = self.V(B0 + 32768, F32, 512)
        cars = [self.V(pb + 3296 + 192 * si, F32, 2, 24) for si in range(3)]
        fins = [self.V(pb + 3872 + 192 * si, F32, 2, 24) for si in range(3)]
        for si, sq in enumerate(SEQS):
            if sq["s"] is not None:
                P.dma(cars[si], self.sst[l, sq["s"]].rearrange("r p i -> p r i"), q="sp")
        blocks = []
        for si, sq in enumerate(SEQS):
            TB = min(512, sq["L"])
            for tb in range(sq["L"] // TB):
                blocks.append((si, tb, TB, sq["c0"] + tb * TB))
        assert len(blocks) <= 6
        hb_i = 0
        for cb in range(6):
            wo = B0
            wB = self.V(wo, BF16, 2, 4, 128)
            wC = self.V(wo + 2048, BF16, 2, 4, 128)
            for r in range(2):
                P.dma(wB[:, r], self.s5b[l, r][:, cb * 512:(cb + 1) * 512].rearrange("p (q m) -> p q m", q=4), q="pool")
                P.dma(wC[:, r], self.s5c[l, r][:, cb * 512:(cb + 1) * 512].rearrange("p (q m) -> p q m", q=4), q="pool")
            psy = [self.bank(2 + bi) for bi in range(len(blocks))]
            for q in range(4):
                i = 4 * cb + q
                P.memset(cosT[:, 0:1], 1.0)
                P.memset(sinT[:, 0:1], 0.0)
                for k in range(9):
                    d = 1 << k
                    uc, us = Uc[:, k, i:i + 1], Us[:, k, i:i + 1]
                    if k % 2 == 0:
                        P.ts(t1[:, 0:d], sinT[:, 0:d], us, -1.0, ALU.mult, ALU.mult)
                        P.ts(t2[:, 0:d], cosT[:, 0:d], us, None, ALU.mult)
                        P.stt(cosT[:, d:2 * d], cosT[:, 0:d], uc, t1[:, 0:d], ALU.mult, ALU.add)
                        P.stt(sinT[:, d:2 * d], sinT[:, 0:d], uc, t2[:, 0:d], ALU.mult, ALU.add)
                    else:
                        P.ts(t2[:, 0:d], cosT[:, 0:d], us, None, ALU.mult)
                        P.ts(t1[:, 0:d], sinT[:, 0:d], us, -1.0, ALU.mult, ALU.mult)
                        P.stt(sinT[:, d:2 * d], sinT[:, 0:d], uc, t2[:, 0:d], ALU.mult, ALU.add)
                        P.stt(cosT[:, d:2 * d], cosT[:, 0:d], uc, t1[:, 0:d], ALU.mult, ALU.add)
                P.ts(t1, sinT, f_im[:, i:i + 1], None, ALU.mult)
                P.ts(t2, sinT, f_re[:, i:i + 1], -1.0, ALU.mult, ALU.mult)
                P.stt(Freb, cosT, f_re[:, i:i + 1], t1, ALU.mult, ALU.add)
                P.stt(Fimb, cosT, f_im[:, i:i + 1], t2, ALU.mult, ALU.add)
                P.copy(cosb, cosT, eng="act")
                P.copy(sinb, sinT, eng="act")
                P.act(nsinb, sinT, AF.Copy, scale=-1.0)
                def front_a(bi):
                    si, tb, TB, c0 = blocks[bi]
                    u = Z[:, cb, c0:c0 + TB]
                    psr = self.bank(0)
                    psi = self.bank(1)
                    P.mm(psr[:, 0:TB], wB[:, 0, q, :], u)
                    P.mm(psi[:, 0:TB], wB[:, 1, q, :], u)
                    P.copy(rawr[:, 0:TB], psr[:, 0:TB], eng="act")
                    P.copy(rawi[:, 0:TB], psi[:, 0:TB], eng="act")

                def front_b(bi):
                    si, tb, TB, c0 = blocks[bi]
                    a_, b_, c_ = ab[:, 0:TB], bb[:, 0:TB], cb_[:, 0:TB]
                    P.tt(a_, rawr[:, 0:TB], Freb[:, 0:TB], ALU.mult)
                    P.tt(b_, rawi[:, 0:TB], Fimb[:, 0:TB], ALU.mult)
                    P.tt(c_, rawr[:, 0:TB], Fimb[:, 0:TB], ALU.mult)
                    P.tt(xi[:, 0:TB], rawi[:, 0:TB], Freb[:, 0:TB], ALU.mult)
                    P.tt(xr[:, 0:TB], a_, b_, ALU.subtract)
                    P.tt(xi[:, 0:TB], c_, xi[:, 0:TB], ALU.add)

                front_a(0)
                front_b(0)
                for bi, (si, tb, TB, c0) in enumerate(blocks):
                    sq = SEQS[si]
                    car = cars[si]
                    a_, b_ = ab[:, 0:TB], bb[:, 0:TB]
                    rrb, rib = rrbs[hb_i % 2], ribs[hb_i % 2]
                    if tb == 0 and sq["s"] is None:
                        ini_r, ini_i = 0.0, 0.0
                        rd_ = []
                    else:
                        kk = 0 if tb == 0 else int(math.log2(TB))
                        uc, us = Uc[:, kk, i:i + 1], Us[:, kk, i:i + 1]
                        cr, ci = car[:, 0, i:i + 1], car[:, 1, i:i + 1]
                        P.ts(cin[:, 0:1], ci, us, -1.0, ALU.mult, ALU.mult)
                        P.ts(cin[:, 1:2], cr, us, None, ALU.mult)
                        P.stt(cin[:, 0:1], cr, uc, cin[:, 0:1], ALU.mult, ALU.add)
                        P.stt(cin[:, 1:2], ci, uc, cin[:, 1:2], ALU.mult, ALU.add)
                        ini_r, ini_i = cin[:, 0:1], cin[:, 1:2]
                        rd_ = [cin]
                    if bi + 1 < len(blocks):
                        front_a(bi + 1)
                    mbc = mag[:, i:i + 1].to_broadcast([128, TB])
                    for (o_, x_, in_) in ((rr_, xr, ini_r), (ri_, xi, ini_i)):
                        P.add("dve", (lambda o_=o_, x_=x_, in_=in_, mbc=mbc, TB=TB:
                                      (lambda e: e.tensor_tensor_scan(o_[:, 0:TB], mbc, x_[:, 0:TB], in_, ALU.mult, ALU.add)))(),
                              [x_[:, 0:TB], mag[:, i:i + 1]] + rd_, [o_[:, 0:TB]])
                    P.copy(car[:, 0, i:i + 1], rr_[:, TB - 1:TB], eng="act")
                    P.copy(car[:, 1, i:i + 1], ri_[:, TB - 1:TB], eng="act")
                    P.copy(rrb[:, 0:TB], rr_[:, 0:TB], eng="act")
                    P.copy(rib[:, 0:TB], ri_[:, 0:TB], eng="act")
                    if tb == sq["L"] // TB - 1:
                        fin = fins[si]
                        cc, ss = cosT[:, TB - 1:TB], sinT[:, TB - 1:TB]
                        P.ts(cin[:, 2:3], ri_[:, TB - 1:TB], ss, -1.0, ALU.mult, ALU.mult)
                        P.ts(cin[:, 3:4], rr_[:, TB - 1:TB], ss, None, ALU.mult)
                        P.stt(fin[:, 0, i:i + 1], rr_[:, TB - 1:TB], cc, cin[:, 2:3], ALU.mult, ALU.add)
                        P.stt(fin[:, 1, i:i + 1], ri_[:, TB - 1:TB], cc, cin[:, 3:4], ALU.mult, ALU.add)
                    if bi + 1 < len(blocks):
                        front_b(bi + 1)
                    hb = self.V(B0 + 28672 + 2048 * (hb_i % 2), BF16, 2, 512)
                    hb_i += 1
                    c_ = cb_[:, 0:TB]
                    P.tt(a_, rrb[:, 0:TB], cosb[:, 0:TB], ALU.mult)
                    P.tt(b_, rib[:, 0:TB], sinb[:, 0:TB], ALU.mult)
                    P.tt(c_, rrb[:, 0:TB], nsinb[:, 0:TB], ALU.mult)
                    P.tt(hb[:, 1, 0:TB], rib[:, 0:TB], cosb[:, 0:TB], ALU.mult)
                    P.tt(hb[:, 0, 0:TB], a_, b_, ALU.subtract)
                    P.tt(hb[:, 1, 0:TB], c_, hb[:, 1, 0:TB], ALU.subtract)
                    P.mm(psy[bi][:, 0:TB], wC[:, 0, q, :], hb[:, 0, 0:TB], start=(q == 0), stop=False)
                    P.mm(psy[bi][:, 0:TB], wC[:, 1, q, :], hb[:, 1, 0:TB], start=False, stop=(q == 3))
            for bi, (si, tb, TB, c0) in enumerate(blocks):
                yt = ytmp[:, 0:TB]
                P.stt(yt, Z[:, cb, c0:c0 + TB], self.sc("ssm_d", 6 * l + cb), psy[bi][:, 0:TB], ALU.mult, ALU.add)
                P.act(XA[:, cb, c0:c0 + TB], yt, AF.Gelu_apprx_tanh)
        for si in range(3):
            P.dma(self.ssm_o[l, si].rearrange("r p i -> p r i"), fins[si], q="sp")
        self.linear(self.w_glu[l], 6, lambda k, c0, n: XA[:, k, c0:c0 + n], [(m * 128, 128) for m in range(6)], TILES,
                    lambda mi, c0, n, ps: P.act(Z[:, mi, c0:c0 + n], ps, AF.Sigmoid, bias=self.sc("b_glu", 6 * l + mi)))
        for m in range(6):
            P.tt(XA[:, m, :], XA[:, m, :], Z[:, m, :], ALU.mult)

    def ckv(self):
        P = self.P
        H, XA = self.H, self.XA
        B0 = PH_OFF
        self.rms_fm(lambda k, c0, n: H[:, k, c0:c0 + n], 8, TILES, "kv_norm", 0,
                    lambda k, c0, n: XA[:, k, c0:c0 + n], 1.0 / D)
        wv = self.V(B0, BF16, 8, 288)
        P.dma(wv, self.w_dkv.rearrange("(k p) o -> p k o", p=128), q="pool")
        S0 = B0 + 8192
        lst = self.V(B0 + 16384, F32, 2, 512)
        rC = self.V(B0 + 20480, F32, 512)
        rS = self.V(B0 + 22528, F32, 512)
        knf = self.V(B0 + 24576, F32, 512)
        t1 = self.V(B0 + 26624, F32, 512)
        t2 = self.V(B0 + 28672, F32, 512)
        krs = self.V(B0 + 30720, F32, 512)
        for (c0, n) in TILES:
            pss = [self.bank(), self.bank(), self.bank()]
            for mc, (m0, mn) in enumerate([(0, 128), (128, 128), (256, 32)]):
                o = pss[mc][0:128, 0:n] if mc < 2 else pss[mc][64:96, 0:n]
                for k in range(8):
                    P.mm(o, wv[:, k, m0:m0 + mn], XA[:, k, c0:c0 + n], start=(k == 0), stop=(k == 7))
            rstd = self.rms_stats([pss[0][:, 0:n], pss[1][:, 0:n]], self.ones, n, 1.0 / 256, [S0, S0 + 2048], S0 + 4096, S0 + 6144)
            for mc in range(2):
                P.stt(lst[:, mc, 0:n], pss[mc][:, 0:n], self.sc("lat_norm", mc), rstd, ALU.mult, ALU.mult)
                P.dma(self.lat_o[mc * 128:(mc + 1) * 128, c0:c0 + n], lst[:, mc, 0:n], q="sp")
                P.copy(self.latn[:, mc, c0:c0 + n], lst[:, mc, 0:n], eng="act")
            pk = pss[2][64:96, 0:n]
            rstd2 = self.rms_stats([pk], self.boq[64:96, 64:96], n, 1.0 / 32, [S0], S0 + 4096, S0 + 6144, p0=64, p1=96)
            P.stt(knf[64:96, 0:n], pk, self.sc("krope_g", 0, 64, 96), rstd2, ALU.mult, ALU.mult)
            self.rope_apply(knf, krs, c0, n, rC, rS, t1, t2)
            P.dma(self.kr_o[:, c0:c0 + n], krs[64:96, 0:n], q="sp")
            P.copy(self.krn[64:96, c0:c0 + n], krs[64:96, 0:n], eng="act")

    def rope_apply(self, src, dst, c0, n, rC, rS, t1, t2):
        P = self.P
        P.dma(rC[64:96, 0:n], self.rope[0][:, c0:c0 + n], q="sp")
        P.dma(rS[64:96, 0:n], self.rope[1][:, c0:c0 + n], q="sp")
        pr = self.bank()
        P.mm(pr[64:96, 0:n], self.rrot[64:96, :], src[64:96, 0:n])
        P.tt(t1[64:96, 0:n], pr[64:96, 0:n], rS[64:96, 0:n], ALU.mult)
        P.tt(t2[64:96, 0:n], src[64:96, 0:n], rC[64:96, 0:n], ALU.mult)
        P.tt(dst[64:96, 0:n], t1[64:96, 0:n], t2[64:96, 0:n], ALU.add)

    def mla(self, j):
        P = self.P
        Z, XA = self.Z, self.XA
        B0 = PH_OFF
        self.rms_fm(lambda k, c0, n: Z[:, k, c0:c0 + n], 6, TILES, "qlat_g", 6 * j,
                    lambda k, c0, n: Z[:, k, c0:c0 + n], 1.0 / MIX)
        wuk = self.V(B0, BF16, 2, MIX)
        wuv = self.V(B0 + 3072, BF16, 2, MIX)
        P.dma(wuk, self.w_uk.rearrange("(k p) o -> p k o", p=128), q="pool")
        P.dma(wuv, self.w_uv.rearrange("(k p) o -> p k o", p=128), q="pool")
        wqs = [self.V(B0 + 6144 + 1536 * i, BF16, 6, 128) for i in range(2)]
        qnf = self.V(B0 + 28672, F32, 512)
        SQ, RS, RSTD = B0 + 30720, B0 + 31744, B0 + 33792
        SQ0, RS0, RSTD0 = SQ, RS, RSTD
        rC = self.V(B0 + 38912, F32, 512)
        rS = self.V(B0 + 40960, F32, 512)
        t1 = self.V(B0 + 43008, F32, 512)
        t2 = self.V(RS, F32, 512)
        acc = self.V(B0 + 25600, F32, 12, 32)
        fz = self.V(B0 + 43008, F32, 512)
        PT0 = B0 + 35840
        qscale = self.sc("qscale", 0)
        self.rr = [0, 1, 2, 3, 4, 5]
        st = dict(pt=0, u=0)

        def head_build(h, lat, nkeys, Kcat, Vh, alt=False):
            stages = []
            chains = []
            nkc = (nkeys + 127) // 128
            for ti, kt in enumerate(range(0, nkeys, 512)):
                kn = min(512, nkeys - kt)
                box = {}
                SQ, RS, RSTD = ((B0 + 22272, B0 + 23296, B0 + 23296) if (alt and ti % 2) else (SQ0, RS0, RSTD0))

                def s1(kt=kt, kn=kn, box=box, SQ=SQ, RS=RS, RSTD=RSTD):
                    ps = self.bank()
                    box["ps"] = ps
                    box["b"] = self.last_bank
                    self.held.add(self.last_bank)
                    for kc2 in range(2):
                        P.mm(ps[0:64, 0:kn], wuk[:, kc2, h * 64:(h + 1) * 64], lat[:, kc2, kt:kt + kn], start=(kc2 == 0), stop=(kc2 == 1))
                    sq = self.V(SQ, BF16, 512, p0=0, p1=64)[:, 0:kn]
                    P.act(sq, ps[0:64, 0:kn], AF.Square)

                def s2(kt=kt, kn=kn, box=box, SQ=SQ, RS=RS, RSTD=RSTD):
                    sq = self.V(SQ, BF16, 512, p0=0, p1=64)[:, 0:kn]
                    p2 = self.bank()
                    box["p2"] = p2
                    P.mm(p2[0:64, 0:kn], self.bo64[0:64, 0:64], sq)
                    rs = self.V(RS, F32, 512, p0=0, p1=64)[:, 0:kn]
                    P.act(rs, p2[0:64, 0:kn], AF.Ln, bias=EPS, scale=1.0 / 64)

                def s3(kt=kt, kn=kn, box=box, SQ=SQ, RS=RS, RSTD=RSTD):
                    rs = self.V(RS, F32, 512, p0=0, p1=64)[:, 0:kn]
                    rstd = self.V(RSTD, F32, 512, p0=0, p1=64)[:, 0:kn]
                    P.act(rstd, rs, AF.Exp, scale=-0.5)
                    P.stt(Kcat[0:64, kt:kt + kn], box["ps"][0:64, 0:kn], self.sc("knope_g", 0, 0, 64), rstd, ALU.mult, ALU.mult)
                    self.held.discard(box["b"])
                chains.append([s1, s2, s3])
            if alt:
                for k in range(3):
                    for ch in chains:
                        stages.append(ch[k])
            else:
                for ch in chains:
                    stages += ch
            for g0 in range(0, nkc, 8):
                def sv(g0=g0):
                    ps = self.bank()
                    g1 = min(nkc, g0 + 8)
                    for kc in range(g0, g1):
                        kn = min(128, nkeys - kc * 128)
                        for kc2 in range(2):
                            P.mm(ps[0:kn, (kc - g0) * 64:(kc - g0 + 1) * 64], lat[:, kc2, kc * 128:kc * 128 + kn],
                                 wuv[:, kc2, h * 64:(h + 1) * 64], start=(kc2 == 0), stop=(kc2 == 1))
                    knl = min(128, nkeys - (g1 - 1) * 128)
                    vo = (h % 2) * 64
                    if knl == 128:
                        P.copy(Vh[:, g0:g1, vo:vo + 64], ps[:, 0:(g1 - g0) * 64].rearrange("p (a b) -> p a b", b=64))
                    else:
                        if g1 - 1 > g0:
                            P.copy(Vh[:, g0:g1 - 1, vo:vo + 64], ps[:, 0:(g1 - 1 - g0) * 64].rearrange("p (a b) -> p a b", b=64))
                        P.copy(Vh[0:knl, g1 - 1, vo:vo + 64], ps[0:knl, (g1 - 1 - g0) * 64:(g1 - g0) * 64])
                stages.append(sv)
            return stages

        def load_wq(h):
            P.dma(wqs[h % 2], self.w_uq[j][:, h * 128:(h + 1) * 128].rearrange("(k p) o -> p k o", p=128), q="pool")

        def q_build(h, c0, n, Qdst):
            wq = wqs[h % 2]
            box = {}

            def s1():
                psq = self.bank()
                box["psq"] = psq
                box["b"] = self.last_bank
                self.held.add(self.last_bank)
                for k in range(6):
                    P.mm(psq[:, 0:n], wq[:, k, :], Z[:, k, c0:c0 + n], start=(k == 0), stop=(k == 5))
                sq = self.V(SQ, BF16, 512)[:, 0:n]
                P.act(sq, psq[:, 0:n], AF.Square)
                P.dma(rC[64:96, 0:n], self.rope[0][:, c0:c0 + n], q="sp")
                P.dma(rS[64:96, 0:n], self.rope[1][:, c0:c0 + n], q="sp")

            def s2():
                sq = self.V(SQ, BF16, 512)[:, 0:n]
                p2 = self.bank()
                P.mm(p2[:, 0:n], self.boq, sq)
                rs = self.V(RS, F32, 512)[:, 0:n]
                P.act(rs, p2[:, 0:n], AF.Ln, bias=EPS, scale=qscale)

            def s3():
                rs = self.V(RS, F32, 512)[:, 0:n]
                rstd = self.V(RSTD, F32, 512)[:, 0:n]
                P.act(rstd, rs, AF.Exp, scale=-0.5)
                P.stt(qnf[:, 0:n], box["psq"][:, 0:n], self.sc("q_g", j), rstd, ALU.mult, ALU.mult)
                P.copy(Qdst[0:64, :], qnf[0:64, 0:n])
                self.held.discard(box["b"])

            def s4():
                pr = self.bank()
                box["pr"] = pr
                box["b2"] = self.last_bank
                self.held.add(self.last_bank)
                P.mm(pr[64:96, 0:n], self.rrot[64:96, :], qnf[64:96, 0:n])
                P.tt(t2[64:96, 0:n], qnf[64:96, 0:n], rC[64:96, 0:n], ALU.mult)

            def s5_():
                P.tt(t1[64:96, 0:n], box["pr"][64:96, 0:n], rS[64:96, 0:n], ALU.mult)
                P.tt(Qdst[64:96, :], t1[64:96, 0:n], t2[64:96, 0:n], ALU.add)
                self.held.discard(box["b2"])
            return [s1, s2, s3, s4, s5_]

        def run_all(stages):
            for f in stages:
                f()

        def core(Kcat, Vh, Q, n, kcs, nkeys, diag0, pso, psd, hp, fillers=()):
            G = 512 // n if n < 512 else 1
            groups = [kcs[gi:gi + G] for gi in range(0, len(kcs), G)]

            def S(grp):
                pss = self.bank()
                pb_ = self.last_bank
                self.held.add(pb_)
                pT = self.V(PT0 + 1024 * (st["pt"] % 3), BF16, 512)
                st["pt"] += 1
                cl = 0
                for gj, kc in enumerate(grp):
                    kn = min(128, nkeys - kc * 128)
                    diag = diag0 is not None and kc >= diag0
                    if diag:
                        cl = min(128 * (kc - diag0), n - 128)
                    P.mm(pss[0:kn, gj * n + cl:(gj + 1) * n], Kcat[0:96, kc * 128:kc * 128 + kn], Q[0:96, cl:n], start=True, stop=not diag)
                    if diag:
                        P.mm(pss[0:kn, cl:n], self.negI, self.masks[:, kc - diag0, cl:n], start=False, stop=True)
                return (grp, pss, pT, pb_, cl)

            def E(item):
                grp, pss, pT, pb_, cl = item
                self.held.discard(pb_)
                full = [kc for kc in grp if min(128, nkeys - kc * 128) == 128]
                if full:
                    P.act(pT[:, cl:len(full) * n], pss[:, cl:len(full) * n], AF.Exp, scale=MLA_SCALE)
                if len(full) < len(grp):
                    kn = nkeys - grp[-1] * 128
                    gj = len(grp) - 1
                    P.act(pT[0:kn, gj * n:(gj + 1) * n], pss[0:kn, gj * n:(gj + 1) * n], AF.Exp, scale=MLA_SCALE)
                for gj, kc in enumerate(grp):
                    kn = min(128, nkeys - kc * 128)
                    first = (kc == kcs[0])
                    last = (kc == kcs[-1])
                    P.mm(pso[:, cl:n], Vh[0:kn, kc, :], pT[0:kn, gj * n + cl:(gj + 1) * n], start=first, stop=last)

            fillers = list(fillers)
            per = -(-len(fillers) // max(1, len(groups)))
            pend = []
            for grp in groups:
                pend.append(S(grp))
                if len(pend) > 2:
                    E(pend.pop(0))
                for _ in range(per):
                    if fillers:
                        fillers.pop(0)()
            while pend:
                E(pend.pop(0))
            run_all(fillers)

        def acc_banks():
            u = st["u"]
            st["u"] += 1
            return self.bank(6 + (u % 2)), None

        Kc = [self.V(B0 + 9216, BF16, 2048), self.V(B0 + 17408, BF16, 2048)]
        Vhs = [self.V(B0 + 13312, BF16, 16, 128), self.V(B0 + 21504, BF16, 16, 128)]
        Qcs = [self.V(B0 + 25600 + 1024 * i, BF16, 512) for i in range(3)]
        P.memset(Vhs[0][:, :, 64:128], 1.0)
        P.memset(Vhs[1][:, :, 0:64], 1.0)
        ptiles = [(c0, n) for (c0, n) in TILES if c0 < TP]
        latp = self.latn[:, :, 0:TP]
        for b_ in range(2):
            P.copy(Kc[b_][64:96, 0:TP], self.krn[64:96, 0:TP], eng="act")
        units = [(h, c0, n) for h in range(12) for (c0, n) in ptiles]
        load_wq(0)
        run_all(head_build(0, latp, TP, Kc[0], Vhs[0]))
        for u0 in range(2):
            h0, c0_, n0_ = units[u0]
            run_all(q_build(h0, c0_, n0_, Qcs[u0 % 3][:, 0:n0_]))
        for ui, (h, c0, n) in enumerate(units):
            fill = []
            if ui + 1 < len(units) and units[ui + 1][0] != h:
                fill += head_build(units[ui + 1][0], latp, TP, Kc[units[ui + 1][0] % 2], Vhs[units[ui + 1][0] % 2])
            if ui + 2 < len(units):
                h2, c2, n2 = units[ui + 2]
                if h2 != units[ui + 1][0] or (ui == 0 and False):
                    load_wq(h2)
                fill += q_build(h2, c2, n2, Qcs[(ui + 2) % 3][:, 0:n2])
            hp = (h % 2) * 64
            qt = c0 // 512
            pso, psd = acc_banks()
            core(Kc[h % 2], Vhs[h % 2], Qcs[ui % 3][:, 0:n], n, list(range(4 * (qt + 1))), TP, 4 * qt, pso, psd, hp, fill)
            dp = 64 - hp
            lg = fz[dp:dp + 64, 0:n]
            P.recip(lg, pso[dp:dp + 64, 0:n])
            P.tt(XA[hp:hp + 64, h // 2, c0:c0 + n], pso[hp:hp + 64, 0:n], lg, ALU.mult)

        SBK = 1024
        latsb = self.V(B0 + 9216, BF16, 2, SBK)
        KcS = [self.V(B0 + 13312, BF16, SBK), self.V(B0 + 17408, BF16, SBK)]
        VhS = [self.V(B0 + 15360, BF16, 8, 128), self.V(B0 + 19456, BF16, 8, 128)]
        Qs = self.V(B0 + 21504, BF16, 12, LS)
        P.memset(VhS[0][:, :, 64:128], 1.0)
        P.memset(VhS[1][:, :, 0:64], 1.0)
        for sq in SEQS[1:]:
            c0, n = sq["c0"], sq["L"]
            for h in range(12):
                if h % 2 == 0:
                    load_wq(h)
                    if h + 1 < 12:
                        load_wq(h + 1)
                run_all(q_build(h, c0, n, Qs[:, h, :]))
            sbs = [("cache", k0_, SBK) for k0_ in range(0, PAST, SBK)] + [("new", c0, n)]
            for sbi, (kind, k0, nkeys) in enumerate(sbs):
                reuse = (kind == "cache" and j == 1)
                spill = (kind == "cache" and j == 0)
                if kind == "new":
                    lat = self.latn[:, :, k0:k0 + nkeys]
                    for b_ in range(2):
                        P.copy(KcS[b_][64:96, 0:nkeys], self.krn[64:96, k0:k0 + nkeys], eng="act")
                else:
                    lat = latsb[:, :, 0:nkeys]
                    if not reuse:
                        for kc2 in range(2):
                            P.dma(latsb[:, kc2, 0:nkeys], self.latc[sq["s"], kc2 * 128:(kc2 + 1) * 128, k0:k0 + nkeys], q="pool")
                    for b_ in range(2):
                        P.dma(KcS[b_][64:96, 0:nkeys], self.krc[sq["s"], :, k0:k0 + nkeys], q="pool")
                nkc = (nkeys + 127) // 128

                def hb_(h):
                    b_ = h % 2
                    vo = b_ * 64
                    if reuse:
                        P.dma(KcS[b_][0:64, 0:nkeys], self.kscr[sq["s"], h, :, k0:k0 + nkeys], q="sp")
                        P.dma(VhS[b_][:, :, vo:vo + 64], self.vscr[sq["s"], h, sbi].rearrange("p (a b) -> p a b", b=64), q="sp")
                        return
                    run_all(head_build(h, lat, nkeys, KcS[b_], VhS[b_], alt=True))
                    if spill:
                        P.dma(self.kscr[sq["s"], h, :, k0:k0 + nkeys], KcS[b_][0:64, 0:nkeys], q="sp")
                        P.dma(self.vscr[sq["s"], h, sbi].rearrange("p (a b) -> p a b", b=64), VhS[b_][:, :, vo:vo + 64], q="sp")

                hb_(0)
                for h in range(12):
                    if h + 1 < 12:
                        hb_(h + 1)
                    hp = (h % 2) * 64
                    pso, psd = acc_banks()
                    core(KcS[h % 2], VhS[h % 2], Qs[:, h, :], n, list(range(nkc)), nkeys, None, pso, psd, hp)
                    ah = acc[:, h, :]
                    if sbi == 0:
                        P.copy(ah, pso[:, 0:n])
                    else:
                        P.tt(ah, ah, pso[:, 0:n], ALU.add)
                    if sbi == len(sbs) - 1:
                        dp = 64 - hp
                        rdh = fz[hp:hp + 64, 0:n]
                        P.recip(rdh, acc[dp:dp + 64, h, :])
                        P.tt(XA[hp:hp + 64, h // 2, c0:c0 + n], acc[hp:hp + 64, h, :], rdh, ALU.mult)
        self.rr = list(range(8))

    def ffn(self, l):
        P = self.P
        H, XA, Z = self.H, self.XA, self.Z
        B0 = PH_OFF
        self.rms_fm(lambda k, c0, n: H[:, k, c0:c0 + n], 8, TILES, "norm_ffn", 8 * l,
                    lambda k, c0, n: XA[:, k, c0:c0 + n], 1.0 / D)
        UW = T + 6
        ub = [self.V(B0 + 32768 + 4352 * i, BF16, 2176)[:, 0:UW] for i in range(2)]
        sg = self.V(B0 + 41472, F32, 512)
        Dm = self.V(B0 + 43520, BF16, 6, 128)
        cbufL = self.V(CONVB_OFF, F32, 3, 88)
        ubase = [sq["c0"] + 2 * si for si, sq in enumerate(SEQS)]
        groups = [list(range(0, 8)), list(range(8, 16)), list(range(16, 22))]
        Wi = self.w_ffn_in[l]
        Wo = self.w_ffn_out[l]
        for grp in groups:
            wblk = {}
            for jj, jc in enumerate(grp):
                if jj % 4 == 0:
                    nb_ = min(4, len(grp) - jj) * 128
                    for part in range(2):
                        so = self.slot()
                        wv = self.V(so, BF16, 8, 512)[:, :, 0:nb_]
                        P.dma(wv, Wi[:, part * DFF + jc * 128: part * DFF + jc * 128 + nb_].rearrange("(k p) o -> p k o", p=128), q="pool")
                        wblk[part] = wv
                for part in range(2):
                    wv = wblk[part]
                    col = part * NJ + jc
                    for si, sq in enumerate(SEQS):
                        if sq["s"] is None:
                            P.memset(ub[part][:, ubase[si]:ubase[si] + 2], 0.0)
                        else:
                            P.dma(ub[part][:, ubase[si]:ubase[si] + 2], self.cst[l, sq["s"]][:, 2 * col:2 * col + 2], q="pool")
                    for k3 in range(3):
                        P.ts(Dm[:, part * 3 + k3, :], self.identb, self.sc("conv_w", l * 132 + k3 * 44 + col), None, ALU.mult)
                    for (c0, n) in TILES:
                        ps = self.bank()
                        for k in range(8):
                            P.mm(ps[:, 0:n], wv[:, k, (jj % 4) * 128:(jj % 4 + 1) * 128], XA[:, k, c0:c0 + n], start=(k == 0), stop=(k == 7))
                        if c0 < TP:
                            P.copy(ub[part][:, 2 + c0:2 + c0 + n], ps[:, 0:n], eng="act")
                            if c0 + n == TP:
                                P.copy(cbufL[:, 0, 2 * col:2 * col + 2], ps[:, n - 2:n])
                        else:
                            for si in (1, 2):
                                o = (si - 1) * LS
                                P.copy(ub[part][:, ubase[si] + 2:ubase[si] + 2 + LS], ps[:, o:o + LS], eng="act")
                                P.copy(cbufL[:, si, 2 * col:2 * col + 2], ps[:, o + LS - 2:o + LS])
                for si, sq in enumerate(SEQS):
                    tl = [(c0, n) for (c0, n) in TILES if c0 < TP] if sq["s"] is None else [(sq["c0"], sq["L"])]
                    for (c0, n) in tl:
                        pa = self.bank()
                        pg = self.bank()
                        u0 = ubase[si] + (c0 - sq["c0"])
                        for part, pp in ((0, pa), (1, pg)):
                            for k3 in range(3):
                                P.mm(pp[:, 0:n], Dm[:, part * 3 + k3, :], ub[part][:, u0 + k3:u0 + k3 + n], start=(k3 == 0), stop=(k3 == 2))
                        P.act(sg[:, 0:n], pg[:, 0:n], AF.Silu, bias=self.sc("conv_b", l * 44 + NJ + jc))
                        P.stt(Z[:, jj, c0:c0 + n], pa[:, 0:n], self.sc("conv_b", l * 44 + jc), sg[:, 0:n], ALU.add, ALU.mult)
            ng = len(grp)
            for half in range(2):
                so = self.slot()
                wv = self.V(so, BF16, 8, 512)[:, 0:ng, :]
                P.dma(wv, Wo[grp[0] * 128:(grp[-1] + 1) * 128, half * 512:(half + 1) * 512].rearrange("(k p) o -> p k o", p=128), q="pool")
                for mo in range(4):
                    m = half * 4 + mo
                    for (c0, n) in TILES:
                        ps = self.bank()
                        for jj in range(ng):
                            P.mm(ps[:, 0:n], wv[:, jj, mo * 128:(mo + 1) * 128], Z[:, jj, c0:c0 + n], start=(jj == 0), stop=(jj == ng - 1))
                        P.tt(H[:, m, c0:c0 + n], H[:, m, c0:c0 + n], ps[:, 0:n], ALU.add)
        P.dma(self.conv_o[l].rearrange("s p c -> p s c"), cbufL, q="sp")


_CACHE = {}


def _get_nc():
    if "nc" not in _CACHE:
        nc = bass.Bass("TRN2", target_bir_lowering=False)
        K(nc).build()
        _CACHE["nc"] = nc
    return _CACHE["nc"]


def _pp(v):
    v = np.asarray(v, np.float32)
    return np.ascontiguousarray(v.reshape(-1, 128).T)


def _consts():
    cf = np.zeros((128, NCF), np.float32)
    R = np.zeros((32, 32), np.float32)
    for m in range(16):
        R[16 + m, m] = -1.0
        R[m, 16 + m] = 1.0
    cf[64:96, 0:32] = R
    cb = np.zeros((128, NCB), np.float32)
    cb[:, 0:128] = 1.0
    cb[:, 128:256] = -30000.0 * np.eye(128, dtype=np.float32)
    cb[:, 256:384] = np.eye(128, dtype=np.float32)
    cb[0:64, 384:448] = 1.0
    cb[64:128, 448:512] = 1.0
    cb[0:64, 512:576] = 1.0
    cb[64:96, 576:608] = 1.0
    p = np.arange(128)[:, None]
    c = np.arange(512)[None, :]
    for i in range(4):
        cb[:, 640 + 512 * i: 640 + 512 * (i + 1)] = (((128 * i + p) // 64) > (c // 64)).astype(np.float32)
    return cf, cb


def _rope_tables():
    inv = (1.0 / (np.float32(10000.0) ** (np.arange(0, 32, 2, dtype=np.float32) / np.float32(32)))).astype(np.float32)
    pos = np.concatenate([np.arange(TP), PAST + np.arange(LS), PAST + np.arange(LS)]).astype(np.float32)
    ang = (pos[:, None] * inv[None, :]).astype(np.float32)
    c = np.cos(ang).astype(np.float32).T
    s = np.sin(ang).astype(np.float32).T
    return np.ascontiguousarray(np.stack([np.concatenate([c, c], 0), np.concatenate([s, s], 0)], 0))


def _pack_shared(inp):
    f = lambda k: np.asarray(inp[k], np.float32)
    sc = np.zeros((128, NSC), np.float32)

    def put(name, col, arr):
        arr = np.asarray(arr, np.float32)
        if arr.ndim == 1:
            arr = arr[:, None]
        sc[:arr.shape[0], SCOL[name] + col: SCOL[name] + col + arr.shape[1]] = arr

    for l in range(DEPTH):
        put("norm_mix", 8 * l, _pp(f("norm_mix_g")[l]))
        put("norm_ffn", 8 * l, _pp(f("norm_ffn_g")[l]))
        put("mem_norm", 8 * l, _pp(f("mem_norm_g")[l]))
        put("memq_g", l, np.tile(f("mem_q_norm_g")[l], 2))
        put("memk_g", l, np.tile(f("mem_k_norm_g")[l], 2))
        cw = f("ffn_conv_w")[l]
        for k3 in range(3):
            put("conv_w", l * 132 + k3 * 44, _pp(cw[k3]))
        put("conv_b", l * 44, _pp(f("ffn_conv_b")[l]))
    put("kv_norm", 0, _pp(f("kv_norm_g")))
    put("lat_norm", 0, _pp(f("latent_norm_g")))
    kg = np.zeros(128, np.float32)
    kg[64:96] = f("krope_norm_g")
    put("krope_g", 0, kg)
    kn = np.zeros(128, np.float32)
    kn[0:64] = f("k_nope_norm_g")
    put("knope_g", 0, kn)
    qs = np.zeros(128, np.float32)
    qs[0:64] = 1.0 / 64
    qs[64:96] = 1.0 / 32
    put("qscale", 0, qs)
    for j in range(2):
        qg = np.zeros(128, np.float32)
        qg[0:64] = f("q_nope_norm_g")[j]
        qg[64:96] = f("q_rope_norm_g")[j]
        put("q_g", j, qg)
        put("qlat_g", 6 * j, _pp(f("q_latent_norm_g")[j]))
        put("ssm_d", 6 * j, _pp(f("ssm_d")[j]))
        put("b_glu", 6 * j, _pp(f("b_glu")[j]))
        put("ssm_p", 72 * j, _pp(f("ssm_a_re")[j].reshape(-1)))
        put("ssm_p", 72 * j + 24, _pp(f("ssm_a_im")[j].reshape(-1)))
        put("ssm_p", 72 * j + 48, _pp(np.repeat(f("ssm_log_dt")[j], 64)))
    s5b = np.zeros((NA, 2, 128, 24, 128), np.float32)
    s5c = np.zeros((NA, 2, 128, 24, 128), np.float32)
    for l in range(NA):
        for r, (bk, ck) in enumerate((("ssm_b_re", "ssm_c_re"), ("ssm_b_im", "ssm_c_im"))):
            b = f(bk)[l]
            c = f(ck)[l]
            for i in range(24):
                q = i % 4
                for gg in range(2):
                    g = 2 * i + gg
                    rows = slice(32 * q + 16 * gg, 32 * q + 16 * gg + 16)
                    cols = slice(64 * gg, 64 * gg + 64)
                    s5b[l, r, rows, i, cols] = b[g].T
                    s5c[l, r, cols, i, rows] = c[g].T
    wuq = np.zeros((2, MIX, 12, 128), np.float32)
    wuq[:, :, :, 0:96] = f("w_uq").reshape(2, MIX, 12, 96)
    cf, cb = _consts()
    return dict(scal=sc, consf=cf, consb=cb, rope=_rope_tables(),
                w_mix_in=f("w_mix_in"), w_mix_out=f("w_mix_out"), w_ffn_in=f("w_ffn_in"), w_ffn_out=f("w_ffn_out"),
                w_mem_kv=f("w_mem_kv"), w_glu=f("w_glu"), w_dkv=f("w_dkv"), w_uk=f("w_uk"), w_uv=f("w_uv"),
                w_uq=np.ascontiguousarray(wuq.reshape(2, MIX, 12 * 128)),
                s5b=np.ascontiguousarray(s5b.reshape(NA, 2, 128, 24 * 128)),
                s5c=np.ascontiguousarray(s5c.reshape(NA, 2, 128, 24 * 128)))


def _pack_core(inp, c):
    f = lambda k: np.asarray(inp[k], np.float32)
    s0, s1 = 2 * c, 2 * c + 1
    xT = np.concatenate([f("x_prompt")[c].T, f("x_sample")[s0].T, f("x_sample")[s1].T], axis=1)
    d = dict(xT=np.ascontiguousarray(xT), memT=np.ascontiguousarray(f("mem_prompt")[c].T))
    d["latc"] = np.ascontiguousarray(np.stack([f("cache_mla_latent")[s].T for s in (s0, s1)]))
    d["krc"] = np.ascontiguousarray(np.stack([f("cache_mla_krope")[s].T for s in (s0, s1)]))
    d["cmk"] = np.ascontiguousarray(np.stack([np.stack([f("cache_mem_k")[l, s].reshape(256, 256).T for s in (s0, s1)]) for l in range(DEPTH)]))
    d["cmv"] = np.ascontiguousarray(np.stack([np.stack([f("cache_mem_v")[l, s].reshape(256, 256) for s in (s0, s1)]) for l in range(DEPTH)]))
    d["sst"] = np.ascontiguousarray(np.stack([np.stack([np.stack([_pp(f(k)[l, s].reshape(-1)) for k in ("state_ssm_re", "state_ssm_im")])
                                                        for s in (s0, s1)]) for l in range(NA)]))
    cst = np.zeros((DEPTH, 2, 128, 44, 2), np.float32)
    for l in range(DEPTH):
        for si, s in enumerate((s0, s1)):
            sc_ = f("state_conv")[l, s]
            cst[l, si] = sc_.reshape(2, 44, 128).transpose(2, 1, 0)
    d["cst"] = np.ascontiguousarray(cst.reshape(DEPTH, 2, 128, 88))
    return d


def kernel(**inputs):
    nc = _get_nc()
    shared = _pack_shared(inputs)
    in_maps = []
    for c in range(8):
        d = dict(shared)
        d.update(_pack_core(inputs, c))
        in_maps.append(d)
    res = run_bass_kernel_spmd(nc, in_maps, core_ids=list(range(8)))
    R = res.results
    B, DB = 8, 16
    y_p = np.stack([R[c]["yT"][:, :TP].T for c in range(B)])
    y_s = np.zeros((DB, LS, D), np.float32)
    lat_p = np.stack([R[c]["lat_o"][:, :TP].T for c in range(B)])
    kr_p = np.stack([R[c]["kr_o"][:, :TP].T for c in range(B)])
    lat_s = np.zeros((DB, LS, 256), np.float32)
    kr_s = np.zeros((DB, LS, 32), np.float32)
    memk = np.zeros((DEPTH, B, NMEM, 4, 64), np.float32)
    memv = np.zeros((DEPTH, B, NMEM, 4, 64), np.float32)
    ssm_p = np.zeros((2, NA, B, 48, 64), np.float32)
    ssm_s = np.zeros((2, NA, DB, 48, 64), np.float32)
    conv_p = np.zeros((DEPTH, B, 2, 2 * DFF), np.float32)
    conv_s = np.zeros((DEPTH, DB, 2, 2 * DFF), np.float32)
    for c in range(B):
        r = R[c]
        for l in range(DEPTH):
            memk[l, c] = r["memk_o"][l].T.reshape(NMEM, 4, 64)
            memv[l, c] = r["memv_o"][l].reshape(NMEM, 4, 64)
            cv = r["conv_o"][l].reshape(3, 128, 44, 2)
            for si in range(3):
                arr = cv[si].transpose(2, 1, 0).reshape(2, 2 * DFF)
                if si == 0:
                    conv_p[l, c] = arr
                else:
                    conv_s[l, 2 * c + si - 1] = arr
        for l in range(NA):
            for si in range(3):
                for ri in range(2):
                    arr = r["ssm_o"][l, si, ri].T.reshape(48, 64)
                    if si == 0:
                        ssm_p[ri, l, c] = arr
                    else:
                        ssm_s[ri, l, 2 * c + si - 1] = arr
        for si in range(2):
            cs = slice(TP + si * LS, TP + (si + 1) * LS)
            y_s[2 * c + si] = r["yT"][:, cs].T
            lat_s[2 * c + si] = r["lat_o"][:, cs].T
            kr_s[2 * c + si] = r["kr_o"][:, cs].T
    return (y_p, y_s, memk, memv, lat_p, kr_p, ssm_p[0], ssm_p[1], conv_p, lat_s, kr_s, ssm_s[0], ssm_s[1], conv_s)
```
